# Optimizing a Trainium2 kernel written in Bass

```python
import math
import jax, jax.numpy as jnp
from jax import lax
import numpy as np

D_MODEL = 1024
BATCH = 4
SEQ = 4096
DEPTH = 1

RET_HEADS = 8
RET_QK_DIM = 64
RET_V_DIM = 128
RET_CHUNK = 128
ROPE_BASE = 10000.0

MOBA_HEADS = 8
MOBA_HEAD_DIM = 64
MOBA_BLOCK = 256
MOBA_TOPK = 3
MOBA_Q_CHUNK = 32

REL_BUCKETS = 32
REL_MAX_DIST = 2048

PEER_HEADS = 8
PEER_N_KEYS = 128
PEER_N_EXPERTS = PEER_N_KEYS * PEER_N_KEYS
PEER_QUERY_DIM = 256
PEER_TOPK = 16
PEER_TOKEN_CHUNK = 128

NORM_EPS = 1e-6
NEG_INF = -1e30

RET_QK_W = RET_HEADS * RET_QK_DIM
RET_V_W = RET_HEADS * RET_V_DIM
MOBA_W = MOBA_HEADS * MOBA_HEAD_DIM
IN_SIZES = (RET_QK_W, RET_QK_W, RET_V_W, RET_V_W, MOBA_W, MOBA_W, MOBA_W, D_MODEL, D_MODEL)
IN_WIDTH = sum(IN_SIZES)

kernel_name = "hybrid_retention_moba_peer_layer"


def rmsnorm(x, g):
    xf = x.astype(jnp.float32)
    y = xf * lax.rsqrt(jnp.mean(xf * xf, axis=-1, keepdims=True) + NORM_EPS)
    return (y * g.astype(jnp.float32)).astype(x.dtype)


def to_heads(t, n_heads):
    b, s, _ = t.shape
    return t.reshape(b, s, n_heads, -1).transpose(0, 2, 1, 3)


def from_heads(t):
    b, h, s, d = t.shape
    return t.transpose(0, 2, 1, 3).reshape(b, s, h * d)


def rotary(x, pos):
    half = x.shape[-1] // 2
    inv = ROPE_BASE ** (-jnp.arange(half, dtype=jnp.float32) / half)
    ang = pos.astype(jnp.float32)[:, None] * inv[None, :]
    cos, sin = jnp.cos(ang), jnp.sin(ang)
    x1, x2 = x[..., :half], x[..., half:]
    return jnp.concatenate([x1 * cos - x2 * sin, x1 * sin + x2 * cos], axis=-1).astype(x.dtype)


def retention(q, k, v):
    B, H, S, dk = q.shape
    dv = v.shape[-1]
    C = RET_CHUNK
    n = S // C
    log_g = jnp.log1p(-jnp.exp2(-5.0 - jnp.arange(H, dtype=jnp.float32)))
    idx = jnp.arange(C, dtype=jnp.float32)
    rel = idx[:, None] - idx[None, :]
    decay_in = jnp.where(rel[None] >= 0, jnp.exp(jnp.maximum(rel, 0.0)[None] * log_g[:, None, None]), 0.0)
    qc = q.reshape(B, H, n, C, dk)
    kc = k.reshape(B, H, n, C, dk)
    vc = v.reshape(B, H, n, C, dv)
    scores = jnp.einsum('bhnid,bhnjd->bhnij', qc, kc) * decay_in[None, :, None]
    o_in = jnp.einsum('bhnij,bhnje->bhnie', scores, vc)
    k_dec = kc * jnp.exp((C - 1 - idx)[None, :] * log_g[:, None])[None, :, None, :, None]
    kv = jnp.einsum('bhncd,bhnce->bhnde', k_dec, vc)
    chunk_decay = jnp.exp(C * log_g)[None, :, None, None]

    def step(state, kv_n):
        return state * chunk_decay + kv_n, state

    state0 = jnp.zeros((B, H, dk, dv), kv.dtype)
    _, prev = lax.scan(step, state0, jnp.moveaxis(kv, 2, 0))
    prev = jnp.moveaxis(prev, 0, 2)
    q_dec = jnp.exp((idx + 1.0)[None, :] * log_g[:, None])
    o_cross = jnp.einsum('bhncd,bhnde->bhnce', qc, prev) * q_dec[None, :, None, :, None]
    o = (o_in + o_cross).reshape(B, H, S, dv).astype(jnp.float32)
    mu = jnp.mean(o, axis=-1, keepdims=True)
    var = jnp.mean(jnp.square(o - mu), axis=-1, keepdims=True)
    return ((o - mu) * lax.rsqrt(var + NORM_EPS)).astype(q.dtype)


def head_rmsnorm(t, g):
    tf = t.astype(jnp.float32)
    y = tf * lax.rsqrt(jnp.mean(tf * tf, axis=-1, keepdims=True) + NORM_EPS)
    return (y * g.astype(jnp.float32)).astype(t.dtype)


def rel_bucket(dist):
    max_exact = REL_BUCKETS // 2
    n = jnp.maximum(dist, 0)
    nf = jnp.maximum(n, 1).astype(jnp.float32)
    large = max_exact + (jnp.log(nf / max_exact) / math.log(REL_MAX_DIST / max_exact)
                         * (REL_BUCKETS - max_exact)).astype(jnp.int32)
    large = jnp.minimum(large, REL_BUCKETS - 1)
    return jnp.where(n < max_exact, n, large)


def moba_attention(q, k, v, rel_bias):
    B, H, S, dh = q.shape
    L = MOBA_BLOCK
    n_blk = -(-S // L)
    s_pad = n_blk * L
    pad = ((0, 0), (0, 0), (0, s_pad - S), (0, 0))
    k_pad = jnp.pad(k, pad)
    v_pad = jnp.pad(v, pad)
    k_blocks = k_pad.reshape(B, H, n_blk, L, dh)
    v_blocks = v_pad.reshape(B, H, n_blk, L, dh)
    k_mean = jnp.mean(k_blocks.astype(jnp.float32), axis=3)
    n_sel = min(MOBA_TOPK, n_blk)
    Qc = MOBA_Q_CHUNK
    n_qc = S // Qc
    q_chunks = jnp.moveaxis(q.reshape(B, H, n_qc, Qc, dh), 2, 0)
    bias_hb = rel_bias.T.astype(jnp.float32)
    b_idx = jnp.arange(B)[:, None, None, None]
    h_idx = jnp.arange(H)[None, :, None, None]
    h_idx5 = jnp.arange(H)[None, :, None, None, None]
    offs = jnp.arange(L)
    blk_ids = jnp.arange(n_blk)

    def one_chunk(args):
        qc, c = args
        q_start = c * Qc
        blk = q_start // L
        qpos = q_start + jnp.arange(Qc)
        gate = jnp.einsum('bhqd,bhnd->bhqn', qc.astype(jnp.float32), k_mean)
        gate = jnp.where(blk_ids < blk, gate, NEG_INF)
        _, sel = lax.top_k(gate, n_sel)
        valid = sel < blk
        k_sel = k_blocks[b_idx, h_idx, sel]
        v_sel = v_blocks[b_idx, h_idx, sel]
        kpos_sel = sel[..., None] * L + offs
        bias_sel = bias_hb[h_idx5, rel_bucket(qpos[None, None, :, None, None] - kpos_sel)]
        logit_sel = jnp.einsum('bhqd,bhqnkd->bhqnk', qc, k_sel).astype(jnp.float32) + bias_sel
        logit_sel = jnp.where(valid[..., None], logit_sel, NEG_INF).reshape(B, H, Qc, n_sel * L)
        k_own = lax.dynamic_slice_in_dim(k_pad, blk * L, L, axis=2)
        v_own = lax.dynamic_slice_in_dim(v_pad, blk * L, L, axis=2)
        dist_own = qpos[:, None] - (blk * L + offs)[None, :]
        bias_own = bias_hb[:, rel_bucket(dist_own)]
        logit_own = jnp.einsum('bhqd,bhkd->bhqk', qc, k_own).astype(jnp.float32) + bias_own[None]
        logit_own = jnp.where(dist_own >= 0, logit_own, NEG_INF)
        p = jax.nn.softmax(jnp.concatenate([logit_sel, logit_own], axis=-1), axis=-1)
        p_sel = p[..., :n_sel * L].reshape(B, H, Qc, n_sel, L).astype(v.dtype)
        p_own = p[..., n_sel * L:].astype(v.dtype)
        return (jnp.einsum('bhqnk,bhqnkd->bhqd', p_sel, v_sel)
                + jnp.einsum('bhqk,bhkd->bhqd', p_own, v_own))

    out = lax.map(one_chunk, (q_chunks, jnp.arange(n_qc)))
    return jnp.moveaxis(out, 0, 2).reshape(B, H, S, dh)


def peer(h, w_q, sub_keys, u_tab, v_tab):
    B, S, D = h.shape
    T = B * S
    ht = h.reshape(T, D)
    half = PEER_QUERY_DIM // 2
    q = (ht @ w_q).reshape(T, PEER_HEADS, 2, half)
    s = jnp.einsum('thcd,hcnd->thcn', q, sub_keys).astype(jnp.float32)
    s1, i1 = lax.top_k(s[:, :, 0], PEER_TOPK)
    s2, i2 = lax.top_k(s[:, :, 1], PEER_TOPK)
    cand_s = (s1[..., :, None] + s2[..., None, :]).reshape(T, PEER_HEADS, PEER_TOPK * PEER_TOPK)
    cand_i = (i1[..., :, None] * PEER_N_KEYS + i2[..., None, :]).reshape(T, PEER_HEADS, PEER_TOPK * PEER_TOPK)
    top_s, pos = lax.top_k(cand_s, PEER_TOPK)
    experts = jnp.take_along_axis(cand_i, pos, axis=-1)
    gates = jax.nn.softmax(top_s, axis=-1).astype(h.dtype)
    Pc = PEER_TOKEN_CHUNK
    n_c = T // Pc

    def one_chunk(args):
        hc, ec, gc = args
        u = u_tab[ec]
        a = jax.nn.gelu(jnp.einsum('td,thkd->thk', hc, u), approximate=False)
        return jnp.einsum('thk,thkd->td', gc * a, v_tab[ec])

    out = lax.map(one_chunk, (ht.reshape(n_c, Pc, D),
                              experts.reshape(n_c, Pc, PEER_HEADS, PEER_TOPK),
                              gates.reshape(n_c, Pc, PEER_HEADS, PEER_TOPK)))
    return out.reshape(B, S, D)


def setup_inputs(seed: int = 0) -> dict:
    key = jax.random.key(seed)
    ks = jax.random.split(key, 14)
    f32 = jnp.float32
    nrm = lambda k, shape, scale: jax.random.normal(k, shape, f32) * scale
    return {
        "x": nrm(ks[0], (BATCH, SEQ, D_MODEL), 1.0),
        "mix_norm_g": 1.0 + nrm(ks[1], (DEPTH, D_MODEL), 0.1),
        "w_in": nrm(ks[2], (DEPTH, D_MODEL, IN_WIDTH), D_MODEL ** -0.5),
        "ret_w_branch": nrm(ks[3], (DEPTH, RET_V_W, D_MODEL), RET_V_W ** -0.5),
        "moba_q_gain": 1.0 + nrm(ks[4], (DEPTH, MOBA_HEAD_DIM), 0.1),
        "moba_k_gain": 1.0 + nrm(ks[5], (DEPTH, MOBA_HEAD_DIM), 0.1),
        "moba_w_branch": nrm(ks[6], (DEPTH, MOBA_W, D_MODEL), MOBA_W ** -0.5),
        "rel_bias": nrm(ks[7], (REL_BUCKETS, MOBA_HEADS), 0.5),
        "w_out": nrm(ks[8], (DEPTH, D_MODEL, D_MODEL), D_MODEL ** -0.5),
        "ffn_norm_g": 1.0 + nrm(ks[9], (DEPTH, D_MODEL), 0.1),
        "peer_w_q": nrm(ks[10], (DEPTH, D_MODEL, PEER_HEADS * PEER_QUERY_DIM), D_MODEL ** -0.5),
        "peer_sub_keys": nrm(ks[11], (DEPTH, PEER_HEADS, 2, PEER_N_KEYS, PEER_QUERY_DIM // 2), (PEER_QUERY_DIM // 2) ** -0.5),
        "peer_u": nrm(ks[12], (DEPTH, PEER_N_EXPERTS, D_MODEL), D_MODEL ** -0.5),
        "peer_v": nrm(ks[13], (DEPTH, PEER_N_EXPERTS, D_MODEL), D_MODEL ** -0.5),
    }


def reference(x, mix_norm_g, w_in, ret_w_branch, moba_q_gain, moba_k_gain, moba_w_branch,
              rel_bias, w_out, ffn_norm_g, peer_w_q, peer_sub_keys, peer_u, peer_v):
    B, S, D = x.shape
    pos = jnp.arange(S)
    offsets = tuple(int(o) for o in np.cumsum(IN_SIZES)[:-1])
    for l in range(DEPTH):
        h = rmsnorm(x, mix_norm_g[l])
        proj = h @ w_in[l]
        rq, rk, rv, rg, mq, mk, mv, ga, gb = jnp.split(proj, offsets, axis=-1)
        rq = rotary(to_heads(rq, RET_HEADS), pos)
        rk = rotary(to_heads(rk, RET_HEADS), pos) * (RET_QK_DIM ** -0.5)
        ret = retention(rq, rk, to_heads(rv, RET_HEADS))
        y_a = (from_heads(ret) * jax.nn.silu(rg)) @ ret_w_branch[l]
        mq = head_rmsnorm(to_heads(mq, MOBA_HEADS), moba_q_gain[l]) * (MOBA_HEAD_DIM ** -0.5)
        mk = head_rmsnorm(to_heads(mk, MOBA_HEADS), moba_k_gain[l])
        att = moba_attention(mq, mk, to_heads(mv, MOBA_HEADS), rel_bias)
        y_b = from_heads(att) @ moba_w_branch[l]
        merged = jax.nn.sigmoid(ga) * y_a + jax.nn.sigmoid(gb) * y_b
        x = x + merged @ w_out[l]
        h2 = rmsnorm(x, ffn_norm_g[l])
        x = x + peer(h2, peer_w_q[l], peer_sub_keys[l], peer_u[l], peer_v[l])
    return x
```

```python
import contextlib
import math
import numpy as np
import ml_dtypes
import concourse.bass as bass
import concourse.mybir as mybir
from concourse.bass_utils import run_bass_kernel_spmd

F32 = mybir.dt.float32
BF16 = mybir.dt.bfloat16
ALU = mybir.AluOpType
AF = mybir.ActivationFunctionType
AX = mybir.AxisListType

NCORES = 8
S_OWN = 2048
S_WIN = 4096
D = 1024
EPS = 1e-6
STRIP_W = 4608
EV_LEN = 4736
FUSE_WAIT = True
SAME_ENGINE_RAW_ONLY = True
OFFS = {"rq": 0, "rk": 512, "rv": 1024, "rg": 2048, "mq": 3072, "mk": 3584, "mv": 4096, "ga": 4608, "gb": 5632}


class _Op:
    __slots__ = ("eng", "fn", "dma", "deps", "seq", "need_inc", "inc_val", "dsem", "dval", "waits")


class MK:
    ENGS = ("pe", "dve", "act", "pool", "sp")

    def __init__(self, nc):
        self.nc = nc
        self.ops = []
        self.last_w = {}
        self.readers = {}
        self.dma_tot = {}
        self.eng_n = {e: 0 for e in self.ENGS}
        self.last_comp = {}
        self.last_dma = {}
        self.wdeps = {}

    def _new(self, eng, fn, dma):
        o = _Op()
        o.eng = eng; o.fn = fn; o.dma = dma; o.need_inc = False; o.inc_val = 0
        o.dsem = None; o.dval = 0; o.waits = None; o.deps = {}
        o.seq = self.eng_n[eng]
        self.eng_n[eng] += 1
        return o

    def op(self, eng, fn, reads=(), writes=(), dma=False, sem_key=None, nowaw=False):
        o = self._new(eng, fn, dma)
        idx = len(self.ops)
        deps = o.deps
        for t in reads:
            j = self.last_w.get(t)
            if j is not None:
                deps[j] = "raw"
        for t in writes:
            j = self.last_w.get(t)
            if nowaw and j is not None and self.ops[j].dma:
                new = self.wdeps.get(t, {})
            else:
                new = {}
                if j is not None:
                    new[j] = "waw"
                for r in self.readers.get(t, ()):
                    new[r] = "war"
                self.wdeps[t] = new
            for r, kind in new.items():
                if r not in deps:
                    deps[r] = kind
        if dma:
            k = sem_key if sem_key is not None else (writes[0] if writes else ("dma", idx))
            self.dma_tot[k] = self.dma_tot.get(k, 0) + 16
            o.dsem = k; o.dval = self.dma_tot[k]
            self.last_dma[k] = idx
        else:
            self.last_comp[eng] = idx
        self.ops.append(o)
        for t in reads:
            self.readers.setdefault(t, []).append(idx)
        for t in writes:
            self.last_w[t] = idx
            self.readers[t] = []
        return idx

    def barrier(self):
        deps = {j: "raw" for j in self.last_comp.values()}
        deps.update({j: "raw" for j in self.last_dma.values()})
        for e in self.ENGS:
            o = self._new(e, None, False)
            o.deps = dict(deps)
            self.ops.append(o)
        self.last_w = {}
        self.readers = {}

    def emit(self, final_wait_tokens=()):
        nc = self.nc
        fo = self._new("sp", None, False)
        for t in final_wait_tokens:
            j = self.last_w.get(t)
            if j is not None:
                fo.deps[j] = "raw"
        ops = self.ops + [fo]
        seen_e = {e: {x: -1 for x in self.ENGS} for e in self.ENGS}
        seen_d = {e: {} for e in self.ENGS}
        for o in ops:
            w = []
            E = o.eng
            best_e = {}
            best_d = {}
            for j, kind in o.deps.items():
                p = ops[j]
                if p.dma:
                    if p.dval > best_d.get(p.dsem, 0):
                        best_d[p.dsem] = p.dval
                else:
                    if p.fn is None:
                        continue
                    if p.eng == E and not o.dma and o.fn is not None:
                        if E == "pe" or E == "sp":
                            continue
                        if SAME_ENGINE_RAW_ONLY and kind != "raw":
                            continue
                    if p.seq > best_e.get(p.eng, -1):
                        best_e[p.eng] = p.seq
            for X, s in best_e.items():
                if s > seen_e[E][X]:
                    seen_e[E][X] = s
                    w.append(("e", X, s))
            for k, v in best_d.items():
                if v > seen_d[E].get(k, 0):
                    seen_d[E][k] = v
                    w.append(("d", k, v))
            o.waits = w
        by_eng_seq = {e: {} for e in self.ENGS}
        for o in ops:
            if not o.dma:
                by_eng_seq[o.eng][o.seq] = o
        for o in ops:
            for w in o.waits:
                if w[0] == "e":
                    by_eng_seq[w[1]][w[2]].need_inc = True
        cnt = {e: 0 for e in self.ENGS}
        for o in ops:
            if not o.dma and o.need_inc:
                cnt[o.eng] += 1
                o.inc_val = cnt[o.eng]
        self.inc_counts = dict(cnt)
        es = contextlib.ExitStack()
        esem = {e: es.enter_context(nc.semaphore("es_" + e)) for e in self.ENGS}
        dsem = {}
        for k in self.dma_tot:
            dsem[k] = es.enter_context(nc.semaphore("ds_%d" % len(dsem)))
        self.n_dsem = len(dsem)
        prog = {e: [] for e in self.ENGS}
        for o in ops:
            prog[o.eng].append(o)

        def run(engname, eng):
            for o in prog[engname]:
                ws = list(o.waits)
                fused = None
                if o.fn is not None and ws and FUSE_WAIT:
                    fused = ws.pop()
                for w in ws:
                    if w[0] == "e":
                        eng.wait_ge(esem[w[1]], by_eng_seq[w[1]][w[2]].inc_val)
                    else:
                        eng.wait_ge(dsem[w[1]], w[2])
                if o.fn is None:
                    continue
                ins = o.fn(eng)
                if fused is not None:
                    if fused[0] == "e":
                        ins._wait_ge(esem[fused[1]], by_eng_seq[fused[1]][fused[2]].inc_val)
                    else:
                        ins._wait_ge(dsem[fused[1]], fused[2])
                if o.dma:
                    ins.then_inc(dsem[o.dsem], 16)
                elif o.need_inc:
                    ins.then_inc(esem[o.eng], 1)

        with es:
            with nc.Block() as block:
                @block.tensor
                def _(e):
                    run("pe", e)

                @block.vector
                def _(e):
                    run("dve", e)

                @block.scalar
                def _(e):
                    run("act", e)

                @block.gpsimd
                def _(e):
                    run("pool", e)

                @block.sync
                def _(e):
                    run("sp", e)


def _rel_bucket_np(dist):
    n = np.maximum(dist, 0)
    nf = np.maximum(n, 1).astype(np.float32)
    large = 16 + (np.log(nf / np.float32(16)) / np.float32(math.log(2048 / 16)) * np.float32(16)).astype(np.int32)
    large = np.minimum(large, 31)
    return np.where(n < 16, n, large)


def _consts(half):
    c = {}
    bf = ml_dtypes.bfloat16
    c["ident_bf"] = np.eye(128, dtype=np.float32).astype(bf)
    c["J_bf"] = np.eye(128, dtype=np.float32)[::-1].copy().astype(bf)
    c["identf"] = np.eye(128, dtype=np.float32)
    w = np.arange(S_WIN)
    pos = (w - S_OWN + S_OWN * half).astype(np.float32)
    inv = (np.float32(10000.0) ** (-np.arange(32, dtype=np.float32) / np.float32(32))).astype(np.float32)
    ang = pos[None, :] * inv[:, None]
    cs, sn = np.cos(ang).astype(np.float32), np.sin(ang).astype(np.float32)
    c["cosT"] = np.concatenate([cs, cs], 0)
    c["sinT"] = np.concatenate([-sn, sn], 0)
    c["csT"] = np.concatenate([c["cosT"], c["sinT"]], 0)
    fold = np.zeros((128, 128), np.float32)
    for p in range(128):
        fold[p, p % 64] = 1.0
        fold[p, p % 64 + 64] = 1.0
    c["fold"] = fold.astype(bf)
    hh = np.arange(8, dtype=np.float32)
    log_g = np.log1p(-np.exp2(-5.0 - hh)).astype(np.float32)
    i = np.arange(128, dtype=np.float32)
    rel = i[None, :] - i[:, None]
    DT = np.where(rel[None] >= 0, np.exp(np.maximum(rel, 0)[None] * log_g[:, None, None]), 0.0) * 0.125
    c["DT"] = np.ascontiguousarray(DT.transpose(1, 0, 2)).astype(np.float32)
    QD = np.exp((i + 1.0)[None, :] * log_g[:, None])
    c["QD"] = np.ascontiguousarray(np.broadcast_to(QD[None], (128, 8, 128))).astype(np.float32)
    kd = np.exp((127.0 - i)[:, None] * log_g[None, :]) * 0.125
    c["kdec"] = kd.astype(np.float32)
    c["gC"] = [float(np.exp(128.0 * lg)) for lg in log_g]
    idx = np.arange(EV_LEN)
    dist = idx - 511
    bk = _rel_bucket_np(dist)
    OH = np.zeros((33, EV_LEN), np.float32)
    valid = dist >= 0
    OH[bk[valid], idx[valid]] = 1.0
    OH[32, ~valid] = 1.0
    c["OH33"] = OH
    c["negrow"] = np.full((1, 8), -30000.0, np.float32)
    cp = np.zeros((128, 16), np.float32)
    if half == 0:
        cp[:, :8] = -1e30
    c["ctxpen"] = cp
    bo = np.zeros((16, S_WIN), np.float32)
    for n in range(16):
        bo[n, n * 256:(n + 1) * 256] = 1.0
    c["blkoh"] = bo.astype(bf)
    return c


class PB:
    def __init__(self, dbg=None):
        self.dbg = dbg or ()
        self.nc = bass.Bass("TRN2", target_bir_lowering=False)
        self.mk = MK(self.nc)
        self.dq = 0

    def din(self, name, shape, dt=F32):
        return self.nc.dram_tensor(name, list(shape), dt, kind="ExternalInput").ap()

    def dout(self, name, shape, dt=F32):
        return self.nc.dram_tensor(name, list(shape), dt, kind="ExternalOutput").ap()

    def dscr(self, name, shape, dt=F32):
        return self.nc.dram_tensor(name, list(shape), dt, kind="Internal").ap()

    def dma(self, out, in_, reads=(), writes=(), eng=None, key=None, nowaw=False, slow=False):
        if eng is None:
            eng = "sp"
        if slow:
            return self.mk.op(eng, lambda e: e.dma_start(out=out, in_=in_, allow_slow_non_contiguous=True), reads=reads,
                              writes=writes, dma=True, sem_key=key, nowaw=nowaw)
        return self.mk.op(eng, lambda e: e.dma_start(out=out, in_=in_), reads=reads, writes=writes,
                          dma=True, sem_key=key, nowaw=nowaw)

    def mm(self, out, lhsT, rhs, start, stop, reads=(), writes=()):
        return self.mk.op("pe", lambda e: e.matmul(out, lhsT=lhsT, rhs=rhs, start=start, stop=stop),
                          reads=reads, writes=writes)

    def tr(self, out, in_, ident, reads=(), writes=()):
        return self.mk.op("pe", lambda e: e.transpose(out=out, in_=in_, identity=ident), reads=reads, writes=writes)

    def act(self, out, in_, func, reads=(), writes=(), **kw):
        return self.mk.op("act", lambda e: e.activation(out=out, in_=in_, func=func, **kw), reads=reads, writes=writes)

    def tt(self, eng, out, in0, in1, op, reads=(), writes=()):
        return self.mk.op(eng, lambda e: e.tensor_tensor(out=out, in0=in0, in1=in1, op=op), reads=reads, writes=writes)

    def ts(self, eng, out, in0, s1, s2, op0, op1=None, reads=(), writes=()):
        if op1 is None:
            return self.mk.op(eng, lambda e: e.tensor_scalar(out=out, in0=in0, scalar1=s1, scalar2=s2, op0=op0),
                              reads=reads, writes=writes)
        return self.mk.op(eng, lambda e: e.tensor_scalar(out=out, in0=in0, scalar1=s1, scalar2=s2, op0=op0, op1=op1),
                          reads=reads, writes=writes)

    def stt(self, eng, out, in0, scalar, in1, op0, op1, reads=(), writes=()):
        return self.mk.op(eng, lambda e: e.scalar_tensor_tensor(out=out, in0=in0, scalar=scalar, in1=in1, op0=op0, op1=op1),
                          reads=reads, writes=writes)

    def cp(self, eng, out, in_, reads=(), writes=()):
        if eng == "act":
            return self.mk.op("act", lambda e: e.copy(out=out, in_=in_), reads=reads, writes=writes)
        return self.mk.op(eng, lambda e: e.tensor_copy(out=out, in_=in_), reads=reads, writes=writes)

    def memset(self, eng, ap, val, writes=()):
        return self.mk.op(eng, lambda e: e.memset(ap, val), writes=writes)

    def recip(self, out, in_, reads=(), writes=()):
        return self.mk.op("dve", lambda e: e.reciprocal(out=out, in_=in_), reads=reads, writes=writes)

    def build(self):
        nc, mk = self.nc, self.mk
        shapes = {"xw": ([S_WIN, D], F32), "w_in": ([D, 6656], F32), "ret_w": ([1024, D], F32), "moba_w": ([512, D], F32),
                  "w_out": ([D, D], F32), "peer_wq": ([D, 2048], F32), "sub_keys": ([16, 128, 128], F32),
                  "peer_u": ([16384, D], F32), "peer_v": ([16384, D], F32), "g1": ([D], F32), "g2": ([D], F32),
                  "mqg": ([1, 64], F32), "mkg": ([1, 64], F32), "rel_bias": ([32, 8], F32),
                  "ident_bf": ([128, 128], BF16), "J_bf": ([128, 128], BF16), "identf": ([128, 128], F32),
                  "cosT": ([64, S_WIN], F32), "sinT": ([64, S_WIN], F32), "DT": ([128, 8, 128], F32),
                  "QD": ([128, 8, 128], F32), "csT": ([128, S_WIN], F32), "fold": ([128, 128], BF16), "kdec": ([128, 8], F32), "OH33": ([33, EV_LEN], F32),
                  "negrow": ([1, 8], F32), "ctxpen": ([128, 16], F32), "blkoh": ([16, S_WIN], BF16)}
        pb = self

        class Lazy(dict):
            def __missing__(self, k):
                sh, dt = shapes[k]
                v = pb.din(k, sh, dt)
                self[k] = v
                return v
        I = Lazy()
        self.I = I
        self.y = self.dout("y", [S_OWN, D])
        self.dbg_out = {}
        self.gC = _consts(0)["gC"]

        with contextlib.ExitStack() as es0:
            self.es0 = es0
            sb0 = lambda n, s, d=F32: es0.enter_context(nc.sbuf_tensor(n, list(s), d))
            self.ps = [es0.enter_context(nc.psum_tensor("ps%d" % i, [128, 512], F32)) for i in range(8)]
            self.ident = sb0("ident", [128, 128], BF16)
            self.Jm = sb0("Jm", [128, 128], BF16)
            self.g1col = sb0("g1col", [128, 8])
            self.g2col = sb0("g2col", [128, 8])
            self.hTo = sb0("hTo", [128, 8, S_OWN], BF16)
            self.dma(self.ident[:], I["ident_bf"][:, :], writes=["ident"])
            self.dma(self.Jm[:], I["J_bf"][:, :], writes=["Jm"])
            self.dma(self.g1col[:], I["g1"].rearrange("(k p) -> p k", p=128), writes=["g1col"], slow=True)
            self.dma(self.g2col[:], I["g2"].rearrange("(k p) -> p k", p=128), writes=["g2col"], slow=True)
            self.do_peer = not any(k in self.dbg for k in ("stop0", "stopA1", "stopA2", "stopB"))
            with contextlib.ExitStack() as esAB:
                sbAB = lambda n, s, d=F32: esAB.enter_context(nc.sbuf_tensor(n, list(s), d))
                self.retT = sbAB("retT", [128, 8, S_OWN], BF16)
                self.attT = sbAB("attT", [64, 8, S_OWN], BF16)
                with contextlib.ExitStack() as esA:
                    sbA = lambda n, s, d=F32: esA.enter_context(nc.sbuf_tensor(n, list(s), d))
                    self.hTc = sbA("hTc", [128, 8, S_OWN], BF16)
                    with contextlib.ExitStack() as es:
                        self.phase0(lambda n, s, d=F32: es.enter_context(nc.sbuf_tensor(n, list(s), d)))
                    mk.barrier()
                    if "hT" in self.dbg:
                        d = self.dout("d_hTo", [128, 8, S_OWN], BF16)
                        self.dma(d[:, :, :], self.hTo[:], writes=["d_hTo"])
                        self.dbg_out["d_hTo"] = "d_hTo"
                        d = self.dout("d_hTc", [128, 8, S_OWN], BF16)
                        self.dma(d[:, :, :], self.hTc[:], writes=["d_hTc"])
                        self.dbg_out["d_hTc"] = "d_hTc"
                    if "stop0" not in self.dbg and "skipA1" not in self.dbg:
                        with contextlib.ExitStack() as es:
                            self.phaseA1(lambda n, s, d=F32: es.enter_context(nc.sbuf_tensor(n, list(s), d)))
                        mk.barrier()
                    if "stop0" not in self.dbg and "stopA1" not in self.dbg:
                        with contextlib.ExitStack() as es:
                            self.phaseAE(lambda n, s, d=F32: es.enter_context(nc.sbuf_tensor(n, list(s), d)))
                        mk.barrier()
                        with contextlib.ExitStack() as es:
                            self.phaseA2(lambda n, s, d=F32: es.enter_context(nc.sbuf_tensor(n, list(s), d)))
                        mk.barrier()
                if "attT" in self.dbg:
                    d = self.dout("d_attT", [64, 8, S_OWN], BF16)
                    self.dma(d[:, :, :], self.attT[:], writes=["d_attT"])
                    self.dbg_out["d_attT"] = "d_attT"
                self.xm_scr = self.dscr("xm_scr", [S_OWN, D])
                if "stop0" not in self.dbg and "stopA1" not in self.dbg and "stopA2" not in self.dbg:
                    with contextlib.ExitStack() as es:
                        self.phaseB(lambda n, s, d=F32: es.enter_context(nc.sbuf_tensor(n, list(s), d)))
                    mk.barrier()
            if "xm" in self.dbg:
                d = self.dout("d_xm", [S_OWN, D])
                self.dma(d[:, :], self.xm_scr[:, :], writes=["d_xm"])
                self.dbg_out["d_xm"] = "d_xm"
                if "retT" in self.dbg:
                    d = self.dout("d_retT", [128, 8, S_OWN], BF16)
                    self.dma(d[:, :, :], self.retT[:], reads=[("retT", h, c) for h in range(8) for c in range(16)], writes=["d_retT"])
                    self.dbg_out["d_retT"] = "d_retT"
            if self.do_peer:
                with contextlib.ExitStack() as es:
                    self.phaseC(lambda n, s, d=F32: es.enter_context(nc.sbuf_tensor(n, list(s), d)))
            else:
                self.dma(self.y[:, :], self.xm_scr[:, :], writes=["y"])
            mk.emit(final_wait_tokens=list(self.dbg_out.values()) + ["y", ("y", 0), ("y", 1)])
        return nc

    def hT(self, kc, col0, n):
        if col0 >= S_OWN:
            return self.hTo[:, kc, col0 - S_OWN:col0 - S_OWN + n]
        assert col0 + n <= S_OWN
        return self.hTc[:, kc, col0:col0 + n]

    def hT_tok(self, col0):
        t = col0 // 128
        return ("hT", t)

    def phase0(self, sb):
        I = self.I
        xt = [sb("xt%d" % i, [128, D]) for i in range(2)]
        xn = [sb("xn%d" % i, [128, D], BF16) for i in range(2)]
        junk = sb("junk0", [128, D], BF16)
        ss = sb("ss0", [128, 32])
        rs = sb("rs0", [128, 32])
        tpb = [self.ps[4][:, :].bitcast(BF16), self.ps[7][:, :].bitcast(BF16)]
        def evac(t):
            s = t % 2
            tp = tpb[s].rearrange("p (k c) -> p k c", k=8)
            dst = (self.hTc if t < 16 else self.hTo)[:, :, (t % 16) * 128:(t % 16 + 1) * 128]
            self.tt("dve", dst, tp, self.g1col[:].unsqueeze(2).to_broadcast([128, 8, 128]), ALU.mult,
                    reads=[("bank", (4, 7)[s]), "g1col"], writes=[("hT", t)])

        for t in range(32):
            s = t % 2
            self.dma(xt[s][:], I["xw"][t * 128:(t + 1) * 128, :], writes=[("xt", s)])
            self.act(junk[:], xt[s][:], AF.Square, reads=[("xt", s)], writes=["junk0", ("ss", t)], accum_out=ss[:, t:t + 1])
            self.act(rs[:, t:t + 1], ss[:, t:t + 1], AF.Sqrt, reads=[("ss", t)], writes=[("rs", t)], scale=1.0 / D, bias=EPS)
            self.recip(rs[:, t:t + 1], rs[:, t:t + 1], reads=[("rs", t)], writes=[("rs", t)])
            self.ts("dve", xn[s][:], xt[s][:], rs[:, t:t + 1], None, ALU.mult, reads=[("xt", s), ("rs", t)], writes=[("xn", s)])
            tp = tpb[s].rearrange("p (k c) -> p k c", k=8)
            for kc in range(8):
                self.tr(tp[:, kc, :], xn[s][:, kc * 128:(kc + 1) * 128], self.ident[:],
                        reads=[("xn", s), "ident"], writes=[("bank", (4, 7)[s])])
            if t >= 1:
                evac(t - 1)
        evac(31)

    def phaseA1(self, sb):
        I, ps = self.I, self.ps
        stage = sb("stageA", [128, 8, 512])
        wbf = sb("wbfA", [128, 8, 512], BF16)
        qT = sb("qT", [128, S_OWN], BF16)
        qdT = sb("qdT", [128, S_OWN], BF16)
        kT = sb("kT", [128, S_WIN], BF16)
        k_tm = sb("k_tm", [128, 32, 128], BF16)
        foldm = sb("foldm", [128, 128], BF16)
        v_tm = sb("v_tm", [128, 32, 128], BF16)
        sg_tm = sb("sg_tm", [128, 16, 128], BF16)
        cs = [sb("cs%d" % i, [128, 512]) for i in range(2)]
        t1 = [sb("t1_%d" % i, [128, 512]) for i in range(2)]
        ub = [sb("ub%d" % i, [128, 512], BF16) for i in range(2)]
        DTs = sb("DTs", [128, 128]); QDs = sb("QDs", [128, 128]); kdec = sb("kdec_s", [128, 8])
        state = sb("state", [128, 128]); state_bf = sb("state_bf", [128, 128], BF16)
        PT = [sb("PT%d" % i, [128, 128], BF16) for i in range(2)]
        st6 = [sb("st6_%d" % i, [128, 6]) for i in range(3)]; mv = [sb("mv%d" % i, [128, 2]) for i in range(3)]
        rstd = [sb("rstdA%d" % i, [128, 1]) for i in range(3)]; o_sb = [sb("o_sb%d" % i, [128, 128]) for i in range(3)]
        yn = [sb("yn%d" % i, [128, 128]) for i in range(2)]; A_tm = [sb("A_tm%d" % i, [128, 128], BF16) for i in range(2)]
        self.dma(kdec[:], I["kdec"][:, :], writes=["kdec"])
        self.dma(foldm[:], I["fold"][:, :], writes=["foldm"])
        w3 = I["w_in"].rearrange("(k p) c -> p k c", p=128)
        csn = [0]

        def fm_proj(wc0, dstT, col0, ntile, is_q, h):
            for T in range(ntile):
                c0 = col0 + T * 512
                s = csn[0] % 2
                csn[0] += 1
                self.dma(cs[s][:], I["csT"][:, c0:c0 + 512], writes=[("cs", s)])
                for kc in range(8):
                    self.mm(ps[s][:, :], wbf[:, kc, wc0:wc0 + 128], self.hT(kc, c0, 512), kc == 0, kc == 7,
                            reads=["wbfA"] + [("hT", c0 // 128 + i) for i in range(4)], writes=[("bank", s)])
                dl = c0 - col0
                if is_q:
                    self.tt("dve", t1[s][:], ps[s][:, :], cs[s][:], ALU.mult, reads=[("bank", s), ("cs", s)], writes=[("t1", s)])
                    self.cp("act", dstT[:, dl:dl + 512], t1[s][:], reads=[("t1", s)], writes=[("qT", T)])
                    self.tt("pool", qdT[:, dl:dl + 512].rearrange("p (c i) -> p c i", c=4), t1[s][:].rearrange("p (c i) -> p c i", c=4),
                            QDs[:].unsqueeze(1).to_broadcast([128, 4, 128]), ALU.mult,
                            reads=[("t1", s), "QDs"], writes=[("qdT", T)])
                else:
                    self.tt("dve", ub[s][:], ps[s][:, :], cs[s][:], ALU.mult, reads=[("bank", s), ("cs", s)], writes=[("ub", s)])
                    self.mm(ps[2 + s][:, :], foldm[:], ub[s][:], True, True, reads=["foldm", ("ub", s)], writes=[("bank", 2 + s)])
                    self.cp("act", dstT[:, dl:dl + 512], ps[2 + s][:, :], reads=[("bank", 2 + s)], writes=[("kT", T)])

        for h in range(8):
            self.dma(DTs[:], I["DT"][:, h, :], writes=["DTs"])
            self.dma(QDs[:], I["QD"][:, h, :], writes=["QDs"])
            q0, k0 = OFFS["rq"] + h * 64, OFFS["rk"] + h * 64
            v0, g0 = OFFS["rv"] + h * 128, OFFS["rg"] + h * 128
            segs = [(0, q0, 64), (64, q0 + 32, 32), (96, q0, 32), (128, k0, 64), (192, k0 + 32, 32), (224, k0, 32),
                    (256, v0, 128), (384, g0, 128)]
            for (d0, s0, n) in segs:
                self.dma(stage[:, :, d0:d0 + n], w3[:, :, s0:s0 + n], writes=["stageA"], nowaw=True)
            self.cp("dve", wbf[:, 0:4, :], stage[:, 0:4, :], reads=["stageA"], writes=["wbfA"])
            self.cp("act", wbf[:, 4:8, :], stage[:, 4:8, :], reads=["stageA"], writes=["wbfA"])
            fm_proj(0, qT, S_OWN, 4, True, h)
            fm_proj(128, kT, 0, 8, False, h)
            for b in range(4):
                tpk = ps[4][:, :].bitcast(BF16).rearrange("p (t d) -> p t d", t=8)
                for tt_ in range(8):
                    t = b * 8 + tt_
                    self.tr(tpk[:, tt_, :], kT[:, t * 128:(t + 1) * 128], self.ident[:],
                            reads=[("kT", t // 4), "ident"], writes=[("bank", 4)])
                self.ts("dve", k_tm[:, b * 8:(b + 1) * 8, :], tpk, kdec[:, h:h + 1], None, ALU.mult,
                        reads=[("bank", 4), "kdec"], writes=[("k_tm", b // 2)])
            for t in range(32):
                bkv = (2, 3, 5, 6)[t % 4]
                own = t >= 16
                n = 256 if own else 128
                pv = ps[bkv][:, 0:n]
                for kc in range(8):
                    self.mm(pv, self.hT(kc, t * 128, 128), wbf[:, kc, 256:256 + n], kc == 0, kc == 7,
                            reads=["wbfA", ("hT", t)], writes=[("bank", bkv)])
                self.cp("act", v_tm[:, t, :], pv[:, 0:128], reads=[("bank", bkv)], writes=[("v_tm", t)])
                if own:
                    self.act(sg_tm[:, t - 16, :], pv[:, 128:256], AF.Silu, reads=[("bank", bkv)], writes=[("sg", t - 16)])
            self.memset("pool", state[:], 0.0, writes=["state"])
            self.memset("pool", state_bf[:], 0.0, writes=["state_bf"])
            gC = self.gC[h]

            def issue_front(n):
                sl = n % 2
                self.mm(ps[5 + sl][:, 0:128], k_tm[:, n, :], v_tm[:, n, :], True, True,
                        reads=[("k_tm", n // 16), ("v_tm", n)], writes=[("bank", 5 + sl)])
                if n >= 16:
                    c = n - 16
                    self.mm(ps[sl][:, 0:128], kT[:, n * 128:(n + 1) * 128], qT[:, c * 128:(c + 1) * 128], True, True,
                            reads=[("kT", n // 4), ("qT", c // 4)], writes=[("bank", sl)])

            issue_front(0)

            def post_stats(n):
                k = n % 3
                self.mk.op("dve", lambda e, k=k: e.bn_stats(out=st6[k][:], in_=o_sb[k][:]), reads=[("o_sb", k)], writes=[("st6", k)])
                self.mk.op("dve", lambda e, k=k: e.bn_aggr(out=mv[k][:], in_=st6[k][:]), reads=[("st6", k)], writes=[("mv", k)])
                self.act(rstd[k][:], mv[k][:, 1:2], AF.Sqrt, reads=[("mv", k)], writes=[("rstdA", k)], bias=EPS, scale=1.0)

            def post_norm(n):
                k = n % 3
                c = n - 16
                sl = n % 2
                self.recip(rstd[k][:], rstd[k][:], reads=[("rstdA", k)], writes=[("rstdA", k)])
                self.ts("dve", yn[sl][:], o_sb[k][:], mv[k][:, 0:1], rstd[k][:, 0:1], ALU.subtract, ALU.mult,
                        reads=[("o_sb", k), ("mv", k), ("rstdA", k)], writes=[("yn", sl)])
                self.tt("pool", A_tm[sl][:], yn[sl][:], sg_tm[:, c, :], ALU.mult, reads=[("yn", sl), ("sg", c)], writes=[("A_tm", sl)])
                atr = ps[(7, 4)[sl]][:, :].bitcast(BF16)[:, 0:128]
                self.tr(atr, A_tm[sl][:], self.ident[:], reads=[("A_tm", sl), "ident"], writes=[("bank", (7, 4)[sl])])
                self.cp("act", self.retT[:, h, c * 128:(c + 1) * 128], atr, reads=[("bank", (7, 4)[sl])], writes=[("retT", h, c)])

            for n in range(32):
                sl = n % 2
                if n + 1 < 32:
                    issue_front(n + 1)
                if n >= 16:
                    c = n - 16
                    self.tt("dve", PT[sl][:], ps[sl][:, 0:128], DTs[:], ALU.mult,
                            reads=[("bank", sl), "DTs"], writes=[("PT", sl)])
                    o_ps = ps[2 + sl][:, 0:128]
                    self.mm(o_ps, PT[sl][:], v_tm[:, n, :], True, False, reads=[("PT", sl), ("v_tm", n)], writes=[("bank", 2 + sl)])
                    self.mm(o_ps, qdT[:, c * 128:(c + 1) * 128], state_bf[:], False, True,
                            reads=[("qdT", c // 4), "state_bf"], writes=[("bank", 2 + sl)])
                    self.cp("act", o_sb[n % 3][:], o_ps, reads=[("bank", 2 + sl)], writes=[("o_sb", n % 3)])
                self.stt("dve", state[:], state[:], gC, ps[5 + sl][:, 0:128], ALU.mult, ALU.add,
                         reads=["state", ("bank", 5 + sl)], writes=["state"])
                if n >= 15:
                    self.cp("act", state_bf[:], state[:], reads=["state"], writes=["state_bf"])
                if n - 2 >= 16:
                    post_norm(n - 2)
                if n - 1 >= 16:
                    post_stats(n - 1)
            post_norm(30)
            post_stats(31)
            post_norm(31)


    def phaseAE(self, sb):
        I, ps = self.I, self.ps
        self.Escr = self.dscr("Escr", [8, EV_LEN], BF16)
        l33 = sb("l33", [33, 8])
        ohs = [sb("ohs%d" % i, [33, 512]) for i in range(2)]
        Ev = sb("Ev", [8, EV_LEN], BF16)
        self.dma(l33[0:32, :], I["rel_bias"][:, :], writes=["l33"])
        self.dma(l33[32:33, :], I["negrow"][:, :], writes=["l33"], nowaw=True)
        for j in range(10):
            c0 = j * 512
            n = min(512, EV_LEN - c0)
            s = j % 2
            self.dma(ohs[s][:, 0:n], I["OH33"][:, c0:c0 + n], writes=[("ohs", s)])
            self.mm(ps[s][0:8, 0:n], l33[:, :], ohs[s][:, 0:n], True, True, reads=["l33", ("ohs", s)], writes=[("bank", s)])
            self.act(Ev[:, c0:c0 + n], ps[s][0:8, 0:n], AF.Exp, reads=[("bank", s)], writes=["Ev"])
        self.dma(self.Escr[:, :], Ev[:], reads=["Ev"], writes=["Escr"])

    def phaseA2(self, sb):
        I, ps, mk = self.I, self.ps, self.mk
        stage = sb("stageM", [128, 8, 192])
        wbf = sb("wbfM", [128, 8, 192], BF16)
        q_tm = sb("q_tm", [128, 16, 64], BF16)
        k_tm = sb("k_tm2", [128, 32, 64], BF16)
        v_aug = sb("v_aug", [128, 32, 65], BF16)
        qTa = sb("qTa", [80, S_OWN], BF16)
        kTa = sb("kTa", [80, S_WIN], BF16)
        kmT = sb("kmT", [64, 16]); kmTb = sb("kmTb", [64, 16], BF16)
        gm = sb("gm", [128, 16, 16]); m8 = sb("m8", [128, 8]); thr = sb("thr", [128, 1])
        nm_tm = sb("nm_tm", [128, 16, 16], BF16); nmT = sb("nmT", [16, S_OWN], BF16)
        Mp = sb("Mp", [128, STRIP_W], BF16); Ms = sb("Ms", [128, STRIP_W], BF16)
        expS = [sb("expS%d" % i, [128, 512], BF16) for i in range(3)]
        PTm = [sb("PTm%d" % i, [128, 512], BF16) for i in range(3)]
        recr = sb("recr", [65, 512]); bc_sb = sb("bc_sb", [64, 512], BF16); onesr = sb("onesr", [65, 64])
        ssqk = sb("ssqk", [128, 8]); rr = sb("rr", [128, 8]); junk = sb("junkM", [128, 64], BF16)
        gq8 = sb("gq8", [128, 64]); gk = sb("gk", [128, 64]); ctxp = sb("ctxp", [128, 16])
        self.dma(gq8[:], I["mqg"][0:1, :].partition_broadcast(128), writes=["gq8"])
        self.dma(gk[:], I["mkg"][0:1, :].partition_broadcast(128), writes=["gk"])
        self.dma(ctxp[:], I["ctxpen"][:, :], writes=["ctxp"])
        self.ts("dve", gq8[:], gq8[:], 0.125, None, ALU.mult, reads=["gq8"], writes=["gq8"])
        self.dma(kTa[64:80, :], I["blkoh"][:, :], writes=["kTa_oh"])
        self.memset("pool", v_aug[:, :, 64:65], 1.0, writes=["v_ones"])
        self.memset("pool", onesr[64:65, :], 1.0, writes=["onesr"])
        self.memset("pool", ssqk[:], 1.0, writes=[("ssqk", i) for i in range(4)])
        w3 = I["w_in"].rearrange("(k p) c -> p k c", p=128)
        tpb = ps[2][:, :].bitcast(BF16)
        B2 = ("bank", 2)
        sct = [0]
        self.UT_scr = self.dscr("UT_scr", [128, 128, 8, 128], BF16)
        self.V_scr = self.dscr("V_scr", [16384, D], BF16)
        if self.do_peer:
            ust = sb("ustP", [128, D]); ubf = sb("ubfP", [128, D], BF16); utb = sb("utbP", [128, 8, 128], BF16)
            vst = sb("vstP", [128, D]); vbf = sb("vbfP", [128, D], BF16)
        pstep = [0]

        tp0 = ps[0][:, :].bitcast(BF16).rearrange("p (k c) -> p k c", k=8)
        B0 = ("bank", 0)

        def P_T():
            a = pstep[0] - 2
            if self.do_peer and 0 <= a < 128:
                for kc in range(8):
                    self.tr(tp0[:, kc, :], ubf[:, kc * 128:(kc + 1) * 128], self.ident[:], reads=["ubfP", "ident"], writes=[B0])
                self.dma(self.V_scr[a * 128:(a + 1) * 128, :], vbf[:], reads=["vbfP"], writes=["V_scrP"], key="V_scrP")

        def P_C():
            g = pstep[0]
            pstep[0] += 1
            if not self.do_peer:
                return
            a = g - 2
            if 0 <= a < 128:
                self.cp("dve", utb[:], tp0, reads=[B0], writes=["utbP"])
                self.dma(self.UT_scr[a, :, :, :], utb[:], reads=["utbP"], writes=["UT_scrP"], key="UT_scrP")
            a = g - 1
            if 0 <= a < 128:
                self.cp("act", ubf[:], ust[:], reads=["ustP"], writes=["ubfP"])
                self.cp("dve", vbf[:], vst[:], reads=["vstP"], writes=["vbfP"])
            a = g
            if 0 <= a < 128:
                self.dma(ust[:], I["peer_u"][a * 128:(a + 1) * 128, :], writes=["ustP"])
                self.dma(vst[:], I["peer_v"][a * 128:(a + 1) * 128, :], writes=["vstP"])

        self.dma(Mp[:], bass.AP(self.Escr.tensor, 0, [[1, 128], [1, STRIP_W]]), reads=["Escr"], writes=["Mp"])
        for h in range(8):
            for i, nm_ in enumerate(("mq", "mk", "mv")):
                self.dma(stage[:, :, i * 64:(i + 1) * 64], w3[:, :, OFFS[nm_] + h * 64:OFFS[nm_] + (h + 1) * 64],
                         writes=["stageM"], nowaw=True)
            self.cp("dve", wbf[:, :, :], stage[:, :, :], reads=["stageM"], writes=["wbfM"])
            for j in range(9):
                bk = (6, 7)[j % 2]
                self.mm(ps[bk][:, :], self.Jm[:], Mp[:, j * 512:(j + 1) * 512], True, True, reads=["Jm", "Mp"], writes=[("bank", bk)])
                self.cp("act" if j % 2 else "dve", Ms[:, j * 512:(j + 1) * 512], ps[bk][:, :], reads=[("bank", bk)], writes=["Ms"])
            if h + 1 < 8:
                self.dma(Mp[:], bass.AP(self.Escr.tensor, (h + 1) * EV_LEN, [[1, 128], [1, STRIP_W]]), reads=["Escr"], writes=["Mp"])
            for t in range(32):
                own = t >= 16
                c0 = 0 if own else 64
                n = 192 - c0
                bk = (0, 1, 4, 5)[t % 4]
                q4 = t % 4
                pp = ps[bk][:, 0:n]
                for kc in range(8):
                    self.mm(pp, self.hT(kc, t * 128, 128), wbf[:, kc, c0:192], kc == 0, kc == 7,
                            reads=["wbfM", ("hT", t)], writes=[("bank", bk)])
                kcol = 64 - c0
                sq = ssqk[:, 2 * q4:2 * q4 + 2]
                rq = rr[:, 2 * q4:2 * q4 + 2]
                self.act(junk[:], pp[:, kcol:kcol + 64], AF.Square, reads=[("bank", bk)], writes=["junkM", ("ssqk", q4)], accum_out=sq[:, 0:1])
                if own:
                    self.act(junk[:], pp[:, 0:64], AF.Square, reads=[("bank", bk)], writes=["junkM", ("ssqk", q4)], accum_out=sq[:, 1:2])
                self.act(rq, sq, AF.Sqrt, reads=[("ssqk", q4)], writes=[("rr", q4)], scale=1.0 / 64, bias=EPS)
                self.recip(rq, rq, reads=[("rr", q4)], writes=[("rr", q4)])
                self.stt("dve", k_tm[:, t, :], pp[:, kcol:kcol + 64], rq[:, 0:1], gk[:], ALU.mult, ALU.mult,
                         reads=[("bank", bk), ("rr", q4), "gk"], writes=[("k_tm2", t // 8)])
                if own:
                    self.stt("dve", q_tm[:, t - 16, :], pp[:, 0:64], rq[:, 1:2], gq8[:], ALU.mult, ALU.mult,
                             reads=[("bank", bk), ("rr", q4), "gq8"], writes=[("q_tm", (t - 16) // 8)])
                self.cp("act", v_aug[:, t, 0:64], pp[:, kcol + 64:kcol + 128], reads=[("bank", bk)], writes=[("v_aug", t)])
            for b in range(4):
                for i in range(8):
                    self.tr(tpb[0:64, i * 128:(i + 1) * 128], k_tm[:, b * 8 + i, :], self.ident[:],
                            reads=[("k_tm2", b), "ident"], writes=[B2])
                self.cp("act", kTa[0:64, b * 1024:(b + 1) * 1024], tpb[0:64, :], reads=[B2], writes=[("kTa", b)])
                mk.op("dve", lambda e, b=b: e.tensor_reduce(out=kmT[:, b * 4:(b + 1) * 4],
                                                            in_=kTa[0:64, b * 1024:(b + 1) * 1024].rearrange("p (n l) -> p n l", l=256),
                                                            axis=AX.X, op=ALU.add),
                      reads=[("kTa", b)], writes=["kmT"])
            for b in range(2):
                for i in range(8):
                    self.tr(tpb[0:64, i * 128:(i + 1) * 128], q_tm[:, b * 8 + i, :], self.ident[:],
                            reads=[("q_tm", b), "ident"], writes=[B2])
                self.cp("act", qTa[0:64, b * 1024:(b + 1) * 1024], tpb[0:64, :], reads=[B2], writes=[("qTa", b)])
            self.ts("dve", kmTb[:], kmT[:], 1.0 / 256, None, ALU.mult, reads=["kmT"], writes=["kmTb"])
            for c in range(16):
                self.mm(ps[3][:, c * 16:(c + 1) * 16], qTa[0:64, c * 128:(c + 1) * 128], kmTb[:, :], True, True,
                        reads=[("qTa", c // 8), "kmTb"], writes=[("bank", 3)])
            self.tt("dve", gm[:], ps[3][:, 0:256].rearrange("p (c n) -> p c n", n=16),
                    ctxp[:].unsqueeze(1).to_broadcast([128, 16, 16]), ALU.add, reads=[("bank", 3), "ctxp"], writes=["gm"])
            self.memset("pool", nm_tm[:], 0.0, writes=["nm_tm"])
            for c in range(16):
                Bk = 8 + c // 2
                mk.op("dve", lambda e, c=c, Bk=Bk: e.max(out=m8[:], in_=gm[:, c, 0:Bk]), reads=["gm"], writes=["m8"])
                self.ts("dve", thr[:], m8[:, 2:3], -1e29, None, ALU.max, reads=["m8"], writes=["thr"])
                self.ts("dve", nm_tm[:, c, 0:Bk], gm[:, c, 0:Bk], thr[:, 0:1], -30000.0, ALU.is_lt, ALU.mult,
                        reads=["gm", "thr", "nm_tm"], writes=["nm_tm"])
            for b in range(2):
                for i in range(8):
                    self.tr(tpb[0:16, i * 128:(i + 1) * 128], nm_tm[:, b * 8 + i, :], self.ident[:],
                            reads=["nm_tm", "ident"], writes=[B2])
                self.cp("act", nmT[:, b * 1024:(b + 1) * 1024], tpb[0:16, :], reads=[B2], writes=["nmT"])
            self.dma(qTa[64:80, :], nmT[:], reads=["nmT"], writes=["qTa_nm"])
            tiles = [(qt, kt) for qt in range(4) for kt in range(20 + 4 * qt)]

            def issue_S(i):
                qt, kt = tiles[i]
                k0 = 128 * kt
                sbk = (4, 5, 6, 1)[i % 4]
                self.mm(ps[sbk][:, :], kTa[0:80, k0:k0 + 128], qTa[0:80, qt * 512:(qt + 1) * 512], True, True,
                        reads=[("kTa", kt // 8), "kTa_oh", ("qTa", qt // 2), "qTa_nm"], writes=[("bank", sbk)])

            def issue_rest(i):
                qt, kt = tiles[i]
                q0 = S_OWN + 512 * qt
                nkt = 20 + 4 * qt
                ob = 7 if qt % 2 == 0 else 3
                OT = ps[ob][0:65, :]
                Dd = q0 - 128 * kt
                sbk = (4, 5, 6, 1)[i % 4]
                s2 = i % 3
                self.act(expS[s2][:], ps[sbk][:, :], AF.Exp, reads=[("bank", sbk)], writes=[("expS", s2)])
                self.tt("dve", PTm[s2][:], expS[s2][:], Ms[:, Dd + 384:Dd + 384 + 512], ALU.mult,
                        reads=[("expS", s2), "Ms"], writes=[("PTm", s2)])
                self.mm(OT, v_aug[:, kt, :], PTm[s2][:, :], kt == 0, kt == nkt - 1,
                        reads=[("PTm", s2), ("v_aug", kt), "v_ones"], writes=[("bank", ob)])
                if kt == nkt - 1:
                    pending.append((i + 3, qt, ob, OT))

            def finalize(qt, ob, OT):
                self.recip(recr[64:65, :], OT[64:65, :], reads=[("bank", ob)], writes=["recr"])
                self.mm(ps[2][0:64, :], onesr[64:65, :], recr[64:65, :], True, True, reads=["onesr", "recr"], writes=[B2])
                self.cp("act", bc_sb[:], ps[2][0:64, :], reads=[B2], writes=["bc_sb"])
                self.tt("dve", self.attT[:, h, qt * 512:(qt + 1) * 512], OT[0:64, :], bc_sb[:], ALU.mult,
                        reads=[("bank", ob), "bc_sb"], writes=[("attT", h, qt)])

            pending = []
            issue_S(0)
            issue_S(1)
            issue_S(2)
            for i in range(len(tiles)):
                if i + 3 < len(tiles):
                    issue_S(i + 3)
                issue_rest(i)
                while pending and pending[0][0] <= i:
                    _, qt_, ob_, O_ = pending.pop(0)
                    finalize(qt_, ob_, O_)
                if i < 96:
                    if i % 6 == 0:
                        P_T()
                    elif i % 6 == 3:
                        P_C()
            while pending:
                _, qt_, ob_, O_ = pending.pop(0)
                finalize(qt_, ob_, O_)
        while pstep[0] < 131:
            P_T()
            P_C()

    def phaseB(self, sb):
        I, ps = self.I, self.ps
        mergedT = sb("mergedT", [128, 8, S_OWN], BF16)
        wout = sb("woutb", [128, 8, D], BF16)
        cst = [sb("cstB%d" % i, [128, 8, 128]) for i in range(4)]
        cbf = [[sb("cbfB%d_%d" % (j, i), [128, 8, 128], BF16) for i in range(4)] for j in range(2)]
        sa = [sb("saB%d" % i, [128, 512]) for i in range(2)]; sg = [sb("sgB%d" % i, [128, 512]) for i in range(2)]
        xo = [sb("xoB%d" % i, [128, D]) for i in range(2)]
        xm = [sb("xmB%d" % i, [128, D]) for i in range(2)]
        hn = sb("hnB", [128, D], BF16); junk = sb("junkB", [128, D], BF16)
        ss = sb("ssB", [128, 16]); rs = sb("rsB", [128, 16])
        w3 = I["w_in"].rearrange("(k p) c -> p k c", p=128)
        wr3 = I["ret_w"].rearrange("(k p) c -> p k c", p=128)
        wb3 = I["moba_w"].rearrange("(h d) c -> d h c", d=64)
        wo3 = I["w_out"].rearrange("(k p) c -> p k c", p=128)
        engs = ("dve", "act", "dve", "act")

        def load_w(fc):
            cols = slice(fc * 128, (fc + 1) * 128)
            j = fc % 2
            self.dma(cst[0][:], wr3[:, :, cols], writes=[("cst", 0)])
            self.dma(cst[1][0:64, :, :], wb3[:, :, cols], writes=[("cst", 1)])
            self.dma(cst[2][:], w3[:, :, OFFS["ga"] + fc * 128:OFFS["ga"] + (fc + 1) * 128], writes=[("cst", 2)])
            self.dma(cst[3][:], w3[:, :, OFFS["gb"] + fc * 128:OFFS["gb"] + (fc + 1) * 128], writes=[("cst", 3)])
            for i in range(4):
                p = 64 if i == 1 else 128
                self.cp(engs[i], cbf[j][i][0:p, :, :], cst[i][0:p, :, :], reads=[("cst", i)], writes=[("cbf", j, i)])

        load_w(0)
        it = 0
        for fc in range(8):
            j = fc % 2
            if fc + 1 < 8:
                load_w(fc + 1)
            for T in range(4):
                tc = slice(T * 512, (T + 1) * 512)
                b0 = 4 * (T % 2)
                q = it % 2
                it += 1
                for kc in range(8):
                    self.mm(ps[b0 + 2][:, :], cbf[j][2][:, kc, :], self.hTo[:, kc, tc], kc == 0, kc == 7,
                            reads=[("cbf", j, 2)] + [("hT", 16 + T * 4 + i) for i in range(4)], writes=[("bank", b0 + 2)])
                for kc in range(8):
                    self.mm(ps[b0 + 3][:, :], cbf[j][3][:, kc, :], self.hTo[:, kc, tc], kc == 0, kc == 7,
                            reads=[("cbf", j, 3)] + [("hT", 16 + T * 4 + i) for i in range(4)], writes=[("bank", b0 + 3)])
                for kc in range(8):
                    self.mm(ps[b0][:, :], cbf[j][0][:, kc, :], self.retT[:, kc, tc], kc == 0, kc == 7,
                            reads=[("cbf", j, 0)], writes=[("bank", b0)])
                for h in range(8):
                    self.mm(ps[b0 + 1][:, :], cbf[j][1][0:64, h, :], self.attT[0:64, h, tc], h == 0, h == 7,
                            reads=[("cbf", j, 1)], writes=[("bank", b0 + 1)])
                self.act(sa[q][:], ps[b0 + 2][:, :], AF.Sigmoid, reads=[("bank", b0 + 2)], writes=[("saB", q)])
                self.act(sg[q][:], ps[b0 + 3][:, :], AF.Sigmoid, reads=[("bank", b0 + 3)], writes=[("sgB", q)])
                self.tt("dve", sa[q][:], sa[q][:], ps[b0][:, :], ALU.mult, reads=[("saB", q), ("bank", b0)], writes=[("saB", q)])
                self.tt("dve", sg[q][:], sg[q][:], ps[b0 + 1][:, :], ALU.mult, reads=[("sgB", q), ("bank", b0 + 1)], writes=[("sgB", q)])
                self.tt("pool", mergedT[:, fc, tc], sa[q][:], sg[q][:], ALU.add, reads=[("saB", q), ("sgB", q)], writes=[("mT", T)])
        for j in range(8):
            i = j % 4
            self.dma(cst[i][:], wo3[:, :, j * 128:(j + 1) * 128], writes=[("cst", i)])
            self.cp(engs[i], wout[:, :, j * 128:(j + 1) * 128], cst[i][:], reads=[("cst", i)], writes=["wout"])
        tpb = [ps[4][:, :].bitcast(BF16), ps[5][:, :].bitcast(BF16)]

        def b2_front(t):
            s = t % 2
            self.dma(xo[s][:], I["xw"][S_OWN + t * 128:S_OWN + (t + 1) * 128, :], writes=[("xo", s)])
            for hf in range(2):
                bk = 2 * s + hf
                for kc in range(8):
                    self.mm(ps[bk][:, :], mergedT[:, kc, t * 128:(t + 1) * 128], wout[:, kc, hf * 512:(hf + 1) * 512], kc == 0, kc == 7,
                            reads=[("mT", t // 4), "wout"], writes=[("bank", bk)])
                self.tt("dve", xm[s][:, hf * 512:(hf + 1) * 512], ps[bk][:, :], xo[s][:, hf * 512:(hf + 1) * 512], ALU.add,
                        reads=[("bank", bk), ("xo", s)], writes=[("xm", s)])
            self.dma(self.xm_scr[t * 128:(t + 1) * 128, :], xm[s][:], reads=[("xm", s)], writes=[("xm_scr", t)], key=("xm_scr", s))
            self.act(junk[:], xm[s][:], AF.Square, reads=[("xm", s)], writes=["junkB", ("ssB", t)], accum_out=ss[:, t:t + 1])
            self.act(rs[:, t:t + 1], ss[:, t:t + 1], AF.Sqrt, reads=[("ssB", t)], writes=[("rsB", t)], scale=1.0 / D, bias=EPS)

        def b2_mid(t):
            s = t % 2
            self.recip(rs[:, t:t + 1], rs[:, t:t + 1], reads=[("rsB", t)], writes=[("rsB", t)])
            self.ts("dve", hn[:], xm[s][:], rs[:, t:t + 1], None, ALU.mult, reads=[("xm", s), ("rsB", t)], writes=["hnB"])
            tp = tpb[s].rearrange("p (k c) -> p k c", k=8)
            for kc in range(8):
                self.tr(tp[:, kc, :], hn[:, kc * 128:(kc + 1) * 128], self.ident[:], reads=["hnB", "ident"], writes=[("bank", 4 + s)])

        def b2_back(t):
            s = t % 2
            tp = tpb[s].rearrange("p (k c) -> p k c", k=8)
            self.tt("dve", self.hTo[:, :, t * 128:(t + 1) * 128], tp, self.g2col[:].unsqueeze(2).to_broadcast([128, 8, 128]), ALU.mult,
                    reads=[("bank", 4 + s), "g2col"], writes=[("hT", 16 + t)])

        b2_front(0)
        for t in range(16):
            if t + 1 < 16:
                b2_front(t + 1)
            b2_mid(t)
            if t >= 1:
                b2_back(t - 1)
        b2_back(15)


    def phaseC(self, sb):
        I, ps, mk = self.I, self.ps, self.mk
        NG, GT = 8, 256
        qT_scr = self.dscr("qT_scr", [128, 16, S_OWN])
        h2T = self.hTo
        skT = sb("skT", [128, 16, 128]); identf = sb("identfC", [128, 128])
        self.dma(identf[:], I["identf"][:, :], writes=["identf"])
        with contextlib.ExitStack() as es:
            skn = es.enter_context(self.nc.sbuf_tensor("skn", [128, 16, 128], F32))
            self.dma(skn[:], I["sub_keys"].rearrange("g n d -> n g d"), writes=["skn"])
            for hc in range(16):
                bk = hc % 2
                self.tr(ps[bk][:, 0:128], skn[:, hc, :], identf[:], reads=["skn", "identf"], writes=[("bank", bk)])
                self.cp("act" if hc % 2 else "dve", skT[:, hc, :], ps[bk][:, 0:128], reads=[("bank", bk)], writes=["skT"])
            wq = es.enter_context(self.nc.sbuf_tensor("wqC", [128, 8, 2048], BF16))
            wst2 = [es.enter_context(self.nc.sbuf_tensor("wstC%d" % i, [128, 8, 512], F32)) for i in range(2)]
            qst = [es.enter_context(self.nc.sbuf_tensor("qstC%d" % i, [128, 512], F32)) for i in range(2)]
            wq3 = I["peer_wq"].rearrange("(k p) c -> p k c", p=128)
            for j in range(4):
                wst = wst2[j % 2]
                self.dma(wst[:], wq3[:, :, j * 512:(j + 1) * 512], writes=[("wstC", j % 2)])
                self.cp("dve", wq[:, 0:4, j * 512:(j + 1) * 512], wst[:, 0:4, :], reads=[("wstC", j % 2)], writes=[("wqC", j, 0)])
                self.cp("act", wq[:, 4:8, j * 512:(j + 1) * 512], wst[:, 4:8, :], reads=[("wstC", j % 2)], writes=[("wqC", j, 1)])
            cnt = 0
            for hc in range(16):
                for T in range(4):
                    bk = 2 + cnt % 2
                    s = cnt % 2
                    cnt += 1
                    for kc in range(8):
                        self.mm(ps[bk][:, :], wq[:, kc, hc * 128:(hc + 1) * 128], h2T[:, kc, T * 512:(T + 1) * 512], kc == 0, kc == 7,
                                reads=[("wqC", hc // 4, 0), ("wqC", hc // 4, 1)], writes=[("bank", bk)])
                    self.cp("act" if s else "dve", qst[s][:], ps[bk][:, :], reads=[("bank", bk)], writes=[("qst", s)])
                    self.dma(qT_scr[:, hc, T * 512:(T + 1) * 512], qst[s][:], reads=[("qst", s)], writes=["qT_scr"], key=("qT_scr", s), nowaw=True)
        self.mk.barrier()
        EE = sb("EE", [128, 2, 16, 128])
        UTc = [sb("UTc%d" % i, [128, 8, 8, 128], BF16) for i in range(2)]
        Vc = [sb("Vc%d" % i, [128, 8, D], BF16) for i in range(2)]
        gact = [sb("gact%d" % i, [128, 8, GT], BF16) for i in range(2)]
        Ew = [sb("Ew%d" % i, [128, 8, 128]) for i in range(5)]
        Wh = [sb("Wh%d" % i, [128, 8, 8, 128], BF16) for i in range(2)]
        WaT = [sb("WaT%d" % i, [128, 8, 128], BF16) for i in range(2)]
        Dg = sb("Dg", [128, 2, 8, 128], BF16)
        qTg = sb("qTg", [128, 16, 128])
        negm = sb("negm", [128, 2, 16]); ev = sb("ev", [128, 2, 16, 16]); tmpE = sb("tmpE", [128, 128])
        tmpC = sb("tmpC", [128, 256]); t16all = sb("t16all", [128, 2, 8, 16])
        ec = sb("ec", [128, 2, 8]); Zs = sb("Zs", [128, 2, 8]); rz = sb("rz", [128, 2, 8])
        xmt = [sb("xmtC0", [128, D])] * 2
        ld = [0]

        def load_chunk(c):
            s = ld[0] % 2
            ld[0] += 1
            self.dma(UTc[s][:], self.UT_scr[c * 8:(c + 1) * 8, :, :, :].rearrange("a d k b -> d a k b"), writes=[("UTc", s)])
            self.dma(Vc[s][:], self.V_scr[c * 1024:(c + 1) * 1024, :].rearrange("(a b) d -> b a d", b=128), writes=[("Vc", s)])
            return s

        for g in range(NG):
            g0 = g * GT
            def st_scores(tl):
                if not (tl == 0 and g > 0):
                    self.dma(qTg[:], qT_scr[:, :, g0 + tl * 128:g0 + (tl + 1) * 128], writes=["qTg"])
                for hc in range(16):
                    bk = 4 * (1 - tl) + hc // 4
                    self.mm(ps[bk][:, (hc % 4) * 128:(hc % 4 + 1) * 128], qTg[:, hc, :], skT[:, hc, :], True, True,
                            reads=["qTg", "skT"], writes=[("bank", bk)])

            def st_max(tl):
                for q4 in range(4):
                    bk = 4 * (1 - tl) + q4
                    mk.op("dve", lambda e, bk=bk, q4=q4, tl=tl: e.tensor_reduce(out=negm[:, tl, q4 * 4:(q4 + 1) * 4],
                                                                            in_=ps[bk][:, :].rearrange("p (g n) -> p g n", n=128),
                                                                            axis=AX.X, op=ALU.max),
                          reads=[("bank", bk)], writes=[("negm", tl)])
                self.ts("dve", negm[:, tl, :], negm[:, tl, :], -1.0, None, ALU.mult, reads=[("negm", tl)], writes=[("negm", tl)])

            def st_exp(tl):
                for hc in range(16):
                    bk = 4 * (1 - tl) + hc // 4
                    self.act(EE[:, tl, hc, :], ps[bk][:, (hc % 4) * 128:(hc % 4 + 1) * 128], AF.Exp, reads=[("bank", bk), ("negm", tl)],
                             writes=[("EE", tl), ("EEh", tl, hc)], bias=negm[:, tl, hc:hc + 1], scale=1.0)

            def st_topk(tl):
                for hc in range(16):
                    mk.op("dve", lambda e, hc=hc, tl=tl: e.max(out=ev[:, tl, hc, 0:8], in_=EE[:, tl, hc, :]), reads=[("EEh", tl, hc)], writes=[("ev", tl)])
                    mk.op("dve", lambda e, hc=hc, tl=tl: e.match_replace(out=tmpE[:], in_to_replace=ev[:, tl, hc, 0:8], in_values=EE[:, tl, hc, :], imm_value=-1.0),
                          reads=[("EEh", tl, hc), ("ev", tl)], writes=["tmpE"])
                    mk.op("dve", lambda e, hc=hc, tl=tl: e.max(out=ev[:, tl, hc, 8:16], in_=tmpE[:]), reads=["tmpE"], writes=[("ev", tl)])

            def st_heads(tl):
                ev4 = ev[:, tl, :, :].rearrange("p (h c) k -> p h c k", c=2)
                for half in range(2):
                    es_ = 2 * tl + half
                    cnd = Ew[es_][:].rearrange("p a b -> p (a b)").rearrange("p (h i j) -> p h i j", h=4, i=16)
                    self.tt("dve", cnd, ev4[:, half * 4:(half + 1) * 4, 0, :].unsqueeze(3).to_broadcast([128, 4, 16, 16]),
                            ev4[:, half * 4:(half + 1) * 4, 1, :].unsqueeze(2).to_broadcast([128, 4, 16, 16]), ALU.mult,
                            reads=[("ev", tl)], writes=[("Ew", es_)] + [("Ew", es_, a) for a in range(8)])
                    for hh in range(4):
                        h = half * 4 + hh
                        cf = Ew[es_][:].rearrange("p a b -> p (a b)")[:, hh * 256:(hh + 1) * 256]
                        mk.op("dve", lambda e, cf=cf, h=h, tl=tl: e.max(out=t16all[:, tl, h, 0:8], in_=cf), reads=[("Ew", es_)] + [("Ew", es_, a) for a in range(8)], writes=[("t16a", tl)])
                        mk.op("dve", lambda e, cf=cf, h=h, tl=tl: e.match_replace(out=tmpC[:], in_to_replace=t16all[:, tl, h, 0:8], in_values=cf, imm_value=-1.0),
                              reads=[("Ew", es_), ("t16a", tl)] + [("Ew", es_, a) for a in range(8)], writes=["tmpC"])
                        mk.op("dve", lambda e, h=h, tl=tl: e.max(out=t16all[:, tl, h, 8:16], in_=tmpC[:]), reads=["tmpC"], writes=[("t16a", tl)])
                self.cp("dve", ec[:, tl, :], t16all[:, tl, :, 15], reads=[("t16a", tl)], writes=["ec"])
                mk.op("dve", lambda e, tl=tl: e.tensor_reduce(out=Zs[:, tl, :], in_=t16all[:, tl, :, :], axis=AX.X, op=ALU.add),
                      reads=[("t16a", tl)], writes=[("Zs", tl)])
                self.recip(rz[:, tl, :], Zs[:, tl, :], reads=[("Zs", tl)], writes=[("rz", tl)])
                for h in range(8):
                    self.ts("dve", Dg[:, tl, h, :], self.ident[:], rz[:, tl, h:h + 1], None, ALU.mult, reads=["ident", ("rz", tl)], writes=[("Dg", tl)])

            for stage in (st_scores, st_max, st_exp, st_topk, st_heads):
                stage(0)
                stage(1)
                if stage is st_scores and g + 1 < NG:
                    self.dma(qTg[:], qT_scr[:, :, g0 + GT:g0 + GT + 128], writes=["qTg"])
            nxt = load_chunk(0)
            units = [(c, tl) for c in range(16) for tl in range(2)]
            cslot = {}

            def emit_tail(u):
                c, tl = units[u]
                w = u % 2
                s_ = cslot[c]
                gs = c % 2
                for hf in range(2):
                    self.tt("dve", WaT[w][:, hf * 4:(hf + 1) * 4, :], ps[4 + hf][:, :].rearrange("p (a t) -> p a t", a=4),
                            gact[gs][:, hf * 4:(hf + 1) * 4, tl * 128:(tl + 1) * 128], ALU.mult,
                            reads=[("bank", 4 + hf), ("gact", gs)], writes=[("WaT", w)])
                for hf in range(2):
                    ob = tl * 2 + hf
                    for a in range(8):
                        self.mm(ps[ob][:, :], WaT[w][:, a, :], Vc[s_][:, a, hf * 512:(hf + 1) * 512], c == 0 and a == 0, c == 15 and a == 7,
                                reads=[("WaT", w), ("Vc", s_)], writes=[("bank", ob)])

            for u, (c, tl) in enumerate(units):
                for h in (4, 5, 6, 7):
                    e = Ew[h - 3]
                    for a in range(8):
                        self.act(e[:, a, :], EE[:, tl, 2 * h + 1, :], AF.Copy, reads=[("EE", tl)], writes=[("Ew", h - 3, a)],
                                 scale=EE[:, tl, 2 * h, c * 8 + a:c * 8 + a + 1])
                if tl == 0:
                    s = nxt
                    cslot[c] = s
                    gs = c % 2
                    for a in range(8):
                        bk = 6 + (a // 2) % 2
                        half = (a % 2) * GT
                        for kc in range(8):
                            self.mm(ps[bk][:, half:half + GT], UTc[s][:, a, kc, :], h2T[:, kc, g0:g0 + GT], kc == 0, kc == 7,
                                    reads=[("UTc", s)], writes=[("bank", bk)])
                        if a % 2 == 1:
                            self.act(gact[gs][:, a - 1:a + 1, :], ps[bk][:, :].rearrange("p (a t) -> p a t", a=2), AF.Gelu,
                                     reads=[("bank", bk)], writes=[("gact", gs)])
                wb = u % 2
                for h in range(8):
                    if h < 4:
                        si = 0
                        e = Ew[0]
                        self.tt("dve", e[:], EE[:, tl, 2 * h, c * 8:(c + 1) * 8].unsqueeze(2).to_broadcast([128, 8, 128]),
                                EE[:, tl, 2 * h + 1, :].unsqueeze(1).to_broadcast([128, 8, 128]), ALU.mult,
                                reads=[("EE", tl)], writes=[("Ew", 0)])
                        rd = [("Ew", 0)]
                    else:
                        si = h - 3
                        e = Ew[si]
                        rd = [("Ew", si, a) for a in range(8)]
                    self.stt("dve", Wh[wb][:, h, :, :], e[:], ec[:, tl, h:h + 1], e[:], ALU.is_ge, ALU.mult,
                             reads=rd + ["ec"], writes=[("Wh", wb, h)])
                if u > 0:
                    emit_tail(u - 1)
                if tl == 0 and c + 1 < 16:
                    nxt = load_chunk(c + 1)
                for a in range(8):
                    bk = 4 + a // 4
                    for h in range(8):
                        self.mm(ps[bk][:, (a % 4) * 128:(a % 4 + 1) * 128], Wh[wb][:, h, a, :], Dg[:, tl, h, :], h == 0, h == 7,
                                reads=[("Wh", wb, h), ("Dg", tl)], writes=[("bank", bk)])
            emit_tail(len(units) - 1)
            for tl in range(2):
                t = g * 2 + tl
                sx = 0
                self.dma(xmt[sx][:], self.xm_scr[t * 128:(t + 1) * 128, :], writes=[("xmt", sx)])
                for hf in range(2):
                    self.tt("dve", xmt[sx][:, hf * 512:(hf + 1) * 512], xmt[sx][:, hf * 512:(hf + 1) * 512], ps[tl * 2 + hf][:, :], ALU.add,
                            reads=[("xmt", sx), ("bank", tl * 2 + hf)], writes=[("xmt", sx)])
                self.dma(self.y[t * 128:(t + 1) * 128, :], xmt[sx][:], reads=[("xmt", sx)], writes=[("y", sx)], key=("y", sx))


def _prep_inputs(inputs):
    x = np.asarray(inputs["x"], np.float32)
    shared = {
        "w_in": np.ascontiguousarray(inputs["w_in"][0], np.float32),
        "ret_w": np.ascontiguousarray(inputs["ret_w_branch"][0], np.float32),
        "moba_w": np.ascontiguousarray(inputs["moba_w_branch"][0], np.float32),
        "w_out": np.ascontiguousarray(inputs["w_out"][0], np.float32),
        "peer_wq": np.ascontiguousarray(inputs["peer_w_q"][0], np.float32),
        "sub_keys": np.ascontiguousarray(np.asarray(inputs["peer_sub_keys"][0], np.float32).reshape(16, 128, 128)),
        "peer_u": np.ascontiguousarray(inputs["peer_u"][0], np.float32),
        "peer_v": np.ascontiguousarray(inputs["peer_v"][0], np.float32),
        "g1": np.ascontiguousarray(inputs["mix_norm_g"][0], np.float32),
        "g2": np.ascontiguousarray(inputs["ffn_norm_g"][0], np.float32),
        "mqg": np.ascontiguousarray(inputs["moba_q_gain"], np.float32).reshape(1, 64),
        "mkg": np.ascontiguousarray(inputs["moba_k_gain"], np.float32).reshape(1, 64),
        "rel_bias": np.ascontiguousarray(inputs["rel_bias"], np.float32),
    }
    consts = [_consts(0), _consts(1)]
    in_maps = []
    for c in range(NCORES):
        b, half = c // 2, c % 2
        if half == 1:
            xw = x[b]
        else:
            xw = np.concatenate([np.zeros((S_OWN, D), np.float32), x[b, :S_OWN]], 0)
        m = dict(shared)
        m["xw"] = np.ascontiguousarray(xw)
        for k, v in consts[half].items():
            if k != "gC":
                m[k] = v
        in_maps.append(m)
    return in_maps


_CACHE = {}


def kernel(**inputs):
    in_maps = _prep_inputs(inputs)
    if "nc" not in _CACHE:
        pb = PB()
        _CACHE["nc"] = pb.build()
        _CACHE["used"] = set(pb.I.keys())
    used = _CACHE["used"]
    in_maps = [{k: v for k, v in m.items() if k in used} for m in in_maps]
    res = run_bass_kernel_spmd(_CACHE["nc"], in_maps, core_ids=list(range(NCORES)))
    out = np.zeros((4, 4096, D), np.float32)
    for c in range(NCORES):
        b, half = c // 2, c % 2
        out[b, half * S_OWN:(half + 1) * S_OWN] = res.results[c]["y"]
    return out
```

```python
import contextlib
import math
import numpy as np
import ml_dtypes
import concourse.bass as bass
import concourse.mybir as mybir
from concourse.bass_utils import run_bass_kernel_spmd

F32 = mybir.dt.float32
BF16 = mybir.dt.bfloat16
ALU = mybir.AluOpType
AF = mybir.ActivationFunctionType
AX = mybir.AxisListType

NCORES = 8
S_OWN = 2048
S_WIN = 4096
D = 1024
EPS = 1e-6
STRIP_W = 4608
EV_LEN = 4736
FUSE_WAIT = True
SAME_ENGINE_RAW_ONLY = True
OFFS = {"rq": 0, "rk": 512, "rv": 1024, "rg": 2048, "mq": 3072, "mk": 3584, "mv": 4096, "ga": 4608, "gb": 5632}


class _Op:
    __slots__ = ("eng", "fn", "dma", "deps", "seq", "need_inc", "inc_val", "dsem", "dval", "waits")


class MK:
    ENGS = ("pe", "dve", "act", "pool", "sp")

    def __init__(self, nc):
        self.nc = nc
        self.ops = []
        self.last_w = {}
        self.readers = {}
        self.dma_tot = {}
        self.eng_n = {e: 0 for e in self.ENGS}
        self.last_comp = {}
        self.last_dma = {}
        self.wdeps = {}

    def _new(self, eng, fn, dma):
        o = _Op()
        o.eng = eng; o.fn = fn; o.dma = dma; o.need_inc = False; o.inc_val = 0
        o.dsem = None; o.dval = 0; o.waits = None; o.deps = {}
        o.seq = self.eng_n[eng]
        self.eng_n[eng] += 1
        return o

    def op(self, eng, fn, reads=(), writes=(), dma=False, sem_key=None, nowaw=False):
        o = self._new(eng, fn, dma)
        idx = len(self.ops)
        deps = o.deps
        for t in reads:
            j = self.last_w.get(t)
            if j is not None:
                deps[j] = "raw"
        for t in writes:
            j = self.last_w.get(t)
            if nowaw and j is not None and self.ops[j].dma:
                new = self.wdeps.get(t, {})
            else:
                new = {}
                if j is not None:
                    new[j] = "waw"
                for r in self.readers.get(t, ()):
                    new[r] = "war"
                self.wdeps[t] = new
            for r, kind in new.items():
                if r not in deps:
                    deps[r] = kind
        if dma:
            k = sem_key if sem_key is not None else (writes[0] if writes else ("dma", idx))
            self.dma_tot[k] = self.dma_tot.get(k, 0) + 16
            o.dsem = k; o.dval = self.dma_tot[k]
            self.last_dma[k] = idx
        else:
            self.last_comp[eng] = idx
        self.ops.append(o)
        for t in reads:
            self.readers.setdefault(t, []).append(idx)
        for t in writes:
            self.last_w[t] = idx
            self.readers[t] = []
        return idx

    def barrier(self):
        deps = {j: "raw" for j in self.last_comp.values()}
        deps.update({j: "raw" for j in self.last_dma.values()})
        for e in self.ENGS:
            o = self._new(e, None, False)
            o.deps = dict(deps)
            self.ops.append(o)
        self.last_w = {}
        self.readers = {}

    def emit(self, final_wait_tokens=()):
        nc = self.nc
        fo = self._new("sp", None, False)
        for t in final_wait_tokens:
            j = self.last_w.get(t)
            if j is not None:
                fo.deps[j] = "raw"
        ops = self.ops + [fo]
        seen_e = {e: {x: -1 for x in self.ENGS} for e in self.ENGS}
        seen_d = {e: {} for e in self.ENGS}
        snap = {}
        for o in ops:
            w = []
            E = o.eng
            best_e = {}
            best_d = {}
            for j, kind in o.deps.items():
                p = ops[j]
                if p.dma:
                    if p.dval > best_d.get(p.dsem, 0):
                        best_d[p.dsem] = p.dval
                else:
                    if p.fn is None:
                        continue
                    if p.eng == E and not o.dma and o.fn is not None:
                        if E == "pe" or E == "sp":
                            continue
                        if SAME_ENGINE_RAW_ONLY and kind != "raw":
                            continue
                    if p.seq > best_e.get(p.eng, -1):
                        best_e[p.eng] = p.seq
            for X, s in sorted(best_e.items(), key=lambda kv: -kv[1]):
                if s > seen_e[E][X]:
                    seen_e[E][X] = s
                    w.append(("e", X, s))
                    for Y, v in snap[(X, s)].items():
                        if v > seen_e[E][Y]:
                            seen_e[E][Y] = v
            w = [x for x in w if not any(y is not x and snap[(y[1], y[2])].get(x[1], -1) >= x[2] for y in w)]
            for k, v in best_d.items():
                if v > seen_d[E].get(k, 0):
                    seen_d[E][k] = v
                    w.append(("d", k, v))
            o.waits = w
            if not o.dma:
                sn = dict(seen_e[E])
                sn[E] = max(sn[E], o.seq)
                snap[(E, o.seq)] = sn
        by_eng_seq = {e: {} for e in self.ENGS}
        for o in ops:
            if not o.dma:
                by_eng_seq[o.eng][o.seq] = o
        for o in ops:
            for w in o.waits:
                if w[0] == "e":
                    by_eng_seq[w[1]][w[2]].need_inc = True
        cnt = {e: 0 for e in self.ENGS}
        for o in ops:
            if not o.dma and o.need_inc:
                cnt[o.eng] += 1
                o.inc_val = cnt[o.eng]
        self.inc_counts = dict(cnt)
        es = contextlib.ExitStack()
        esem = {e: es.enter_context(nc.semaphore("es_" + e)) for e in self.ENGS}
        dsem = {}
        for k in self.dma_tot:
            dsem[k] = es.enter_context(nc.semaphore("ds_%d" % len(dsem)))
        self.n_dsem = len(dsem)
        prog = {e: [] for e in self.ENGS}
        for o in ops:
            prog[o.eng].append(o)

        def run(engname, eng):
            for o in prog[engname]:
                ws = list(o.waits)
                fused = None
                if o.fn is not None and ws and FUSE_WAIT:
                    fused = ws.pop()
                for w in ws:
                    if w[0] == "e":
                        eng.wait_ge(esem[w[1]], by_eng_seq[w[1]][w[2]].inc_val)
                    else:
                        eng.wait_ge(dsem[w[1]], w[2])
                if o.fn is None:
                    continue
                ins = o.fn(eng)
                if fused is not None:
                    if fused[0] == "e":
                        ins._wait_ge(esem[fused[1]], by_eng_seq[fused[1]][fused[2]].inc_val)
                    else:
                        ins._wait_ge(dsem[fused[1]], fused[2])
                if o.dma:
                    ins.then_inc(dsem[o.dsem], 16)
                elif o.need_inc:
                    ins.then_inc(esem[o.eng], 1)

        with es:
            with nc.Block() as block:
                @block.tensor
                def _(e):
                    run("pe", e)

                @block.vector
                def _(e):
                    run("dve", e)

                @block.scalar
                def _(e):
                    run("act", e)

                @block.gpsimd
                def _(e):
                    run("pool", e)

                @block.sync
                def _(e):
                    run("sp", e)


def _rel_bucket_np(dist):
    n = np.maximum(dist, 0)
    nf = np.maximum(n, 1).astype(np.float32)
    large = 16 + (np.log(nf / np.float32(16)) / np.float32(math.log(2048 / 16)) * np.float32(16)).astype(np.int32)
    large = np.minimum(large, 31)
    return np.where(n < 16, n, large)


def _consts(half):
    c = {}
    bf = ml_dtypes.bfloat16
    c["ident_bf"] = np.eye(128, dtype=np.float32).astype(bf)
    c["J_bf"] = np.eye(128, dtype=np.float32)[::-1].copy().astype(bf)
    c["identf"] = np.eye(128, dtype=np.float32)
    w = np.arange(S_WIN)
    pos = (w - S_OWN + S_OWN * half).astype(np.float32)
    inv = (np.float32(10000.0) ** (-np.arange(32, dtype=np.float32) / np.float32(32))).astype(np.float32)
    ang = pos[None, :] * inv[:, None]
    cs, sn = np.cos(ang).astype(np.float32), np.sin(ang).astype(np.float32)
    c["cosT"] = np.concatenate([cs, cs], 0)
    c["sinT"] = np.concatenate([-sn, sn], 0)
    c["csT"] = np.concatenate([c["cosT"], c["sinT"]], 0)
    fold = np.zeros((128, 128), np.float32)
    for p in range(128):
        fold[p, p % 64] = 1.0
        fold[p, p % 64 + 64] = 1.0
    c["fold"] = fold.astype(bf)
    hh = np.arange(8, dtype=np.float32)
    log_g = np.log1p(-np.exp2(-5.0 - hh)).astype(np.float32)
    i = np.arange(128, dtype=np.float32)
    rel = i[None, :] - i[:, None]
    DT = np.where(rel[None] >= 0, np.exp(np.maximum(rel, 0)[None] * log_g[:, None, None]), 0.0) * 0.125
    c["DT"] = np.ascontiguousarray(DT.transpose(1, 0, 2)).astype(np.float32)
    QD = np.exp((i + 1.0)[None, :] * log_g[:, None])
    c["QD"] = np.ascontiguousarray(np.broadcast_to(QD[None], (128, 8, 128))).astype(np.float32)
    kd = np.exp((127.0 - i)[:, None] * log_g[None, :]) * 0.125
    c["kdec"] = kd.astype(np.float32)
    c["gC"] = [float(np.exp(128.0 * lg)) for lg in log_g]
    idx = np.arange(EV_LEN)
    dist = idx - 511
    bk = _rel_bucket_np(dist)
    OH = np.zeros((33, EV_LEN), np.float32)
    valid = dist >= 0
    OH[bk[valid], idx[valid]] = 1.0
    OH[32, ~valid] = 1.0
    c["OH33"] = OH
    c["negrow"] = np.full((1, 8), -30000.0, np.float32)
    cp = np.zeros((128, 16), np.float32)
    if half == 0:
        cp[:, :8] = -1e30
    c["ctxpen"] = cp
    bo = np.zeros((16, S_WIN), np.float32)
    for n in range(16):
        bo[n, n * 256:(n + 1) * 256] = 1.0
    c["blkoh"] = bo.astype(bf)
    return c


class PB:
    def __init__(self, dbg=None):
        self.dbg = dbg or ()
        self.nc = bass.Bass("TRN2", target_bir_lowering=False)
        self.mk = MK(self.nc)
        self.dq = 0

    def din(self, name, shape, dt=F32):
        return self.nc.dram_tensor(name, list(shape), dt, kind="ExternalInput").ap()

    def dout(self, name, shape, dt=F32):
        return self.nc.dram_tensor(name, list(shape), dt, kind="ExternalOutput").ap()

    def dscr(self, name, shape, dt=F32):
        return self.nc.dram_tensor(name, list(shape), dt, kind="Internal").ap()

    def dma(self, out, in_, reads=(), writes=(), eng=None, key=None, nowaw=False, slow=False):
        if eng is None:
            eng = "sp"
        if slow:
            return self.mk.op(eng, lambda e: e.dma_start(out=out, in_=in_, allow_slow_non_contiguous=True), reads=reads,
                              writes=writes, dma=True, sem_key=key, nowaw=nowaw)
        return self.mk.op(eng, lambda e: e.dma_start(out=out, in_=in_), reads=reads, writes=writes,
                          dma=True, sem_key=key, nowaw=nowaw)

    def mm(self, out, lhsT, rhs, start, stop, reads=(), writes=()):
        return self.mk.op("pe", lambda e: e.matmul(out, lhsT=lhsT, rhs=rhs, start=start, stop=stop),
                          reads=reads, writes=writes)

    def tr(self, out, in_, ident, reads=(), writes=()):
        return self.mk.op("pe", lambda e: e.transpose(out=out, in_=in_, identity=ident), reads=reads, writes=writes)

    def act(self, out, in_, func, reads=(), writes=(), **kw):
        return self.mk.op("act", lambda e: e.activation(out=out, in_=in_, func=func, **kw), reads=reads, writes=writes)

    def tt(self, eng, out, in0, in1, op, reads=(), writes=()):
        return self.mk.op(eng, lambda e: e.tensor_tensor(out=out, in0=in0, in1=in1, op=op), reads=reads, writes=writes)

    def ts(self, eng, out, in0, s1, s2, op0, op1=None, reads=(), writes=()):
        if op1 is None:
            return self.mk.op(eng, lambda e: e.tensor_scalar(out=out, in0=in0, scalar1=s1, scalar2=s2, op0=op0),
                              reads=reads, writes=writes)
        return self.mk.op(eng, lambda e: e.tensor_scalar(out=out, in0=in0, scalar1=s1, scalar2=s2, op0=op0, op1=op1),
                          reads=reads, writes=writes)

    def stt(self, eng, out, in0, scalar, in1, op0, op1, reads=(), writes=()):
        return self.mk.op(eng, lambda e: e.scalar_tensor_tensor(out=out, in0=in0, scalar=scalar, in1=in1, op0=op0, op1=op1),
                          reads=reads, writes=writes)

    def cp(self, eng, out, in_, reads=(), writes=()):
        if eng == "act":
            return self.mk.op("act", lambda e: e.copy(out=out, in_=in_), reads=reads, writes=writes)
        return self.mk.op(eng, lambda e: e.tensor_copy(out=out, in_=in_), reads=reads, writes=writes)

    def memset(self, eng, ap, val, writes=()):
        return self.mk.op(eng, lambda e: e.memset(ap, val), writes=writes)

    def recip(self, out, in_, reads=(), writes=()):
        return self.mk.op("dve", lambda e: e.reciprocal(out=out, in_=in_), reads=reads, writes=writes)

    def build(self):
        nc, mk = self.nc, self.mk
        shapes = {"xw": ([S_WIN, D], F32), "w_in": ([D, 6656], F32), "ret_w": ([1024, D], F32), "moba_w": ([512, D], F32),
                  "w_out": ([D, D], F32), "peer_wq": ([D, 2048], F32), "sub_keys": ([16, 128, 128], F32),
                  "peer_u": ([16384, D], F32), "peer_v": ([16384, D], F32), "g1": ([D], F32), "g2": ([D], F32),
                  "mqg": ([1, 64], F32), "mkg": ([1, 64], F32), "rel_bias": ([32, 8], F32),
                  "ident_bf": ([128, 128], BF16), "J_bf": ([128, 128], BF16), "identf": ([128, 128], F32),
                  "cosT": ([64, S_WIN], F32), "sinT": ([64, S_WIN], F32), "DT": ([128, 8, 128], F32),
                  "QD": ([128, 8, 128], F32), "csT": ([128, S_WIN], F32), "fold": ([128, 128], BF16), "kdec": ([128, 8], F32), "OH33": ([33, EV_LEN], F32),
                  "negrow": ([1, 8], F32), "ctxpen": ([128, 16], F32), "blkoh": ([16, S_WIN], BF16)}
        pb = self

        class Lazy(dict):
            def __missing__(self, k):
                sh, dt = shapes[k]
                v = pb.din(k, sh, dt)
                self[k] = v
                return v
        I = Lazy()
        self.I = I
        self.y = self.dout("y", [S_OWN, D])
        self.dbg_out = {}
        self.gC = _consts(0)["gC"]

        with contextlib.ExitStack() as es0:
            self.es0 = es0
            sb0 = lambda n, s, d=F32: es0.enter_context(nc.sbuf_tensor(n, list(s), d))
            self.ps = [es0.enter_context(nc.psum_tensor("ps%d" % i, [128, 512], F32)) for i in range(8)]
            self.ident = sb0("ident", [128, 128], BF16)
            self.Jm = sb0("Jm", [128, 128], BF16)
            self.g1col = sb0("g1col", [128, 8])
            self.g2col = sb0("g2col", [128, 8])
            self.hTo = sb0("hTo", [128, 8, S_OWN], BF16)
            self.dma(self.ident[:], I["ident_bf"][:, :], writes=["ident"])
            self.dma(self.Jm[:], I["J_bf"][:, :], writes=["Jm"])
            self.dma(self.g1col[:], I["g1"].rearrange("(k p) -> p k", p=128), writes=["g1col"], slow=True)
            self.dma(self.g2col[:], I["g2"].rearrange("(k p) -> p k", p=128), writes=["g2col"], slow=True)
            self.do_peer = not any(k in self.dbg for k in ("stop0", "stopA1", "stopA2", "stopB"))
            with contextlib.ExitStack() as esAB:
                sbAB = lambda n, s, d=F32: esAB.enter_context(nc.sbuf_tensor(n, list(s), d))
                self.retT = sbAB("retT", [128, 8, S_OWN], BF16)
                self.attT = sbAB("attT", [64, 8, S_OWN], BF16)
                with contextlib.ExitStack() as esA:
                    sbA = lambda n, s, d=F32: esA.enter_context(nc.sbuf_tensor(n, list(s), d))
                    self.hTc = sbA("hTc", [128, 8, S_OWN], BF16)
                    with contextlib.ExitStack() as es:
                        self.phase0(lambda n, s, d=F32: es.enter_context(nc.sbuf_tensor(n, list(s), d)))
                    mk.barrier()
                    if "hT" in self.dbg:
                        d = self.dout("d_hTo", [128, 8, S_OWN], BF16)
                        self.dma(d[:, :, :], self.hTo[:], writes=["d_hTo"])
                        self.dbg_out["d_hTo"] = "d_hTo"
                        d = self.dout("d_hTc", [128, 8, S_OWN], BF16)
                        self.dma(d[:, :, :], self.hTc[:], writes=["d_hTc"])
                        self.dbg_out["d_hTc"] = "d_hTc"
                    if "stop0" not in self.dbg and "skipA1" not in self.dbg:
                        with contextlib.ExitStack() as es:
                            self.phaseA1(lambda n, s, d=F32: es.enter_context(nc.sbuf_tensor(n, list(s), d)))
                        mk.barrier()
                    if "stop0" not in self.dbg and "stopA1" not in self.dbg:
                        with contextlib.ExitStack() as es:
                            self.phaseAE(lambda n, s, d=F32: es.enter_context(nc.sbuf_tensor(n, list(s), d)))
                        mk.barrier()
                        with contextlib.ExitStack() as es:
                            self.phaseA2(lambda n, s, d=F32: es.enter_context(nc.sbuf_tensor(n, list(s), d)))
                        mk.barrier()
                if "attT" in self.dbg:
                    d = self.dout("d_attT", [64, 8, S_OWN], BF16)
                    self.dma(d[:, :, :], self.attT[:], writes=["d_attT"])
                    self.dbg_out["d_attT"] = "d_attT"
                self.xm_scr = self.dscr("xm_scr", [S_OWN, D])
                if "stop0" not in self.dbg and "stopA1" not in self.dbg and "stopA2" not in self.dbg:
                    with contextlib.ExitStack() as es:
                        self.phaseB(lambda n, s, d=F32: es.enter_context(nc.sbuf_tensor(n, list(s), d)))
                    mk.barrier()
            if "xm" in self.dbg:
                d = self.dout("d_xm", [S_OWN, D])
                self.dma(d[:, :], self.xm_scr[:, :], writes=["d_xm"])
                self.dbg_out["d_xm"] = "d_xm"
                if "retT" in self.dbg:
                    d = self.dout("d_retT", [128, 8, S_OWN], BF16)
                    self.dma(d[:, :, :], self.retT[:], reads=[("retT", h, c) for h in range(8) for c in range(16)], writes=["d_retT"])
                    self.dbg_out["d_retT"] = "d_retT"
            if self.do_peer:
                with contextlib.ExitStack() as es:
                    self.phaseC(lambda n, s, d=F32: es.enter_context(nc.sbuf_tensor(n, list(s), d)))
            else:
                self.dma(self.y[:, :], self.xm_scr[:, :], writes=["y"])
            mk.emit(final_wait_tokens=list(self.dbg_out.values()) + ["y", ("y", 0), ("y", 1)])
        return nc

    def hT(self, kc, col0, n):
        if col0 >= S_OWN:
            return self.hTo[:, kc, col0 - S_OWN:col0 - S_OWN + n]
        assert col0 + n <= S_OWN
        return self.hTc[:, kc, col0:col0 + n]

    def hT_tok(self, col0):
        t = col0 // 128
        return ("hT", t)

    def phase0(self, sb):
        I = self.I
        xt = [sb("xt%d" % i, [128, D]) for i in range(2)]
        xn = [sb("xn%d" % i, [128, D], BF16) for i in range(2)]
        junk = sb("junk0", [128, D], BF16)
        ss = sb("ss0", [128, 32])
        rs = sb("rs0", [128, 32])
        tpb = [self.ps[4][:, :].bitcast(BF16), self.ps[7][:, :].bitcast(BF16)]
        def evac(t):
            s = t % 2
            tp = tpb[s].rearrange("p (k c) -> p k c", k=8)
            dst = (self.hTc if t < 16 else self.hTo)[:, :, (t % 16) * 128:(t % 16 + 1) * 128]
            self.tt("dve", dst, tp, self.g1col[:].unsqueeze(2).to_broadcast([128, 8, 128]), ALU.mult,
                    reads=[("bank", (4, 7)[s]), "g1col"], writes=[("hT", t)])

        for t in range(32):
            s = t % 2
            self.dma(xt[s][:], I["xw"][t * 128:(t + 1) * 128, :], writes=[("xt", s)])
            self.act(junk[:], xt[s][:], AF.Square, reads=[("xt", s)], writes=["junk0", ("ss", t)], accum_out=ss[:, t:t + 1])
            self.act(rs[:, t:t + 1], ss[:, t:t + 1], AF.Sqrt, reads=[("ss", t)], writes=[("rs", t)], scale=1.0 / D, bias=EPS)
            self.recip(rs[:, t:t + 1], rs[:, t:t + 1], reads=[("rs", t)], writes=[("rs", t)])
            self.ts("dve", xn[s][:], xt[s][:], rs[:, t:t + 1], None, ALU.mult, reads=[("xt", s), ("rs", t)], writes=[("xn", s)])
            tp = tpb[s].rearrange("p (k c) -> p k c", k=8)
            for kc in range(8):
                self.tr(tp[:, kc, :], xn[s][:, kc * 128:(kc + 1) * 128], self.ident[:],
                        reads=[("xn", s), "ident"], writes=[("bank", (4, 7)[s])])
            if t >= 1:
                evac(t - 1)
        evac(31)

    def phaseA1(self, sb):
        I, ps = self.I, self.ps
        stage = sb("stageA", [128, 8, 512])
        wbf = sb("wbfA", [128, 8, 512], BF16)
        qT = sb("qT", [128, S_OWN], BF16)
        qdT = sb("qdT", [128, S_OWN], BF16)
        kT = sb("kT", [128, S_WIN], BF16)
        k_tm = sb("k_tm", [128, 32, 128], BF16)
        foldm = sb("foldm", [128, 128], BF16)
        v_tm = sb("v_tm", [128, 32, 128], BF16)
        sg_tm = sb("sg_tm", [128, 16, 128], BF16)
        cs = [sb("cs%d" % i, [128, 512]) for i in range(2)]
        t1 = [sb("t1_%d" % i, [128, 512]) for i in range(2)]
        ub = [sb("ub%d" % i, [128, 512], BF16) for i in range(2)]
        DTs = sb("DTs", [128, 128]); QDs = sb("QDs", [128, 128]); kdec = sb("kdec_s", [128, 8])
        state = sb("state", [128, 128]); state_bf = sb("state_bf", [128, 128], BF16)
        PT = [sb("PT%d" % i, [128, 128], BF16) for i in range(2)]
        st6 = [sb("st6_%d" % i, [128, 6]) for i in range(3)]; mv = [sb("mv%d" % i, [128, 2]) for i in range(3)]
        rstd = [sb("rstdA%d" % i, [128, 1]) for i in range(3)]; o_sb = [sb("o_sb%d" % i, [128, 128]) for i in range(3)]
        yn = [sb("yn%d" % i, [128, 128]) for i in range(2)]; A_tm = [sb("A_tm%d" % i, [128, 128], BF16) for i in range(2)]
        self.dma(kdec[:], I["kdec"][:, :], writes=["kdec"])
        self.dma(foldm[:], I["fold"][:, :], writes=["foldm"])
        w3 = I["w_in"].rearrange("(k p) c -> p k c", p=128)
        csn = [0]

        def fm_proj(wc0, dstT, col0, ntile, is_q, h):
            for T in range(ntile):
                c0 = col0 + T * 512
                s = csn[0] % 2
                csn[0] += 1
                self.dma(cs[s][:], I["csT"][:, c0:c0 + 512], writes=[("cs", s)])
                for kc in range(8):
                    self.mm(ps[s][:, :], wbf[:, kc, wc0:wc0 + 128], self.hT(kc, c0, 512), kc == 0, kc == 7,
                            reads=["wbfA"] + [("hT", c0 // 128 + i) for i in range(4)], writes=[("bank", s)])
                dl = c0 - col0
                if is_q:
                    self.tt("dve", t1[s][:], ps[s][:, :], cs[s][:], ALU.mult, reads=[("bank", s), ("cs", s)], writes=[("t1", s)])
                    self.cp("act", dstT[:, dl:dl + 512], t1[s][:], reads=[("t1", s)], writes=[("qT", T)])
                    self.tt("pool", qdT[:, dl:dl + 512].rearrange("p (c i) -> p c i", c=4), t1[s][:].rearrange("p (c i) -> p c i", c=4),
                            QDs[:].unsqueeze(1).to_broadcast([128, 4, 128]), ALU.mult,
                            reads=[("t1", s), "QDs"], writes=[("qdT", T)])
                else:
                    self.tt("dve", ub[s][:], ps[s][:, :], cs[s][:], ALU.mult, reads=[("bank", s), ("cs", s)], writes=[("ub", s)])
                    self.mm(ps[2 + s][:, :], foldm[:], ub[s][:], True, True, reads=["foldm", ("ub", s)], writes=[("bank", 2 + s)])
                    self.cp("act", dstT[:, dl:dl + 512], ps[2 + s][:, :], reads=[("bank", 2 + s)], writes=[("kT", T)])

        for h in range(8):
            self.dma(DTs[:], I["DT"][:, h, :], writes=["DTs"])
            self.dma(QDs[:], I["QD"][:, h, :], writes=["QDs"])
            q0, k0 = OFFS["rq"] + h * 64, OFFS["rk"] + h * 64
            v0, g0 = OFFS["rv"] + h * 128, OFFS["rg"] + h * 128
            segs = [(0, q0, 64), (64, q0 + 32, 32), (96, q0, 32), (128, k0, 64), (192, k0 + 32, 32), (224, k0, 32),
                    (256, v0, 128), (384, g0, 128)]
            for (d0, s0, n) in segs:
                self.dma(stage[:, :, d0:d0 + n], w3[:, :, s0:s0 + n], writes=["stageA"], nowaw=True)
            self.cp("dve", wbf[:, 0:4, :], stage[:, 0:4, :], reads=["stageA"], writes=["wbfA"])
            self.cp("act", wbf[:, 4:8, :], stage[:, 4:8, :], reads=["stageA"], writes=["wbfA"])
            fm_proj(0, qT, S_OWN, 4, True, h)
            fm_proj(128, kT, 0, 8, False, h)
            for b in range(4):
                tpk = ps[4][:, :].bitcast(BF16).rearrange("p (t d) -> p t d", t=8)
                for tt_ in range(8):
                    t = b * 8 + tt_
                    self.tr(tpk[:, tt_, :], kT[:, t * 128:(t + 1) * 128], self.ident[:],
                            reads=[("kT", t // 4), "ident"], writes=[("bank", 4)])
                self.ts("dve", k_tm[:, b * 8:(b + 1) * 8, :], tpk, kdec[:, h:h + 1], None, ALU.mult,
                        reads=[("bank", 4), "kdec"], writes=[("k_tm", b // 2)])
            for t in range(32):
                bkv = (2, 3, 5, 6)[t % 4]
                own = t >= 16
                n = 256 if own else 128
                pv = ps[bkv][:, 0:n]
                for kc in range(8):
                    self.mm(pv, self.hT(kc, t * 128, 128), wbf[:, kc, 256:256 + n], kc == 0, kc == 7,
                            reads=["wbfA", ("hT", t)], writes=[("bank", bkv)])
                self.cp("act", v_tm[:, t, :], pv[:, 0:128], reads=[("bank", bkv)], writes=[("v_tm", t)])
                if own:
                    self.act(sg_tm[:, t - 16, :], pv[:, 128:256], AF.Silu, reads=[("bank", bkv)], writes=[("sg", t - 16)])
            self.memset("pool", state[:], 0.0, writes=["state"])
            self.memset("pool", state_bf[:], 0.0, writes=["state_bf"])
            gC = self.gC[h]

            def issue_front(n):
                sl = n % 2
                self.mm(ps[5 + sl][:, 0:128], k_tm[:, n, :], v_tm[:, n, :], True, True,
                        reads=[("k_tm", n // 16), ("v_tm", n)], writes=[("bank", 5 + sl)])
                if n >= 16:
                    c = n - 16
                    self.mm(ps[sl][:, 0:128], kT[:, n * 128:(n + 1) * 128], qT[:, c * 128:(c + 1) * 128], True, True,
                            reads=[("kT", n // 4), ("qT", c // 4)], writes=[("bank", sl)])

            issue_front(0)

            def post_stats(n):
                k = n % 3
                self.mk.op("dve", lambda e, k=k: e.bn_stats(out=st6[k][:], in_=o_sb[k][:]), reads=[("o_sb", k)], writes=[("st6", k)])
                self.mk.op("dve", lambda e, k=k: e.bn_aggr(out=mv[k][:], in_=st6[k][:]), reads=[("st6", k)], writes=[("mv", k)])
                self.act(rstd[k][:], mv[k][:, 1:2], AF.Sqrt, reads=[("mv", k)], writes=[("rstdA", k)], bias=EPS, scale=1.0)

            def post_norm(n):
                k = n % 3
                c = n - 16
                sl = n % 2
                self.recip(rstd[k][:], rstd[k][:], reads=[("rstdA", k)], writes=[("rstdA", k)])
                self.ts("dve", yn[sl][:], o_sb[k][:], mv[k][:, 0:1], rstd[k][:, 0:1], ALU.subtract, ALU.mult,
                        reads=[("o_sb", k), ("mv", k), ("rstdA", k)], writes=[("yn", sl)])
                self.tt("pool", A_tm[sl][:], yn[sl][:], sg_tm[:, c, :], ALU.mult, reads=[("yn", sl), ("sg", c)], writes=[("A_tm", sl)])
                atr = ps[(7, 4)[sl]][:, :].bitcast(BF16)[:, 0:128]
                self.tr(atr, A_tm[sl][:], self.ident[:], reads=[("A_tm", sl), "ident"], writes=[("bank", (7, 4)[sl])])
                self.cp("act", self.retT[:, h, c * 128:(c + 1) * 128], atr, reads=[("bank", (7, 4)[sl])], writes=[("retT", h, c)])

            for n in range(32):
                sl = n % 2
                if n + 1 < 32:
                    issue_front(n + 1)
                if n >= 16:
                    c = n - 16
                    self.tt("dve", PT[sl][:], ps[sl][:, 0:128], DTs[:], ALU.mult,
                            reads=[("bank", sl), "DTs"], writes=[("PT", sl)])
                    o_ps = ps[2 + sl][:, 0:128]
                    self.mm(o_ps, PT[sl][:], v_tm[:, n, :], True, False, reads=[("PT", sl), ("v_tm", n)], writes=[("bank", 2 + sl)])
                    self.mm(o_ps, qdT[:, c * 128:(c + 1) * 128], state_bf[:], False, True,
                            reads=[("qdT", c // 4), "state_bf"], writes=[("bank", 2 + sl)])
                    self.cp("act", o_sb[n % 3][:], o_ps, reads=[("bank", 2 + sl)], writes=[("o_sb", n % 3)])
                self.stt("dve", state[:], state[:], gC, ps[5 + sl][:, 0:128], ALU.mult, ALU.add,
                         reads=["state", ("bank", 5 + sl)], writes=["state"])
                if n >= 15:
                    self.cp("act", state_bf[:], state[:], reads=["state"], writes=["state_bf"])
                if n - 2 >= 16:
                    post_norm(n - 2)
                if n - 1 >= 16:
                    post_stats(n - 1)
            post_norm(30)
            post_stats(31)
            post_norm(31)


    def phaseAE(self, sb):
        I, ps = self.I, self.ps
        self.Escr = self.dscr("Escr", [8, EV_LEN], BF16)
        l33 = sb("l33", [33, 8])
        ohs = [sb("ohs%d" % i, [33, 512]) for i in range(2)]
        Ev = sb("Ev", [8, EV_LEN], BF16)
        self.dma(l33[0:32, :], I["rel_bias"][:, :], writes=["l33"])
        self.dma(l33[32:33, :], I["negrow"][:, :], writes=["l33"], nowaw=True)
        for j in range(10):
            c0 = j * 512
            n = min(512, EV_LEN - c0)
            s = j % 2
            self.dma(ohs[s][:, 0:n], I["OH33"][:, c0:c0 + n], writes=[("ohs", s)])
            self.mm(ps[s][0:8, 0:n], l33[:, :], ohs[s][:, 0:n], True, True, reads=["l33", ("ohs", s)], writes=[("bank", s)])
            self.act(Ev[:, c0:c0 + n], ps[s][0:8, 0:n], AF.Exp, reads=[("bank", s)], writes=["Ev"])
        self.dma(self.Escr[:, :], Ev[:], reads=["Ev"], writes=["Escr"])

    def phaseA2(self, sb):
        I, ps, mk = self.I, self.ps, self.mk
        stage = sb("stageM", [128, 8, 192])
        wbf = sb("wbfM", [128, 8, 192], BF16)
        q_tm = sb("q_tm", [128, 16, 64], BF16)
        k_tm = sb("k_tm2", [128, 32, 64], BF16)
        v_aug = sb("v_aug", [128, 32, 65], BF16)
        qTa = sb("qTa", [80, S_OWN], BF16)
        kTa = sb("kTa", [80, S_WIN], BF16)
        kmT = sb("kmT", [64, 16]); kmTb = sb("kmTb", [64, 16], BF16)
        gm = sb("gm", [128, 16, 16]); m8 = sb("m8", [128, 8]); thr = sb("thr", [128, 1])
        nm_tm = sb("nm_tm", [128, 16, 16], BF16); nmT = sb("nmT", [16, S_OWN], BF16)
        Mp = sb("Mp", [128, STRIP_W], BF16); Ms = sb("Ms", [128, STRIP_W], BF16)
        expS = [sb("expS%d" % i, [128, 512], BF16) for i in range(3)]
        PTm = [sb("PTm%d" % i, [128, 512], BF16) for i in range(3)]
        rec = sb("rec", [128, 4]); att_tm = sb("att_tm", [128, 4, 64], BF16)
        ssqk = sb("ssqk", [128, 8]); rr = sb("rr", [128, 8]); junk = sb("junkM", [128, 64], BF16)
        gq8 = sb("gq8", [128, 64]); gk = sb("gk", [128, 64]); ctxp = sb("ctxp", [128, 16])
        self.dma(gq8[:], I["mqg"][0:1, :].partition_broadcast(128), writes=["gq8"])
        self.dma(gk[:], I["mkg"][0:1, :].partition_broadcast(128), writes=["gk"])
        self.dma(ctxp[:], I["ctxpen"][:, :], writes=["ctxp"])
        self.ts("dve", gq8[:], gq8[:], 0.125, None, ALU.mult, reads=["gq8"], writes=["gq8"])
        self.dma(kTa[64:80, :], I["blkoh"][:, :], writes=["kTa_oh"])
        self.memset("pool", v_aug[:, :, 64:65], 1.0, writes=["v_ones"])
        self.memset("pool", ssqk[:], 1.0, writes=[("ssqk", i) for i in range(4)])
        w3 = I["w_in"].rearrange("(k p) c -> p k c", p=128)
        tpb = ps[2][:, :].bitcast(BF16)
        B2 = ("bank", 2)
        sct = [0]
        self.UT_scr = self.dscr("UT_scr", [128, 128, 8, 128], BF16)
        self.V_scr = self.dscr("V_scr", [16384, D], BF16)
        if self.do_peer:
            ust = sb("ustP", [128, D]); ubf = sb("ubfP", [128, D], BF16); utb = sb("utbP", [128, 8, 128], BF16)
            vst = sb("vstP", [128, D]); vbf = sb("vbfP", [128, D], BF16)
        pstep = [0]

        tp0 = ps[0][:, :].bitcast(BF16).rearrange("p (k c) -> p k c", k=8)
        B0 = ("bank", 0)

        def P_T():
            a = pstep[0] - 2
            if self.do_peer and 0 <= a < 128:
                for kc in range(8):
                    self.tr(tp0[:, kc, :], ubf[:, kc * 128:(kc + 1) * 128], self.ident[:], reads=["ubfP", "ident"], writes=[B0])
                self.dma(self.V_scr[a * 128:(a + 1) * 128, :], vbf[:], reads=["vbfP"], writes=["V_scrP"], key="V_scrP")

        def P_C():
            g = pstep[0]
            pstep[0] += 1
            if not self.do_peer:
                return
            a = g - 2
            if 0 <= a < 128:
                self.cp("dve", utb[:], tp0, reads=[B0], writes=["utbP"])
                self.dma(self.UT_scr[a, :, :, :], utb[:], reads=["utbP"], writes=["UT_scrP"], key="UT_scrP")
            a = g - 1
            if 0 <= a < 128:
                self.cp("act", ubf[:], ust[:], reads=["ustP"], writes=["ubfP"])
                self.cp("dve", vbf[:], vst[:], reads=["vstP"], writes=["vbfP"])
            a = g
            if 0 <= a < 128:
                self.dma(ust[:], I["peer_u"][a * 128:(a + 1) * 128, :], writes=["ustP"])
                self.dma(vst[:], I["peer_v"][a * 128:(a + 1) * 128, :], writes=["vstP"])

        self.dma(Mp[:], bass.AP(self.Escr.tensor, 0, [[1, 128], [1, STRIP_W]]), reads=["Escr"], writes=["Mp"])
        for h in range(8):
            for i, nm_ in enumerate(("mq", "mk", "mv")):
                self.dma(stage[:, :, i * 64:(i + 1) * 64], w3[:, :, OFFS[nm_] + h * 64:OFFS[nm_] + (h + 1) * 64],
                         writes=["stageM"], nowaw=True)
            self.cp("dve", wbf[:, :, :], stage[:, :, :], reads=["stageM"], writes=["wbfM"])
            for j in range(9):
                bk = (6, 7)[j % 2]
                self.mm(ps[bk][:, :], self.Jm[:], Mp[:, j * 512:(j + 1) * 512], True, True, reads=["Jm", "Mp"], writes=[("bank", bk)])
                self.cp("act" if j % 2 else "dve", Ms[:, j * 512:(j + 1) * 512], ps[bk][:, :], reads=[("bank", bk)], writes=["Ms"])
            if h + 1 < 8:
                self.dma(Mp[:], bass.AP(self.Escr.tensor, (h + 1) * EV_LEN, [[1, 128], [1, STRIP_W]]), reads=["Escr"], writes=["Mp"])
            for t in range(32):
                own = t >= 16
                c0 = 0 if own else 64
                n = 192 - c0
                bk = (0, 1, 4, 5)[t % 4]
                q4 = t % 4
                pp = ps[bk][:, 0:n]
                for kc in range(8):
                    self.mm(pp, self.hT(kc, t * 128, 128), wbf[:, kc, c0:192], kc == 0, kc == 7,
                            reads=["wbfM", ("hT", t)], writes=[("bank", bk)])
                kcol = 64 - c0
                sq = ssqk[:, 2 * q4:2 * q4 + 2]
                rq = rr[:, 2 * q4:2 * q4 + 2]
                self.act(junk[:], pp[:, kcol:kcol + 64], AF.Square, reads=[("bank", bk)], writes=["junkM", ("ssqk", q4)], accum_out=sq[:, 0:1])
                if own:
                    self.act(junk[:], pp[:, 0:64], AF.Square, reads=[("bank", bk)], writes=["junkM", ("ssqk", q4)], accum_out=sq[:, 1:2])
                self.act(rq, sq, AF.Sqrt, reads=[("ssqk", q4)], writes=[("rr", q4)], scale=1.0 / 64, bias=EPS)
                self.recip(rq, rq, reads=[("rr", q4)], writes=[("rr", q4)])
                self.stt("dve", k_tm[:, t, :], pp[:, kcol:kcol + 64], rq[:, 0:1], gk[:], ALU.mult, ALU.mult,
                         reads=[("bank", bk), ("rr", q4), "gk"], writes=[("k_tm2", t // 8)])
                if own:
                    self.stt("dve", q_tm[:, t - 16, :], pp[:, 0:64], rq[:, 1:2], gq8[:], ALU.mult, ALU.mult,
                             reads=[("bank", bk), ("rr", q4), "gq8"], writes=[("q_tm", (t - 16) // 8)])
                self.cp("act", v_aug[:, t, 0:64], pp[:, kcol + 64:kcol + 128], reads=[("bank", bk)], writes=[("v_aug", t)])
            for b in range(4):
                for i in range(8):
                    self.tr(tpb[0:64, i * 128:(i + 1) * 128], k_tm[:, b * 8 + i, :], self.ident[:],
                            reads=[("k_tm2", b), "ident"], writes=[B2])
                self.cp("act", kTa[0:64, b * 1024:(b + 1) * 1024], tpb[0:64, :], reads=[B2], writes=[("kTa", b)])
                mk.op("dve", lambda e, b=b: e.tensor_reduce(out=kmT[:, b * 4:(b + 1) * 4],
                                                            in_=kTa[0:64, b * 1024:(b + 1) * 1024].rearrange("p (n l) -> p n l", l=256),
                                                            axis=AX.X, op=ALU.add),
                      reads=[("kTa", b)], writes=["kmT"])
            for b in range(2):
                for i in range(8):
                    self.tr(tpb[0:64, i * 128:(i + 1) * 128], q_tm[:, b * 8 + i, :], self.ident[:],
                            reads=[("q_tm", b), "ident"], writes=[B2])
                self.cp("act", qTa[0:64, b * 1024:(b + 1) * 1024], tpb[0:64, :], reads=[B2], writes=[("qTa", b)])
            self.ts("dve", kmTb[:], kmT[:], 1.0 / 256, None, ALU.mult, reads=["kmT"], writes=["kmTb"])
            for c in range(16):
                self.mm(ps[3][:, c * 16:(c + 1) * 16], qTa[0:64, c * 128:(c + 1) * 128], kmTb[:, :], True, True,
                        reads=[("qTa", c // 8), "kmTb"], writes=[("bank", 3)])
            self.tt("dve", gm[:], ps[3][:, 0:256].rearrange("p (c n) -> p c n", n=16),
                    ctxp[:].unsqueeze(1).to_broadcast([128, 16, 16]), ALU.add, reads=[("bank", 3), "ctxp"], writes=["gm"])
            self.memset("pool", nm_tm[:], 0.0, writes=["nm_tm"])
            for c in range(16):
                Bk = 8 + c // 2
                mk.op("dve", lambda e, c=c, Bk=Bk: e.max(out=m8[:], in_=gm[:, c, 0:Bk]), reads=["gm"], writes=["m8"])
                self.ts("dve", thr[:], m8[:, 2:3], -1e29, None, ALU.max, reads=["m8"], writes=["thr"])
                self.ts("dve", nm_tm[:, c, 0:Bk], gm[:, c, 0:Bk], thr[:, 0:1], -30000.0, ALU.is_lt, ALU.mult,
                        reads=["gm", "thr", "nm_tm"], writes=["nm_tm"])
            for b in range(2):
                for i in range(8):
                    self.tr(tpb[0:16, i * 128:(i + 1) * 128], nm_tm[:, b * 8 + i, :], self.ident[:],
                            reads=["nm_tm", "ident"], writes=[B2])
                self.cp("act", nmT[:, b * 1024:(b + 1) * 1024], tpb[0:16, :], reads=[B2], writes=["nmT"])
            self.dma(qTa[64:80, :], nmT[:], reads=["nmT"], writes=["qTa_nm"])
            tiles = [(qt, kt) for qt in range(4) for kt in range(20 + 4 * qt)]

            def issue_S(i):
                qt, kt = tiles[i]
                k0 = 128 * kt
                sbk = (4, 5, 6, 1)[i % 4]
                self.mm(ps[sbk][:, :], kTa[0:80, k0:k0 + 128], qTa[0:80, qt * 512:(qt + 1) * 512], True, True,
                        reads=[("kTa", kt // 8), "kTa_oh", ("qTa", qt // 2), "qTa_nm"], writes=[("bank", sbk)])

            def issue_rest(i):
                qt, kt = tiles[i]
                q0 = S_OWN + 512 * qt
                nkt = 20 + 4 * qt
                ob = 7 if qt % 2 == 0 else 3
                O = ps[ob][:, 0:260].rearrange("p (j e) -> p j e", e=65)
                Dd = q0 - 128 * kt
                sbk = (4, 5, 6, 1)[i % 4]
                s2 = i % 3
                self.act(expS[s2][:], ps[sbk][:, :], AF.Exp, reads=[("bank", sbk)], writes=[("expS", s2)])
                self.tt("dve", PTm[s2][:], expS[s2][:], Ms[:, Dd + 384:Dd + 384 + 512], ALU.mult,
                        reads=[("expS", s2), "Ms"], writes=[("PTm", s2)])
                for j in range(4):
                    qo = 4 * qt + j
                    if kt - 16 > qo:
                        continue
                    self.mm(O[:, j, :], PTm[s2][:, j * 128:(j + 1) * 128], v_aug[:, kt, :], kt == 0 and j == 0, kt == nkt - 1 and j == 3,
                            reads=[("PTm", s2), ("v_aug", kt), "v_ones"], writes=[("bank", ob)])
                if kt == nkt - 1:
                    pending.append((i + 3, qt, ob, O))

            def finalize(qt, ob, O):
                self.recip(rec[:], O[:, :, 64], reads=[("bank", ob)], writes=["rec"])
                self.tt("dve", att_tm[:], O[:, :, 0:64], rec[:].unsqueeze(2).to_broadcast([128, 4, 64]), ALU.mult,
                        reads=[("bank", ob), "rec"], writes=["att_tm"])
                for j in range(4):
                    self.tr(tpb[0:64, j * 128:(j + 1) * 128], att_tm[:, j, :], self.ident[:], reads=["att_tm", "ident"], writes=[B2])
                self.cp("act", self.attT[:, h, qt * 512:(qt + 1) * 512], tpb[0:64, 0:512], reads=[B2], writes=[("attT", h, qt)])

            pending = []
            issue_S(0)
            issue_S(1)
            issue_S(2)
            for i in range(len(tiles)):
                if i + 3 < len(tiles):
                    issue_S(i + 3)
                issue_rest(i)
                while pending and pending[0][0] <= i:
                    _, qt_, ob_, O_ = pending.pop(0)
                    finalize(qt_, ob_, O_)
                if i < 96:
                    if i % 6 == 0:
                        P_T()
                    elif i % 6 == 3:
                        P_C()
            while pending:
                _, qt_, ob_, O_ = pending.pop(0)
                finalize(qt_, ob_, O_)
        while pstep[0] < 131:
            P_T()
            P_C()

    def phaseB(self, sb):
        I, ps = self.I, self.ps
        mergedT = sb("mergedT", [128, 8, S_OWN], BF16)
        wout = sb("woutb", [128, 8, D], BF16)
        cst = [sb("cstB%d" % i, [128, 8, 128]) for i in range(4)]
        cbf = [[sb("cbfB%d_%d" % (j, i), [128, 8, 128], BF16) for i in range(4)] for j in range(2)]
        sa = [sb("saB%d" % i, [128, 512]) for i in range(2)]; sg = [sb("sgB%d" % i, [128, 512]) for i in range(2)]
        xo = [sb("xoB%d" % i, [128, D]) for i in range(2)]
        xm = [sb("xmB%d" % i, [128, D]) for i in range(2)]
        hn = sb("hnB", [128, D], BF16); junk = sb("junkB", [128, D], BF16)
        ss = sb("ssB", [128, 16]); rs = sb("rsB", [128, 16])
        w3 = I["w_in"].rearrange("(k p) c -> p k c", p=128)
        wr3 = I["ret_w"].rearrange("(k p) c -> p k c", p=128)
        wb3 = I["moba_w"].rearrange("(h d) c -> d h c", d=64)
        wo3 = I["w_out"].rearrange("(k p) c -> p k c", p=128)
        engs = ("dve", "act", "dve", "act")

        def load_w(fc):
            cols = slice(fc * 128, (fc + 1) * 128)
            j = fc % 2
            self.dma(cst[0][:], wr3[:, :, cols], writes=[("cst", 0)])
            self.dma(cst[1][0:64, :, :], wb3[:, :, cols], writes=[("cst", 1)])
            self.dma(cst[2][:], w3[:, :, OFFS["ga"] + fc * 128:OFFS["ga"] + (fc + 1) * 128], writes=[("cst", 2)])
            self.dma(cst[3][:], w3[:, :, OFFS["gb"] + fc * 128:OFFS["gb"] + (fc + 1) * 128], writes=[("cst", 3)])
            for i in range(4):
                p = 64 if i == 1 else 128
                self.cp(engs[i], cbf[j][i][0:p, :, :], cst[i][0:p, :, :], reads=[("cst", i)], writes=[("cbf", j, i)])

        load_w(0)
        it = 0
        for fc in range(8):
            j = fc % 2
            if fc + 1 < 8:
                load_w(fc + 1)
            for T in range(4):
                tc = slice(T * 512, (T + 1) * 512)
                b0 = 4 * (T % 2)
                q = it % 2
                it += 1
                for kc in range(8):
                    self.mm(ps[b0 + 2][:, :], cbf[j][2][:, kc, :], self.hTo[:, kc, tc], kc == 0, kc == 7,
                            reads=[("cbf", j, 2)] + [("hT", 16 + T * 4 + i) for i in range(4)], writes=[("bank", b0 + 2)])
                for kc in range(8):
                    self.mm(ps[b0 + 3][:, :], cbf[j][3][:, kc, :], self.hTo[:, kc, tc], kc == 0, kc == 7,
                            reads=[("cbf", j, 3)] + [("hT", 16 + T * 4 + i) for i in range(4)], writes=[("bank", b0 + 3)])
                for kc in range(8):
                    self.mm(ps[b0][:, :], cbf[j][0][:, kc, :], self.retT[:, kc, tc], kc == 0, kc == 7,
                            reads=[("cbf", j, 0)], writes=[("bank", b0)])
                for h in range(8):
                    self.mm(ps[b0 + 1][:, :], cbf[j][1][0:64, h, :], self.attT[0:64, h, tc], h == 0, h == 7,
                            reads=[("cbf", j, 1)], writes=[("bank", b0 + 1)])
                self.act(sa[q][:], ps[b0 + 2][:, :], AF.Sigmoid, reads=[("bank", b0 + 2)], writes=[("saB", q)])
                self.act(sg[q][:], ps[b0 + 3][:, :], AF.Sigmoid, reads=[("bank", b0 + 3)], writes=[("sgB", q)])
                self.tt("dve", sa[q][:], sa[q][:], ps[b0][:, :], ALU.mult, reads=[("saB", q), ("bank", b0)], writes=[("saB", q)])
                self.tt("dve", sg[q][:], sg[q][:], ps[b0 + 1][:, :], ALU.mult, reads=[("sgB", q), ("bank", b0 + 1)], writes=[("sgB", q)])
                self.tt("pool", mergedT[:, fc, tc], sa[q][:], sg[q][:], ALU.add, reads=[("saB", q), ("sgB", q)], writes=[("mT", T)])
        for j in range(8):
            i = j % 4
            self.dma(cst[i][:], wo3[:, :, j * 128:(j + 1) * 128], writes=[("cst", i)])
            self.cp(engs[i], wout[:, :, j * 128:(j + 1) * 128], cst[i][:], reads=[("cst", i)], writes=["wout"])
        tpb = [ps[4][:, :].bitcast(BF16), ps[5][:, :].bitcast(BF16)]

        def b2_front(t):
            s = t % 2
            self.dma(xo[s][:], I["xw"][S_OWN + t * 128:S_OWN + (t + 1) * 128, :], writes=[("xo", s)])
            for hf in range(2):
                bk = 2 * s + hf
                for kc in range(8):
                    self.mm(ps[bk][:, :], mergedT[:, kc, t * 128:(t + 1) * 128], wout[:, kc, hf * 512:(hf + 1) * 512], kc == 0, kc == 7,
                            reads=[("mT", t // 4), "wout"], writes=[("bank", bk)])
                self.tt("dve", xm[s][:, hf * 512:(hf + 1) * 512], ps[bk][:, :], xo[s][:, hf * 512:(hf + 1) * 512], ALU.add,
                        reads=[("bank", bk), ("xo", s)], writes=[("xm", s)])
            self.dma(self.xm_scr[t * 128:(t + 1) * 128, :], xm[s][:], reads=[("xm", s)], writes=[("xm_scr", t)], key=("xm_scr", s))
            self.act(junk[:], xm[s][:], AF.Square, reads=[("xm", s)], writes=["junkB", ("ssB", t)], accum_out=ss[:, t:t + 1])
            self.act(rs[:, t:t + 1], ss[:, t:t + 1], AF.Sqrt, reads=[("ssB", t)], writes=[("rsB", t)], scale=1.0 / D, bias=EPS)

        def b2_mid(t):
            s = t % 2
            self.recip(rs[:, t:t + 1], rs[:, t:t + 1], reads=[("rsB", t)], writes=[("rsB", t)])
            self.ts("dve", hn[:], xm[s][:], rs[:, t:t + 1], None, ALU.mult, reads=[("xm", s), ("rsB", t)], writes=["hnB"])
            tp = tpb[s].rearrange("p (k c) -> p k c", k=8)
            for kc in range(8):
                self.tr(tp[:, kc, :], hn[:, kc * 128:(kc + 1) * 128], self.ident[:], reads=["hnB", "ident"], writes=[("bank", 4 + s)])

        def b2_back(t):
            s = t % 2
            tp = tpb[s].rearrange("p (k c) -> p k c", k=8)
            self.tt("dve", self.hTo[:, :, t * 128:(t + 1) * 128], tp, self.g2col[:].unsqueeze(2).to_broadcast([128, 8, 128]), ALU.mult,
                    reads=[("bank", 4 + s), "g2col"], writes=[("hT", 16 + t)])

        b2_front(0)
        for t in range(16):
            if t + 1 < 16:
                b2_front(t + 1)
            b2_mid(t)
            if t >= 1:
                b2_back(t - 1)
        b2_back(15)


    def phaseC(self, sb):
        I, ps, mk = self.I, self.ps, self.mk
        NG, GT = 8, 256
        qT_scr = self.dscr("qT_scr", [128, 16, S_OWN])
        h2T = self.hTo
        skT = sb("skT", [128, 16, 128]); identf = sb("identfC", [128, 128])
        self.dma(identf[:], I["identf"][:, :], writes=["identf"])
        with contextlib.ExitStack() as es:
            skn = es.enter_context(self.nc.sbuf_tensor("skn", [128, 16, 128], F32))
            self.dma(skn[:], I["sub_keys"].rearrange("g n d -> n g d"), writes=["skn"])
            for hc in range(16):
                bk = hc % 2
                self.tr(ps[bk][:, 0:128], skn[:, hc, :], identf[:], reads=["skn", "identf"], writes=[("bank", bk)])
                self.cp("act" if hc % 2 else "dve", skT[:, hc, :], ps[bk][:, 0:128], reads=[("bank", bk)], writes=["skT"])
            wq = es.enter_context(self.nc.sbuf_tensor("wqC", [128, 8, 2048], BF16))
            wst2 = [es.enter_context(self.nc.sbuf_tensor("wstC%d" % i, [128, 8, 512], F32)) for i in range(2)]
            qst = [es.enter_context(self.nc.sbuf_tensor("qstC%d" % i, [128, 512], F32)) for i in range(2)]
            wq3 = I["peer_wq"].rearrange("(k p) c -> p k c", p=128)
            for j in range(4):
                wst = wst2[j % 2]
                self.dma(wst[:], wq3[:, :, j * 512:(j + 1) * 512], writes=[("wstC", j % 2)])
                self.cp("dve", wq[:, 0:4, j * 512:(j + 1) * 512], wst[:, 0:4, :], reads=[("wstC", j % 2)], writes=[("wqC", j, 0)])
                self.cp("act", wq[:, 4:8, j * 512:(j + 1) * 512], wst[:, 4:8, :], reads=[("wstC", j % 2)], writes=[("wqC", j, 1)])
            cnt = 0
            for hc in range(16):
                for T in range(4):
                    bk = 2 + cnt % 2
                    s = cnt % 2
                    cnt += 1
                    for kc in range(8):
                        self.mm(ps[bk][:, :], wq[:, kc, hc * 128:(hc + 1) * 128], h2T[:, kc, T * 512:(T + 1) * 512], kc == 0, kc == 7,
                                reads=[("wqC", hc // 4, 0), ("wqC", hc // 4, 1)], writes=[("bank", bk)])
                    self.cp("act" if s else "dve", qst[s][:], ps[bk][:, :], reads=[("bank", bk)], writes=[("qst", s)])
                    self.dma(qT_scr[:, hc, T * 512:(T + 1) * 512], qst[s][:], reads=[("qst", s)], writes=["qT_scr"], key=("qT_scr", s), nowaw=True)
        self.mk.barrier()
        EE = sb("EE", [128, 2, 16, 128])
        UTc = [sb("UTc%d" % i, [128, 8, 8, 128], BF16) for i in range(2)]
        Vc = [sb("Vc%d" % i, [128, 8, D], BF16) for i in range(2)]
        gact = [sb("gact%d" % i, [128, 8, GT], BF16) for i in range(2)]
        Ew = [sb("Ew%d" % i, [128, 8, 128]) for i in range(5)]
        Wh = [sb("Wh%d" % i, [128, 8, 8, 128], BF16) for i in range(2)]
        WaT = [sb("WaT%d" % i, [128, 8, 128], BF16) for i in range(2)]
        Dg = sb("Dg", [128, 2, 8, 128], BF16)
        qTg = sb("qTg", [128, 16, 128])
        negm = sb("negm", [128, 2, 16]); ev = sb("ev", [128, 2, 16, 16]); tmpE = sb("tmpE", [128, 128])
        tmpC = sb("tmpC", [128, 256]); t16all = sb("t16all", [128, 2, 8, 16])
        ec = sb("ec", [128, 2, 8]); Zs = sb("Zs", [128, 2, 8]); rz = sb("rz", [128, 2, 8])
        xmt = [sb("xmtC0", [128, D])] * 2
        ld = [0]

        def load_chunk(c):
            s = ld[0] % 2
            ld[0] += 1
            self.dma(UTc[s][:], self.UT_scr[c * 8:(c + 1) * 8, :, :, :].rearrange("a d k b -> d a k b"), writes=[("UTc", s)])
            self.dma(Vc[s][:], self.V_scr[c * 1024:(c + 1) * 1024, :].rearrange("(a b) d -> b a d", b=128), writes=[("Vc", s)])
            return s

        for g in range(NG):
            g0 = g * GT
            def st_scores(tl):
                if not (tl == 0 and g > 0):
                    self.dma(qTg[:], qT_scr[:, :, g0 + tl * 128:g0 + (tl + 1) * 128], writes=["qTg"])
                for hc in range(16):
                    bk = 4 * (1 - tl) + hc // 4
                    self.mm(ps[bk][:, (hc % 4) * 128:(hc % 4 + 1) * 128], qTg[:, hc, :], skT[:, hc, :], True, True,
                            reads=["qTg", "skT"], writes=[("bank", bk)])

            def st_max(tl):
                for q4 in range(4):
                    bk = 4 * (1 - tl) + q4
                    mk.op("dve", lambda e, bk=bk, q4=q4, tl=tl: e.tensor_reduce(out=negm[:, tl, q4 * 4:(q4 + 1) * 4],
                                                                            in_=ps[bk][:, :].rearrange("p (g n) -> p g n", n=128),
                                                                            axis=AX.X, op=ALU.max),
                          reads=[("bank", bk)], writes=[("negm", tl)])
                self.ts("dve", negm[:, tl, :], negm[:, tl, :], -1.0, None, ALU.mult, reads=[("negm", tl)], writes=[("negm", tl)])

            def st_exp(tl):
                for hc in range(16):
                    bk = 4 * (1 - tl) + hc // 4
                    self.act(EE[:, tl, hc, :], ps[bk][:, (hc % 4) * 128:(hc % 4 + 1) * 128], AF.Exp, reads=[("bank", bk), ("negm", tl)],
                             writes=[("EE", tl), ("EEh", tl, hc)], bias=negm[:, tl, hc:hc + 1], scale=1.0)

            def st_topk(tl):
                for hc in range(16):
                    mk.op("dve", lambda e, hc=hc, tl=tl: e.max(out=ev[:, tl, hc, 0:8], in_=EE[:, tl, hc, :]), reads=[("EEh", tl, hc)], writes=[("ev", tl)])
                    mk.op("dve", lambda e, hc=hc, tl=tl: e.match_replace(out=tmpE[:], in_to_replace=ev[:, tl, hc, 0:8], in_values=EE[:, tl, hc, :], imm_value=-1.0),
                          reads=[("EEh", tl, hc), ("ev", tl)], writes=["tmpE"])
                    mk.op("dve", lambda e, hc=hc, tl=tl: e.max(out=ev[:, tl, hc, 8:16], in_=tmpE[:]), reads=["tmpE"], writes=[("ev", tl)])

            def st_heads(tl):
                ev4 = ev[:, tl, :, :].rearrange("p (h c) k -> p h c k", c=2)
                for half in range(2):
                    es_ = 2 * tl + half
                    cnd = Ew[es_][:].rearrange("p a b -> p (a b)").rearrange("p (h i j) -> p h i j", h=4, i=16)
                    self.tt("dve", cnd, ev4[:, half * 4:(half + 1) * 4, 0, :].unsqueeze(3).to_broadcast([128, 4, 16, 16]),
                            ev4[:, half * 4:(half + 1) * 4, 1, :].unsqueeze(2).to_broadcast([128, 4, 16, 16]), ALU.mult,
                            reads=[("ev", tl)], writes=[("Ew", es_)] + [("Ew", es_, a) for a in range(8)])
                    for hh in range(4):
                        h = half * 4 + hh
                        cf = Ew[es_][:].rearrange("p a b -> p (a b)")[:, hh * 256:(hh + 1) * 256]
                        mk.op("dve", lambda e, cf=cf, h=h, tl=tl: e.max(out=t16all[:, tl, h, 0:8], in_=cf), reads=[("Ew", es_)] + [("Ew", es_, a) for a in range(8)], writes=[("t16a", tl)])
                        mk.op("dve", lambda e, cf=cf, h=h, tl=tl: e.match_replace(out=tmpC[:], in_to_replace=t16all[:, tl, h, 0:8], in_values=cf, imm_value=-1.0),
                              reads=[("Ew", es_), ("t16a", tl)] + [("Ew", es_, a) for a in range(8)], writes=["tmpC"])
                        mk.op("dve", lambda e, h=h, tl=tl: e.max(out=t16all[:, tl, h, 8:16], in_=tmpC[:]), reads=["tmpC"], writes=[("t16a", tl)])
                self.cp("dve", ec[:, tl, :], t16all[:, tl, :, 15], reads=[("t16a", tl)], writes=["ec"])
                mk.op("dve", lambda e, tl=tl: e.tensor_reduce(out=Zs[:, tl, :], in_=t16all[:, tl, :, :], axis=AX.X, op=ALU.add),
                      reads=[("t16a", tl)], writes=[("Zs", tl)])
                self.recip(rz[:, tl, :], Zs[:, tl, :], reads=[("Zs", tl)], writes=[("rz", tl)])
                for h in range(8):
                    self.ts("dve", Dg[:, tl, h, :], self.ident[:], rz[:, tl, h:h + 1], None, ALU.mult, reads=["ident", ("rz", tl)], writes=[("Dg", tl)])

            for stage in (st_scores, st_max, st_exp, st_topk, st_heads):
                stage(0)
                stage(1)
                if stage is st_scores and g + 1 < NG:
                    self.dma(qTg[:], qT_scr[:, :, g0 + GT:g0 + GT + 128], writes=["qTg"])
            nxt = load_chunk(0)
            units = [(c, tl) for c in range(16) for tl in range(2)]
            cslot = {}

            def emit_tail(u):
                c, tl = units[u]
                w = u % 2
                s_ = cslot[c]
                gs = c % 2
                for hf in range(2):
                    self.tt("dve", WaT[w][:, hf * 4:(hf + 1) * 4, :], ps[4 + hf][:, :].rearrange("p (a t) -> p a t", a=4),
                            gact[gs][:, hf * 4:(hf + 1) * 4, tl * 128:(tl + 1) * 128], ALU.mult,
                            reads=[("bank", 4 + hf), ("gact", gs)], writes=[("WaT", w)])
                for hf in range(2):
                    ob = tl * 2 + hf
                    for a in range(8):
                        self.mm(ps[ob][:, :], WaT[w][:, a, :], Vc[s_][:, a, hf * 512:(hf + 1) * 512], c == 0 and a == 0, c == 15 and a == 7,
                                reads=[("WaT", w), ("Vc", s_)], writes=[("bank", ob)])

            for u, (c, tl) in enumerate(units):
                for h in (4, 5, 6, 7):
                    e = Ew[h - 3]
                    for a in range(8):
                        self.act(e[:, a, :], EE[:, tl, 2 * h + 1, :], AF.Copy, reads=[("EE", tl)], writes=[("Ew", h - 3, a)],
                                 scale=EE[:, tl, 2 * h, c * 8 + a:c * 8 + a + 1])
                if tl == 0:
                    s = nxt
                    cslot[c] = s
                    gs = c % 2
                    for a in range(8):
                        bk = 6 + (a // 2) % 2
                        half = (a % 2) * GT
                        for kc in range(8):
                            self.mm(ps[bk][:, half:half + GT], UTc[s][:, a, kc, :], h2T[:, kc, g0:g0 + GT], kc == 0, kc == 7,
                                    reads=[("UTc", s)], writes=[("bank", bk)])
                        if a % 2 == 1:
                            self.act(gact[gs][:, a - 1:a + 1, :], ps[bk][:, :].rearrange("p (a t) -> p a t", a=2), AF.Gelu,
                                     reads=[("bank", bk)], writes=[("gact", gs)])
                wb = u % 2
                for h in range(8):
                    if h < 4:
                        si = 0
                        e = Ew[0]
                        self.tt("dve", e[:], EE[:, tl, 2 * h, c * 8:(c + 1) * 8].unsqueeze(2).to_broadcast([128, 8, 128]),
                                EE[:, tl, 2 * h + 1, :].unsqueeze(1).to_broadcast([128, 8, 128]), ALU.mult,
                                reads=[("EE", tl)], writes=[("Ew", 0)])
                        rd = [("Ew", 0)]
                    else:
                        si = h - 3
                        e = Ew[si]
                        rd = [("Ew", si, a) for a in range(8)]
                    self.stt("dve", Wh[wb][:, h, :, :], e[:], ec[:, tl, h:h + 1], e[:], ALU.is_ge, ALU.mult,
                             reads=rd + ["ec"], writes=[("Wh", wb, h)])
                if u > 0:
                    emit_tail(u - 1)
                if tl == 0 and c + 1 < 16:
                    nxt = load_chunk(c + 1)
                for a in range(8):
                    bk = 4 + a // 4
                    for h in range(8):
                        self.mm(ps[bk][:, (a % 4) * 128:(a % 4 + 1) * 128], Wh[wb][:, h, a, :], Dg[:, tl, h, :], h == 0, h == 7,
                                reads=[("Wh", wb, h), ("Dg", tl)], writes=[("bank", bk)])
            emit_tail(len(units) - 1)
            for tl in range(2):
                t = g * 2 + tl
                sx = 0
                self.dma(xmt[sx][:], self.xm_scr[t * 128:(t + 1) * 128, :], writes=[("xmt", sx)])
                for hf in range(2):
                    self.tt("dve", xmt[sx][:, hf * 512:(hf + 1) * 512], xmt[sx][:, hf * 512:(hf + 1) * 512], ps[tl * 2 + hf][:, :], ALU.add,
                            reads=[("xmt", sx), ("bank", tl * 2 + hf)], writes=[("xmt", sx)])
                self.dma(self.y[t * 128:(t + 1) * 128, :], xmt[sx][:], reads=[("xmt", sx)], writes=[("y", sx)], key=("y", sx))


def _prep_inputs(inputs):
    x = np.asarray(inputs["x"], np.float32)
    shared = {
        "w_in": np.ascontiguousarray(inputs["w_in"][0], np.float32),
        "ret_w": np.ascontiguousarray(inputs["ret_w_branch"][0], np.float32),
        "moba_w": np.ascontiguousarray(inputs["moba_w_branch"][0], np.float32),
        "w_out": np.ascontiguousarray(inputs["w_out"][0], np.float32),
        "peer_wq": np.ascontiguousarray(inputs["peer_w_q"][0], np.float32),
        "sub_keys": np.ascontiguousarray(np.asarray(inputs["peer_sub_keys"][0], np.float32).reshape(16, 128, 128)),
        "peer_u": np.ascontiguousarray(inputs["peer_u"][0], np.float32),
        "peer_v": np.ascontiguousarray(inputs["peer_v"][0], np.float32),
        "g1": np.ascontiguousarray(inputs["mix_norm_g"][0], np.float32),
        "g2": np.ascontiguousarray(inputs["ffn_norm_g"][0], np.float32),
        "mqg": np.ascontiguousarray(inputs["moba_q_gain"], np.float32).reshape(1, 64),
        "mkg": np.ascontiguousarray(inputs["moba_k_gain"], np.float32).reshape(1, 64),
        "rel_bias": np.ascontiguousarray(inputs["rel_bias"], np.float32),
    }
    consts = [_consts(0), _consts(1)]
    in_maps = []
    for c in range(NCORES):
        b, half = c // 2, c % 2
        if half == 1:
            xw = x[b]
        else:
            xw = np.concatenate([np.zeros((S_OWN, D), np.float32), x[b, :S_OWN]], 0)
        m = dict(shared)
        m["xw"] = np.ascontiguousarray(xw)
        for k, v in consts[half].items():
            if k != "gC":
                m[k] = v
        in_maps.append(m)
    return in_maps


_CACHE = {}


def kernel(**inputs):
    in_maps = _prep_inputs(inputs)
    if "nc" not in _CACHE:
        pb = PB()
        _CACHE["nc"] = pb.build()
        _CACHE["used"] = set(pb.I.keys())
    used = _CACHE["used"]
    in_maps = [{k: v for k, v in m.items() if k in used} for m in in_maps]
    res = run_bass_kernel_spmd(_CACHE["nc"], in_maps, core_ids=list(range(NCORES)))
    out = np.zeros((4, 4096, D), np.float32)
    for c in range(NCORES):
        b, half = c // 2, c % 2
        out[b, half * S_OWN:(half + 1) * S_OWN] = res.results[c]["y"]
    return out
```

```python
import contextlib
import math
import numpy as np
import ml_dtypes
import concourse.bass as bass
import concourse.mybir as mybir
from concourse.bass_utils import run_bass_kernel_spmd

F32 = mybir.dt.float32
BF16 = mybir.dt.bfloat16
ALU = mybir.AluOpType
AF = mybir.ActivationFunctionType
AX = mybir.AxisListType

NCORES = 8
S_OWN = 2048
S_WIN = 4096
D = 1024
EPS = 1e-6
STRIP_W = 4608
EV_LEN = 4736
FUSE_WAIT = True
SAME_ENGINE_RAW_ONLY = True
OFFS = {"rq": 0, "rk": 512, "rv": 1024, "rg": 2048, "mq": 3072, "mk": 3584, "mv": 4096, "ga": 4608, "gb": 5632}


class _Op:
    __slots__ = ("eng", "fn", "dma", "deps", "seq", "need_inc", "inc_val", "dsem", "dval", "waits")


class MK:
    ENGS = ("pe", "dve", "act", "pool", "sp")

    def __init__(self, nc):
        self.nc = nc
        self.ops = []
        self.last_w = {}
        self.readers = {}
        self.dma_tot = {}
        self.eng_n = {e: 0 for e in self.ENGS}
        self.last_comp = {}
        self.last_dma = {}
        self.wdeps = {}

    def _new(self, eng, fn, dma):
        o = _Op()
        o.eng = eng; o.fn = fn; o.dma = dma; o.need_inc = False; o.inc_val = 0
        o.dsem = None; o.dval = 0; o.waits = None; o.deps = {}
        o.seq = self.eng_n[eng]
        self.eng_n[eng] += 1
        return o

    def op(self, eng, fn, reads=(), writes=(), dma=False, sem_key=None, nowaw=False):
        o = self._new(eng, fn, dma)
        idx = len(self.ops)
        deps = o.deps
        for t in reads:
            j = self.last_w.get(t)
            if j is not None:
                deps[j] = "raw"
        for t in writes:
            j = self.last_w.get(t)
            if nowaw and j is not None and self.ops[j].dma:
                new = self.wdeps.get(t, {})
            else:
                new = {}
                if j is not None:
                    new[j] = "waw"
                for r in self.readers.get(t, ()):
                    new[r] = "war"
                self.wdeps[t] = new
            for r, kind in new.items():
                if r not in deps:
                    deps[r] = kind
        if dma:
            k = sem_key if sem_key is not None else (writes[0] if writes else ("dma", idx))
            self.dma_tot[k] = self.dma_tot.get(k, 0) + 16
            o.dsem = k; o.dval = self.dma_tot[k]
            self.last_dma[k] = idx
        else:
            self.last_comp[eng] = idx
        self.ops.append(o)
        for t in reads:
            self.readers.setdefault(t, []).append(idx)
        for t in writes:
            self.last_w[t] = idx
            self.readers[t] = []
        return idx

    def barrier(self):
        deps = {j: "raw" for j in self.last_comp.values()}
        deps.update({j: "raw" for j in self.last_dma.values()})
        for e in self.ENGS:
            o = self._new(e, None, False)
            o.deps = dict(deps)
            self.ops.append(o)
        self.last_w = {}
        self.readers = {}

    def emit(self, final_wait_tokens=()):
        nc = self.nc
        fo = self._new("sp", None, False)
        for t in final_wait_tokens:
            j = self.last_w.get(t)
            if j is not None:
                fo.deps[j] = "raw"
        ops = self.ops + [fo]
        seen_e = {e: {x: -1 for x in self.ENGS} for e in self.ENGS}
        seen_d = {e: {} for e in self.ENGS}
        for o in ops:
            w = []
            E = o.eng
            best_e = {}
            best_d = {}
            for j, kind in o.deps.items():
                p = ops[j]
                if p.dma:
                    if p.dval > best_d.get(p.dsem, 0):
                        best_d[p.dsem] = p.dval
                else:
                    if p.fn is None:
                        continue
                    if p.eng == E and not o.dma and o.fn is not None:
                        if E == "pe" or E == "sp":
                            continue
                        if SAME_ENGINE_RAW_ONLY and kind != "raw":
                            continue
                    if p.seq > best_e.get(p.eng, -1):
                        best_e[p.eng] = p.seq
            for X, s in best_e.items():
                if s > seen_e[E][X]:
                    seen_e[E][X] = s
                    w.append(("e", X, s))
            for k, v in best_d.items():
                if v > seen_d[E].get(k, 0):
                    seen_d[E][k] = v
                    w.append(("d", k, v))
            o.waits = w
        by_eng_seq = {e: {} for e in self.ENGS}
        for o in ops:
            if not o.dma:
                by_eng_seq[o.eng][o.seq] = o
        for o in ops:
            for w in o.waits:
                if w[0] == "e":
                    by_eng_seq[w[1]][w[2]].need_inc = True
        cnt = {e: 0 for e in self.ENGS}
        for o in ops:
            if not o.dma and o.need_inc:
                cnt[o.eng] += 1
                o.inc_val = cnt[o.eng]
        self.inc_counts = dict(cnt)
        es = contextlib.ExitStack()
        esem = {e: es.enter_context(nc.semaphore("es_" + e)) for e in self.ENGS}
        dsem = {}
        for k in self.dma_tot:
            dsem[k] = es.enter_context(nc.semaphore("ds_%d" % len(dsem)))
        self.n_dsem = len(dsem)
        prog = {e: [] for e in self.ENGS}
        for o in ops:
            prog[o.eng].append(o)

        def run(engname, eng):
            for o in prog[engname]:
                ws = list(o.waits)
                fused = None
                if o.fn is not None and ws and FUSE_WAIT:
                    fused = ws.pop()
                for w in ws:
                    if w[0] == "e":
                        eng.wait_ge(esem[w[1]], by_eng_seq[w[1]][w[2]].inc_val)
                    else:
                        eng.wait_ge(dsem[w[1]], w[2])
                if o.fn is None:
                    continue
                ins = o.fn(eng)
                if fused is not None:
                    if fused[0] == "e":
                        ins._wait_ge(esem[fused[1]], by_eng_seq[fused[1]][fused[2]].inc_val)
                    else:
                        ins._wait_ge(dsem[fused[1]], fused[2])
                if o.dma:
                    ins.then_inc(dsem[o.dsem], 16)
                elif o.need_inc:
                    ins.then_inc(esem[o.eng], 1)

        with es:
            with nc.Block() as block:
                @block.tensor
                def _(e):
                    run("pe", e)

                @block.vector
                def _(e):
                    run("dve", e)

                @block.scalar
                def _(e):
                    run("act", e)

                @block.gpsimd
                def _(e):
                    run("pool", e)

                @block.sync
                def _(e):
                    run("sp", e)


def _rel_bucket_np(dist):
    n = np.maximum(dist, 0)
    nf = np.maximum(n, 1).astype(np.float32)
    large = 16 + (np.log(nf / np.float32(16)) / np.float32(math.log(2048 / 16)) * np.float32(16)).astype(np.int32)
    large = np.minimum(large, 31)
    return np.where(n < 16, n, large)


def _consts(half):
    c = {}
    bf = ml_dtypes.bfloat16
    c["ident_bf"] = np.eye(128, dtype=np.float32).astype(bf)
    c["J_bf"] = np.eye(128, dtype=np.float32)[::-1].copy().astype(bf)
    c["identf"] = np.eye(128, dtype=np.float32)
    w = np.arange(S_WIN)
    pos = (w - S_OWN + S_OWN * half).astype(np.float32)
    inv = (np.float32(10000.0) ** (-np.arange(32, dtype=np.float32) / np.float32(32))).astype(np.float32)
    ang = pos[None, :] * inv[:, None]
    cs, sn = np.cos(ang).astype(np.float32), np.sin(ang).astype(np.float32)
    c["cosT"] = np.concatenate([cs, cs], 0)
    c["sinT"] = np.concatenate([-sn, sn], 0)
    c["csT"] = np.concatenate([c["cosT"], c["sinT"]], 0)
    fold = np.zeros((128, 128), np.float32)
    for p in range(128):
        fold[p, p % 64] = 1.0
        fold[p, p % 64 + 64] = 1.0
    c["fold"] = fold.astype(bf)
    hh = np.arange(8, dtype=np.float32)
    log_g = np.log1p(-np.exp2(-5.0 - hh)).astype(np.float32)
    i = np.arange(128, dtype=np.float32)
    rel = i[None, :] - i[:, None]
    DT = np.where(rel[None] >= 0, np.exp(np.maximum(rel, 0)[None] * log_g[:, None, None]), 0.0) * 0.125
    c["DT"] = np.ascontiguousarray(DT.transpose(1, 0, 2)).astype(np.float32)
    QD = np.exp((i + 1.0)[None, :] * log_g[:, None])
    c["QD"] = np.ascontiguousarray(np.broadcast_to(QD[None], (128, 8, 128))).astype(np.float32)
    kd = np.exp((127.0 - i)[:, None] * log_g[None, :]) * 0.125
    c["kdec"] = kd.astype(np.float32)
    c["gC"] = [float(np.exp(128.0 * lg)) for lg in log_g]
    idx = np.arange(EV_LEN)
    dist = idx - 511
    bk = _rel_bucket_np(dist)
    OH = np.zeros((33, EV_LEN), np.float32)
    valid = dist >= 0
    OH[bk[valid], idx[valid]] = 1.0
    OH[32, ~valid] = 1.0
    c["OH33"] = OH
    c["negrow"] = np.full((1, 8), -30000.0, np.float32)
    cp = np.zeros((128, 16), np.float32)
    if half == 0:
        cp[:, :8] = -1e30
    c["ctxpen"] = cp
    bo = np.zeros((16, S_WIN), np.float32)
    for n in range(16):
        bo[n, n * 256:(n + 1) * 256] = 1.0
    c["blkoh"] = bo.astype(bf)
    return c


class PB:
    def __init__(self, dbg=None):
        self.dbg = dbg or ()
        self.nc = bass.Bass("TRN2", target_bir_lowering=False)
        self.mk = MK(self.nc)
        self.dq = 0

    def din(self, name, shape, dt=F32):
        return self.nc.dram_tensor(name, list(shape), dt, kind="ExternalInput").ap()

    def dout(self, name, shape, dt=F32):
        return self.nc.dram_tensor(name, list(shape), dt, kind="ExternalOutput").ap()

    def dscr(self, name, shape, dt=F32):
        return self.nc.dram_tensor(name, list(shape), dt, kind="Internal").ap()

    def dma(self, out, in_, reads=(), writes=(), eng=None, key=None, nowaw=False, slow=False):
        if eng is None:
            eng = "sp"
        if slow:
            return self.mk.op(eng, lambda e: e.dma_start(out=out, in_=in_, allow_slow_non_contiguous=True), reads=reads,
                              writes=writes, dma=True, sem_key=key, nowaw=nowaw)
        return self.mk.op(eng, lambda e: e.dma_start(out=out, in_=in_), reads=reads, writes=writes,
                          dma=True, sem_key=key, nowaw=nowaw)

    def mm(self, out, lhsT, rhs, start, stop, reads=(), writes=()):
        return self.mk.op("pe", lambda e: e.matmul(out, lhsT=lhsT, rhs=rhs, start=start, stop=stop),
                          reads=reads, writes=writes)

    def tr(self, out, in_, ident, reads=(), writes=()):
        return self.mk.op("pe", lambda e: e.transpose(out=out, in_=in_, identity=ident), reads=reads, writes=writes)

    def act(self, out, in_, func, reads=(), writes=(), **kw):
        return self.mk.op("act", lambda e: e.activation(out=out, in_=in_, func=func, **kw), reads=reads, writes=writes)

    def tt(self, eng, out, in0, in1, op, reads=(), writes=()):
        return self.mk.op(eng, lambda e: e.tensor_tensor(out=out, in0=in0, in1=in1, op=op), reads=reads, writes=writes)

    def ts(self, eng, out, in0, s1, s2, op0, op1=None, reads=(), writes=()):
        if op1 is None:
            return self.mk.op(eng, lambda e: e.tensor_scalar(out=out, in0=in0, scalar1=s1, scalar2=s2, op0=op0),
                              reads=reads, writes=writes)
        return self.mk.op(eng, lambda e: e.tensor_scalar(out=out, in0=in0, scalar1=s1, scalar2=s2, op0=op0, op1=op1),
                          reads=reads, writes=writes)

    def stt(self, eng, out, in0, scalar, in1, op0, op1, reads=(), writes=()):
        return self.mk.op(eng, lambda e: e.scalar_tensor_tensor(out=out, in0=in0, scalar=scalar, in1=in1, op0=op0, op1=op1),
                          reads=reads, writes=writes)

    def cp(self, eng, out, in_, reads=(), writes=()):
        if eng == "act":
            return self.mk.op("act", lambda e: e.copy(out=out, in_=in_), reads=reads, writes=writes)
        return self.mk.op(eng, lambda e: e.tensor_copy(out=out, in_=in_), reads=reads, writes=writes)

    def memset(self, eng, ap, val, writes=()):
        return self.mk.op(eng, lambda e: e.memset(ap, val), writes=writes)

    def recip(self, out, in_, reads=(), writes=()):
        return self.mk.op("dve", lambda e: e.reciprocal(out=out, in_=in_), reads=reads, writes=writes)

    def build(self):
        nc, mk = self.nc, self.mk
        shapes = {"xw": ([S_WIN, D], F32), "w_in": ([D, 6656], F32), "ret_w": ([1024, D], F32), "moba_w": ([512, D], F32),
                  "w_out": ([D, D], F32), "peer_wq": ([D, 2048], F32), "sub_keys": ([16, 128, 128], F32),
                  "peer_u": ([16384, D], F32), "peer_v": ([16384, D], F32), "g1": ([D], F32), "g2": ([D], F32),
                  "mqg": ([1, 64], F32), "mkg": ([1, 64], F32), "rel_bias": ([32, 8], F32),
                  "ident_bf": ([128, 128], BF16), "J_bf": ([128, 128], BF16), "identf": ([128, 128], F32),
                  "cosT": ([64, S_WIN], F32), "sinT": ([64, S_WIN], F32), "DT": ([128, 8, 128], F32),
                  "QD": ([128, 8, 128], F32), "csT": ([128, S_WIN], F32), "fold": ([128, 128], BF16), "kdec": ([128, 8], F32), "OH33": ([33, EV_LEN], F32),
                  "negrow": ([1, 8], F32), "ctxpen": ([128, 16], F32), "blkoh": ([16, S_WIN], BF16)}
        pb = self

        class Lazy(dict):
            def __missing__(self, k):
                sh, dt = shapes[k]
                v = pb.din(k, sh, dt)
                self[k] = v
                return v
        I = Lazy()
        self.I = I
        self.y = self.dout("y", [S_OWN, D])
        self.dbg_out = {}
        self.gC = _consts(0)["gC"]

        with contextlib.ExitStack() as es0:
            self.es0 = es0
            sb0 = lambda n, s, d=F32: es0.enter_context(nc.sbuf_tensor(n, list(s), d))
            self.ps = [es0.enter_context(nc.psum_tensor("ps%d" % i, [128, 512], F32)) for i in range(8)]
            self.ident = sb0("ident", [128, 128], BF16)
            self.Jm = sb0("Jm", [128, 128], BF16)
            self.g1col = sb0("g1col", [128, 8])
            self.g2col = sb0("g2col", [128, 8])
            self.hTo = sb0("hTo", [128, 8, S_OWN], BF16)
            self.dma(self.ident[:], I["ident_bf"][:, :], writes=["ident"])
            self.dma(self.Jm[:], I["J_bf"][:, :], writes=["Jm"])
            self.dma(self.g1col[:], I["g1"].rearrange("(k p) -> p k", p=128), writes=["g1col"], slow=True)
            self.dma(self.g2col[:], I["g2"].rearrange("(k p) -> p k", p=128), writes=["g2col"], slow=True)
            self.do_peer = not any(k in self.dbg for k in ("stop0", "stopA1", "stopA2", "stopB"))
            with contextlib.ExitStack() as esAB:
                sbAB = lambda n, s, d=F32: esAB.enter_context(nc.sbuf_tensor(n, list(s), d))
                self.retT = sbAB("retT", [128, 8, S_OWN], BF16)
                self.attT = sbAB("attT", [64, 8, S_OWN], BF16)
                with contextlib.ExitStack() as esA:
                    sbA = lambda n, s, d=F32: esA.enter_context(nc.sbuf_tensor(n, list(s), d))
                    self.hTc = sbA("hTc", [128, 8, S_OWN], BF16)
                    with contextlib.ExitStack() as es:
                        self.phase0(lambda n, s, d=F32: es.enter_context(nc.sbuf_tensor(n, list(s), d)))
                    mk.barrier()
                    if "hT" in self.dbg:
                        d = self.dout("d_hTo", [128, 8, S_OWN], BF16)
                        self.dma(d[:, :, :], self.hTo[:], writes=["d_hTo"])
                        self.dbg_out["d_hTo"] = "d_hTo"
                        d = self.dout("d_hTc", [128, 8, S_OWN], BF16)
                        self.dma(d[:, :, :], self.hTc[:], writes=["d_hTc"])
                        self.dbg_out["d_hTc"] = "d_hTc"
                    if "stop0" not in self.dbg and "skipA1" not in self.dbg:
                        with contextlib.ExitStack() as es:
                            self.phaseA1(lambda n, s, d=F32: es.enter_context(nc.sbuf_tensor(n, list(s), d)))
                        mk.barrier()
                    if "stop0" not in self.dbg and "stopA1" not in self.dbg:
                        with contextlib.ExitStack() as es:
                            self.phaseAE(lambda n, s, d=F32: es.enter_context(nc.sbuf_tensor(n, list(s), d)))
                        mk.barrier()
                        with contextlib.ExitStack() as es:
                            self.phaseA2(lambda n, s, d=F32: es.enter_context(nc.sbuf_tensor(n, list(s), d)))
                        mk.barrier()
                if "attT" in self.dbg:
                    d = self.dout("d_attT", [64, 8, S_OWN], BF16)
                    self.dma(d[:, :, :], self.attT[:], writes=["d_attT"])
                    self.dbg_out["d_attT"] = "d_attT"
                self.xm_scr = self.dscr("xm_scr", [S_OWN, D])
                if "stop0" not in self.dbg and "stopA1" not in self.dbg and "stopA2" not in self.dbg:
                    with contextlib.ExitStack() as es:
                        self.phaseB(lambda n, s, d=F32: es.enter_context(nc.sbuf_tensor(n, list(s), d)))
                    mk.barrier()
            if "xm" in self.dbg:
                d = self.dout("d_xm", [S_OWN, D])
                self.dma(d[:, :], self.xm_scr[:, :], writes=["d_xm"])
                self.dbg_out["d_xm"] = "d_xm"
                if "retT" in self.dbg:
                    d = self.dout("d_retT", [128, 8, S_OWN], BF16)
                    self.dma(d[:, :, :], self.retT[:], reads=[("retT", h, c) for h in range(8) for c in range(16)], writes=["d_retT"])
                    self.dbg_out["d_retT"] = "d_retT"
            if self.do_peer:
                with contextlib.ExitStack() as es:
                    self.phaseC(lambda n, s, d=F32: es.enter_context(nc.sbuf_tensor(n, list(s), d)))
            else:
                self.dma(self.y[:, :], self.xm_scr[:, :], writes=["y"])
            mk.emit(final_wait_tokens=list(self.dbg_out.values()) + ["y", ("y", 0), ("y", 1)])
        return nc

    def hT(self, kc, col0, n):
        if col0 >= S_OWN:
            return self.hTo[:, kc, col0 - S_OWN:col0 - S_OWN + n]
        assert col0 + n <= S_OWN
        return self.hTc[:, kc, col0:col0 + n]

    def hT_tok(self, col0):
        t = col0 // 128
        return ("hT", t)

    def phase0(self, sb):
        I = self.I
        xt = [sb("xt%d" % i, [128, D]) for i in range(2)]
        xn = [sb("xn%d" % i, [128, D], BF16) for i in range(2)]
        junk = sb("junk0", [128, D], BF16)
        ss = sb("ss0", [128, 32])
        rs = sb("rs0", [128, 32])
        tpb = [self.ps[4][:, :].bitcast(BF16), self.ps[7][:, :].bitcast(BF16)]
        def evac(t):
            s = t % 2
            tp = tpb[s].rearrange("p (k c) -> p k c", k=8)
            dst = (self.hTc if t < 16 else self.hTo)[:, :, (t % 16) * 128:(t % 16 + 1) * 128]
            self.tt("dve", dst, tp, self.g1col[:].unsqueeze(2).to_broadcast([128, 8, 128]), ALU.mult,
                    reads=[("bank", (4, 7)[s]), "g1col"], writes=[("hT", t)])

        for t in range(32):
            s = t % 2
            self.dma(xt[s][:], I["xw"][t * 128:(t + 1) * 128, :], writes=[("xt", s)])
            self.act(junk[:], xt[s][:], AF.Square, reads=[("xt", s)], writes=["junk0", ("ss", t)], accum_out=ss[:, t:t + 1])
            self.act(rs[:, t:t + 1], ss[:, t:t + 1], AF.Sqrt, reads=[("ss", t)], writes=[("rs", t)], scale=1.0 / D, bias=EPS)
            self.recip(rs[:, t:t + 1], rs[:, t:t + 1], reads=[("rs", t)], writes=[("rs", t)])
            self.ts("dve", xn[s][:], xt[s][:], rs[:, t:t + 1], None, ALU.mult, reads=[("xt", s), ("rs", t)], writes=[("xn", s)])
            tp = tpb[s].rearrange("p (k c) -> p k c", k=8)
            for kc in range(8):
                self.tr(tp[:, kc, :], xn[s][:, kc * 128:(kc + 1) * 128], self.ident[:],
                        reads=[("xn", s), "ident"], writes=[("bank", (4, 7)[s])])
            if t >= 1:
                evac(t - 1)
        evac(31)

    def phaseA1(self, sb):
        I, ps = self.I, self.ps
        stage = sb("stageA", [128, 8, 512])
        wbf = sb("wbfA", [128, 8, 512], BF16)
        qT = sb("qT", [128, S_OWN], BF16)
        qdT = sb("qdT", [128, S_OWN], BF16)
        kT = sb("kT", [128, S_WIN], BF16)
        k_tm = sb("k_tm", [128, 32, 128], BF16)
        foldm = sb("foldm", [128, 128], BF16)
        v_tm = sb("v_tm", [128, 32, 128], BF16)
        sg_tm = sb("sg_tm", [128, 16, 128], BF16)
        cs = [sb("cs%d" % i, [128, 512]) for i in range(2)]
        t1 = [sb("t1_%d" % i, [128, 512]) for i in range(2)]
        ub = [sb("ub%d" % i, [128, 512], BF16) for i in range(2)]
        DTs = sb("DTs", [128, 128]); QDs = sb("QDs", [128, 128]); kdec = sb("kdec_s", [128, 8])
        state = sb("state", [128, 128]); state_bf = sb("state_bf", [128, 128], BF16)
        PT = [sb("PT%d" % i, [128, 128], BF16) for i in range(2)]
        st6 = [sb("st6_%d" % i, [128, 6]) for i in range(3)]; mv = [sb("mv%d" % i, [128, 2]) for i in range(3)]
        rstd = [sb("rstdA%d" % i, [128, 1]) for i in range(3)]; o_sb = [sb("o_sb%d" % i, [128, 128]) for i in range(3)]
        yn = [sb("yn%d" % i, [128, 128]) for i in range(2)]; A_tm = [sb("A_tm%d" % i, [128, 128], BF16) for i in range(2)]
        self.dma(kdec[:], I["kdec"][:, :], writes=["kdec"])
        self.dma(foldm[:], I["fold"][:, :], writes=["foldm"])
        w3 = I["w_in"].rearrange("(k p) c -> p k c", p=128)
        csn = [0]

        def fm_proj(wc0, dstT, col0, ntile, is_q, h):
            for T in range(ntile):
                c0 = col0 + T * 512
                s = csn[0] % 2
                csn[0] += 1
                self.dma(cs[s][:], I["csT"][:, c0:c0 + 512], writes=[("cs", s)])
                for kc in range(8):
                    self.mm(ps[s][:, :], wbf[:, kc, wc0:wc0 + 128], self.hT(kc, c0, 512), kc == 0, kc == 7,
                            reads=["wbfA"] + [("hT", c0 // 128 + i) for i in range(4)], writes=[("bank", s)])
                dl = c0 - col0
                if is_q:
                    self.tt("dve", t1[s][:], ps[s][:, :], cs[s][:], ALU.mult, reads=[("bank", s), ("cs", s)], writes=[("t1", s)])
                    self.cp("act", dstT[:, dl:dl + 512], t1[s][:], reads=[("t1", s)], writes=[("qT", T)])
                    self.tt("pool", qdT[:, dl:dl + 512].rearrange("p (c i) -> p c i", c=4), t1[s][:].rearrange("p (c i) -> p c i", c=4),
                            QDs[:].unsqueeze(1).to_broadcast([128, 4, 128]), ALU.mult,
                            reads=[("t1", s), "QDs"], writes=[("qdT", T)])
                else:
                    self.tt("dve", ub[s][:], ps[s][:, :], cs[s][:], ALU.mult, reads=[("bank", s), ("cs", s)], writes=[("ub", s)])
                    self.mm(ps[2 + s][:, :], foldm[:], ub[s][:], True, True, reads=["foldm", ("ub", s)], writes=[("bank", 2 + s)])
                    self.cp("act", dstT[:, dl:dl + 512], ps[2 + s][:, :], reads=[("bank", 2 + s)], writes=[("kT", T)])

        for h in range(8):
            self.dma(DTs[:], I["DT"][:, h, :], writes=["DTs"])
            self.dma(QDs[:], I["QD"][:, h, :], writes=["QDs"])
            q0, k0 = OFFS["rq"] + h * 64, OFFS["rk"] + h * 64
            v0, g0 = OFFS["rv"] + h * 128, OFFS["rg"] + h * 128
            segs = [(0, q0, 64), (64, q0 + 32, 32), (96, q0, 32), (128, k0, 64), (192, k0 + 32, 32), (224, k0, 32),
                    (256, v0, 128), (384, g0, 128)]
            for (d0, s0, n) in segs:
                self.dma(stage[:, :, d0:d0 + n], w3[:, :, s0:s0 + n], writes=["stageA"], nowaw=True)
            self.cp("dve", wbf[:, 0:4, :], stage[:, 0:4, :], reads=["stageA"], writes=["wbfA"])
            self.cp("act", wbf[:, 4:8, :], stage[:, 4:8, :], reads=["stageA"], writes=["wbfA"])
            fm_proj(0, qT, S_OWN, 4, True, h)
            fm_proj(128, kT, 0, 8, False, h)
            for b in range(4):
                tpk = ps[4][:, :].bitcast(BF16).rearrange("p (t d) -> p t d", t=8)
                for tt_ in range(8):
                    t = b * 8 + tt_
                    self.tr(tpk[:, tt_, :], kT[:, t * 128:(t + 1) * 128], self.ident[:],
                            reads=[("kT", t // 4), "ident"], writes=[("bank", 4)])
                self.ts("dve", k_tm[:, b * 8:(b + 1) * 8, :], tpk, kdec[:, h:h + 1], None, ALU.mult,
                        reads=[("bank", 4), "kdec"], writes=[("k_tm", b // 2)])
            for t in range(32):
                bkv = (2, 3, 5, 6)[t % 4]
                own = t >= 16
                n = 256 if own else 128
                pv = ps[bkv][:, 0:n]
                for kc in range(8):
                    self.mm(pv, self.hT(kc, t * 128, 128), wbf[:, kc, 256:256 + n], kc == 0, kc == 7,
                            reads=["wbfA", ("hT", t)], writes=[("bank", bkv)])
                self.cp("act", v_tm[:, t, :], pv[:, 0:128], reads=[("bank", bkv)], writes=[("v_tm", t)])
                if own:
                    self.act(sg_tm[:, t - 16, :], pv[:, 128:256], AF.Silu, reads=[("bank", bkv)], writes=[("sg", t - 16)])
            self.memset("pool", state[:], 0.0, writes=["state"])
            self.memset("pool", state_bf[:], 0.0, writes=["state_bf"])
            gC = self.gC[h]

            def issue_front(n):
                sl = n % 2
                self.mm(ps[5 + sl][:, 0:128], k_tm[:, n, :], v_tm[:, n, :], True, True,
                        reads=[("k_tm", n // 16), ("v_tm", n)], writes=[("bank", 5 + sl)])
                if n >= 16:
                    c = n - 16
                    self.mm(ps[sl][:, 0:128], kT[:, n * 128:(n + 1) * 128], qT[:, c * 128:(c + 1) * 128], True, True,
                            reads=[("kT", n // 4), ("qT", c // 4)], writes=[("bank", sl)])

            issue_front(0)

            def post_stats(n):
                k = n % 3
                self.mk.op("dve", lambda e, k=k: e.bn_stats(out=st6[k][:], in_=o_sb[k][:]), reads=[("o_sb", k)], writes=[("st6", k)])
                self.mk.op("dve", lambda e, k=k: e.bn_aggr(out=mv[k][:], in_=st6[k][:]), reads=[("st6", k)], writes=[("mv", k)])
                self.act(rstd[k][:], mv[k][:, 1:2], AF.Sqrt, reads=[("mv", k)], writes=[("rstdA", k)], bias=EPS, scale=1.0)

            def post_norm(n):
                k = n % 3
                c = n - 16
                sl = n % 2
                self.recip(rstd[k][:], rstd[k][:], reads=[("rstdA", k)], writes=[("rstdA", k)])
                self.ts("dve", yn[sl][:], o_sb[k][:], mv[k][:, 0:1], rstd[k][:, 0:1], ALU.subtract, ALU.mult,
                        reads=[("o_sb", k), ("mv", k), ("rstdA", k)], writes=[("yn", sl)])
                self.tt("dve", A_tm[sl][:], yn[sl][:], sg_tm[:, c, :], ALU.mult, reads=[("yn", sl), ("sg", c)], writes=[("A_tm", sl)])
                atr = ps[(7, 4)[sl]][:, :].bitcast(BF16)[:, 0:128]
                self.tr(atr, A_tm[sl][:], self.ident[:], reads=[("A_tm", sl), "ident"], writes=[("bank", (7, 4)[sl])])
                self.cp("act", self.retT[:, h, c * 128:(c + 1) * 128], atr, reads=[("bank", (7, 4)[sl])], writes=[("retT", h, c)])

            for n in range(32):
                sl = n % 2
                if n + 1 < 32:
                    issue_front(n + 1)
                if n >= 16:
                    c = n - 16
                    self.tt("dve", PT[sl][:], ps[sl][:, 0:128], DTs[:], ALU.mult,
                            reads=[("bank", sl), "DTs"], writes=[("PT", sl)])
                    o_ps = ps[2 + sl][:, 0:128]
                    self.mm(o_ps, PT[sl][:], v_tm[:, n, :], True, False, reads=[("PT", sl), ("v_tm", n)], writes=[("bank", 2 + sl)])
                    self.mm(o_ps, qdT[:, c * 128:(c + 1) * 128], state_bf[:], False, True,
                            reads=[("qdT", c // 4), "state_bf"], writes=[("bank", 2 + sl)])
                    self.cp("act", o_sb[n % 3][:], o_ps, reads=[("bank", 2 + sl)], writes=[("o_sb", n % 3)])
                self.stt("dve", state[:], state[:], gC, ps[5 + sl][:, 0:128], ALU.mult, ALU.add,
                         reads=["state", ("bank", 5 + sl)], writes=["state"])
                if n >= 15:
                    self.cp("act", state_bf[:], state[:], reads=["state"], writes=["state_bf"])
                if n - 2 >= 16:
                    post_norm(n - 2)
                if n - 1 >= 16:
                    post_stats(n - 1)
            post_norm(30)
            post_stats(31)
            post_norm(31)


    def phaseAE(self, sb):
        I, ps = self.I, self.ps
        self.Escr = self.dscr("Escr", [8, EV_LEN], BF16)
        l33 = sb("l33", [33, 8])
        ohs = [sb("ohs%d" % i, [33, 512]) for i in range(2)]
        Ev = sb("Ev", [8, EV_LEN], BF16)
        self.dma(l33[0:32, :], I["rel_bias"][:, :], writes=["l33"])
        self.dma(l33[32:33, :], I["negrow"][:, :], writes=["l33"], nowaw=True)
        for j in range(10):
            c0 = j * 512
            n = min(512, EV_LEN - c0)
            s = j % 2
            self.dma(ohs[s][:, 0:n], I["OH33"][:, c0:c0 + n], writes=[("ohs", s)])
            self.mm(ps[s][0:8, 0:n], l33[:, :], ohs[s][:, 0:n], True, True, reads=["l33", ("ohs", s)], writes=[("bank", s)])
            self.act(Ev[:, c0:c0 + n], ps[s][0:8, 0:n], AF.Exp, reads=[("bank", s)], writes=["Ev"])
        self.dma(self.Escr[:, :], Ev[:], reads=["Ev"], writes=["Escr"])

    def phaseA2(self, sb):
        I, ps, mk = self.I, self.ps, self.mk
        stage = sb("stageM", [128, 8, 192])
        wbf = sb("wbfM", [128, 8, 192], BF16)
        q_tm = sb("q_tm", [128, 16, 64], BF16)
        k_tm = sb("k_tm2", [128, 32, 64], BF16)
        v_aug = sb("v_aug", [128, 32, 65], BF16)
        qTa = sb("qTa", [80, S_OWN], BF16)
        kTa = sb("kTa", [80, S_WIN], BF16)
        kmT = sb("kmT", [64, 16]); kmTb = sb("kmTb", [64, 16], BF16)
        gm = sb("gm", [128, 16, 16]); m8 = sb("m8", [128, 8]); thr = sb("thr", [128, 1])
        nm_tm = sb("nm_tm", [128, 16, 16], BF16); nmT = sb("nmT", [16, S_OWN], BF16)
        Mp = sb("Mp", [128, STRIP_W], BF16); Ms = sb("Ms", [128, STRIP_W], BF16)
        expS = [sb("expS%d" % i, [128, 512], BF16) for i in range(3)]
        PTm = [sb("PTm%d" % i, [128, 512], BF16) for i in range(3)]
        rec = sb("rec", [128, 4]); att_tm = sb("att_tm", [128, 4, 64], BF16)
        ssqk = sb("ssqk", [128, 8]); rr = sb("rr", [128, 8]); junk = sb("junkM", [128, 64], BF16)
        gq8 = sb("gq8", [128, 64]); gk = sb("gk", [128, 64]); ctxp = sb("ctxp", [128, 16])
        self.dma(gq8[:], I["mqg"][0:1, :].partition_broadcast(128), writes=["gq8"])
        self.dma(gk[:], I["mkg"][0:1, :].partition_broadcast(128), writes=["gk"])
        self.dma(ctxp[:], I["ctxpen"][:, :], writes=["ctxp"])
        self.ts("dve", gq8[:], gq8[:], 0.125, None, ALU.mult, reads=["gq8"], writes=["gq8"])
        self.dma(kTa[64:80, :], I["blkoh"][:, :], writes=["kTa_oh"])
        self.memset("pool", v_aug[:, :, 64:65], 1.0, writes=["v_ones"])
        self.memset("pool", ssqk[:], 1.0, writes=[("ssqk", i) for i in range(4)])
        w3 = I["w_in"].rearrange("(k p) c -> p k c", p=128)
        tpb = ps[2][:, :].bitcast(BF16)
        B2 = ("bank", 2)
        sct = [0]
        self.UT_scr = self.dscr("UT_scr", [128, 128, 8, 128], BF16)
        self.V_scr = self.dscr("V_scr", [16384, D], BF16)
        if self.do_peer:
            ust = sb("ustP", [128, D]); ubf = sb("ubfP", [128, D], BF16); utb = sb("utbP", [128, 8, 128], BF16)
            vst = sb("vstP", [128, D]); vbf = sb("vbfP", [128, D], BF16)
        pstep = [0]

        tp0 = ps[0][:, :].bitcast(BF16).rearrange("p (k c) -> p k c", k=8)
        B0 = ("bank", 0)

        def P_T():
            a = pstep[0] - 2
            if self.do_peer and 0 <= a < 128:
                for kc in range(8):
                    self.tr(tp0[:, kc, :], ubf[:, kc * 128:(kc + 1) * 128], self.ident[:], reads=["ubfP", "ident"], writes=[B0])
                self.dma(self.V_scr[a * 128:(a + 1) * 128, :], vbf[:], reads=["vbfP"], writes=["V_scrP"], key="V_scrP")

        def P_C():
            g = pstep[0]
            pstep[0] += 1
            if not self.do_peer:
                return
            a = g - 2
            if 0 <= a < 128:
                self.cp("dve", utb[:], tp0, reads=[B0], writes=["utbP"])
                self.dma(self.UT_scr[a, :, :, :], utb[:], reads=["utbP"], writes=["UT_scrP"], key="UT_scrP")
            a = g - 1
            if 0 <= a < 128:
                self.cp("act", ubf[:], ust[:], reads=["ustP"], writes=["ubfP"])
                self.cp("dve", vbf[:], vst[:], reads=["vstP"], writes=["vbfP"])
            a = g
            if 0 <= a < 128:
                self.dma(ust[:], I["peer_u"][a * 128:(a + 1) * 128, :], writes=["ustP"])
                self.dma(vst[:], I["peer_v"][a * 128:(a + 1) * 128, :], writes=["vstP"])

        self.dma(Mp[:], bass.AP(self.Escr.tensor, 0, [[1, 128], [1, STRIP_W]]), reads=["Escr"], writes=["Mp"])
        for h in range(8):
            for i, nm_ in enumerate(("mq", "mk", "mv")):
                self.dma(stage[:, :, i * 64:(i + 1) * 64], w3[:, :, OFFS[nm_] + h * 64:OFFS[nm_] + (h + 1) * 64],
                         writes=["stageM"], nowaw=True)
            self.cp("dve", wbf[:, :, :], stage[:, :, :], reads=["stageM"], writes=["wbfM"])
            for j in range(9):
                bk = (6, 7)[j % 2]
                self.mm(ps[bk][:, :], self.Jm[:], Mp[:, j * 512:(j + 1) * 512], True, True, reads=["Jm", "Mp"], writes=[("bank", bk)])
                self.cp("act" if j % 2 else "dve", Ms[:, j * 512:(j + 1) * 512], ps[bk][:, :], reads=[("bank", bk)], writes=["Ms"])
            if h + 1 < 8:
                self.dma(Mp[:], bass.AP(self.Escr.tensor, (h + 1) * EV_LEN, [[1, 128], [1, STRIP_W]]), reads=["Escr"], writes=["Mp"])
            for t in range(32):
                own = t >= 16
                c0 = 0 if own else 64
                n = 192 - c0
                bk = (0, 1, 4, 5)[t % 4]
                q4 = t % 4
                pp = ps[bk][:, 0:n]
                for kc in range(8):
                    self.mm(pp, self.hT(kc, t * 128, 128), wbf[:, kc, c0:192], kc == 0, kc == 7,
                            reads=["wbfM", ("hT", t)], writes=[("bank", bk)])
                kcol = 64 - c0
                sq = ssqk[:, 2 * q4:2 * q4 + 2]
                rq = rr[:, 2 * q4:2 * q4 + 2]
                self.act(junk[:], pp[:, kcol:kcol + 64], AF.Square, reads=[("bank", bk)], writes=["junkM", ("ssqk", q4)], accum_out=sq[:, 0:1])
                if own:
                    self.act(junk[:], pp[:, 0:64], AF.Square, reads=[("bank", bk)], writes=["junkM", ("ssqk", q4)], accum_out=sq[:, 1:2])
                self.act(rq, sq, AF.Sqrt, reads=[("ssqk", q4)], writes=[("rr", q4)], scale=1.0 / 64, bias=EPS)
                self.recip(rq, rq, reads=[("rr", q4)], writes=[("rr", q4)])
                self.stt("dve", k_tm[:, t, :], pp[:, kcol:kcol + 64], rq[:, 0:1], gk[:], ALU.mult, ALU.mult,
                         reads=[("bank", bk), ("rr", q4), "gk"], writes=[("k_tm2", t // 8)])
                if own:
                    self.stt("dve", q_tm[:, t - 16, :], pp[:, 0:64], rq[:, 1:2], gq8[:], ALU.mult, ALU.mult,
                             reads=[("bank", bk), ("rr", q4), "gq8"], writes=[("q_tm", (t - 16) // 8)])
                self.cp("act", v_aug[:, t, 0:64], pp[:, kcol + 64:kcol + 128], reads=[("bank", bk)], writes=[("v_aug", t)])
            for b in range(4):
                for i in range(8):
                    self.tr(tpb[0:64, i * 128:(i + 1) * 128], k_tm[:, b * 8 + i, :], self.ident[:],
                            reads=[("k_tm2", b), "ident"], writes=[B2])
                self.cp("act", kTa[0:64, b * 1024:(b + 1) * 1024], tpb[0:64, :], reads=[B2], writes=[("kTa", b)])
                mk.op("dve", lambda e, b=b: e.tensor_reduce(out=kmT[:, b * 4:(b + 1) * 4],
                                                            in_=kTa[0:64, b * 1024:(b + 1) * 1024].rearrange("p (n l) -> p n l", l=256),
                                                            axis=AX.X, op=ALU.add),
                      reads=[("kTa", b)], writes=["kmT"])
            for b in range(2):
                for i in range(8):
                    self.tr(tpb[0:64, i * 128:(i + 1) * 128], q_tm[:, b * 8 + i, :], self.ident[:],
                            reads=[("q_tm", b), "ident"], writes=[B2])
                self.cp("act", qTa[0:64, b * 1024:(b + 1) * 1024], tpb[0:64, :], reads=[B2], writes=[("qTa", b)])
            self.ts("dve", kmTb[:], kmT[:], 1.0 / 256, None, ALU.mult, reads=["kmT"], writes=["kmTb"])
            for c in range(16):
                self.mm(ps[3][:, c * 16:(c + 1) * 16], qTa[0:64, c * 128:(c + 1) * 128], kmTb[:, :], True, True,
                        reads=[("qTa", c // 8), "kmTb"], writes=[("bank", 3)])
            self.tt("dve", gm[:], ps[3][:, 0:256].rearrange("p (c n) -> p c n", n=16),
                    ctxp[:].unsqueeze(1).to_broadcast([128, 16, 16]), ALU.add, reads=[("bank", 3), "ctxp"], writes=["gm"])
            self.memset("pool", nm_tm[:], 0.0, writes=["nm_tm"])
            for c in range(16):
                Bk = 8 + c // 2
                mk.op("dve", lambda e, c=c, Bk=Bk: e.max(out=m8[:], in_=gm[:, c, 0:Bk]), reads=["gm"], writes=["m8"])
                self.ts("dve", thr[:], m8[:, 2:3], -1e29, None, ALU.max, reads=["m8"], writes=["thr"])
                self.ts("dve", nm_tm[:, c, 0:Bk], gm[:, c, 0:Bk], thr[:, 0:1], -30000.0, ALU.is_lt, ALU.mult,
                        reads=["gm", "thr", "nm_tm"], writes=["nm_tm"])
            for b in range(2):
                for i in range(8):
                    self.tr(tpb[0:16, i * 128:(i + 1) * 128], nm_tm[:, b * 8 + i, :], self.ident[:],
                            reads=["nm_tm", "ident"], writes=[B2])
                self.cp("act", nmT[:, b * 1024:(b + 1) * 1024], tpb[0:16, :], reads=[B2], writes=["nmT"])
            self.dma(qTa[64:80, :], nmT[:], reads=["nmT"], writes=["qTa_nm"])
            tiles = [(qt, kt) for qt in range(4) for kt in range(20 + 4 * qt)]

            def issue_S(i):
                qt, kt = tiles[i]
                k0 = 128 * kt
                sbk = (4, 5, 6, 1)[i % 4]
                self.mm(ps[sbk][:, :], kTa[0:80, k0:k0 + 128], qTa[0:80, qt * 512:(qt + 1) * 512], True, True,
                        reads=[("kTa", kt // 8), "kTa_oh", ("qTa", qt // 2), "qTa_nm"], writes=[("bank", sbk)])

            def issue_rest(i):
                qt, kt = tiles[i]
                q0 = S_OWN + 512 * qt
                nkt = 20 + 4 * qt
                ob = 7 if qt % 2 == 0 else 3
                O = ps[ob][:, 0:260].rearrange("p (j e) -> p j e", e=65)
                Dd = q0 - 128 * kt
                sbk = (4, 5, 6, 1)[i % 4]
                s2 = i % 3
                self.act(expS[s2][:], ps[sbk][:, :], AF.Exp, reads=[("bank", sbk)], writes=[("expS", s2)])
                self.tt("dve", PTm[s2][:], expS[s2][:], Ms[:, Dd + 384:Dd + 384 + 512], ALU.mult,
                        reads=[("expS", s2), "Ms"], writes=[("PTm", s2)])
                for j in range(4):
                    qo = 4 * qt + j
                    if kt - 16 > qo:
                        continue
                    self.mm(O[:, j, :], PTm[s2][:, j * 128:(j + 1) * 128], v_aug[:, kt, :], kt == 0 and j == 0, kt == nkt - 1 and j == 3,
                            reads=[("PTm", s2), ("v_aug", kt), "v_ones"], writes=[("bank", ob)])
                if kt == nkt - 1:
                    pending.append((i + 3, qt, ob, O))

            def finalize(qt, ob, O):
                self.recip(rec[:], O[:, :, 64], reads=[("bank", ob)], writes=["rec"])
                self.tt("dve", att_tm[:], O[:, :, 0:64], rec[:].unsqueeze(2).to_broadcast([128, 4, 64]), ALU.mult,
                        reads=[("bank", ob), "rec"], writes=["att_tm"])
                for j in range(4):
                    self.tr(tpb[0:64, j * 128:(j + 1) * 128], att_tm[:, j, :], self.ident[:], reads=["att_tm", "ident"], writes=[B2])
                self.cp("act", self.attT[:, h, qt * 512:(qt + 1) * 512], tpb[0:64, 0:512], reads=[B2], writes=[("attT", h, qt)])

            pending = []
            issue_S(0)
            issue_S(1)
            issue_S(2)
            for i in range(len(tiles)):
                if i + 3 < len(tiles):
                    issue_S(i + 3)
                issue_rest(i)
                while pending and pending[0][0] <= i:
                    _, qt_, ob_, O_ = pending.pop(0)
                    finalize(qt_, ob_, O_)
                if i < 96:
                    if i % 6 == 0:
                        P_T()
                    elif i % 6 == 3:
                        P_C()
            while pending:
                _, qt_, ob_, O_ = pending.pop(0)
                finalize(qt_, ob_, O_)
        while pstep[0] < 131:
            P_T()
            P_C()

    def phaseB(self, sb):
        I, ps = self.I, self.ps
        mergedT = sb("mergedT", [128, 8, S_OWN], BF16)
        wout = sb("woutb", [128, 8, D], BF16)
        cst = [sb("cstB%d" % i, [128, 8, 128]) for i in range(4)]
        cbf = [[sb("cbfB%d_%d" % (j, i), [128, 8, 128], BF16) for i in range(4)] for j in range(2)]
        sa = [sb("saB%d" % i, [128, 512]) for i in range(2)]; sg = [sb("sgB%d" % i, [128, 512]) for i in range(2)]
        xo = [sb("xoB%d" % i, [128, D]) for i in range(2)]
        xm = [sb("xmB%d" % i, [128, D]) for i in range(2)]
        hn = sb("hnB", [128, D], BF16); junk = sb("junkB", [128, D], BF16)
        ss = sb("ssB", [128, 16]); rs = sb("rsB", [128, 16])
        w3 = I["w_in"].rearrange("(k p) c -> p k c", p=128)
        wr3 = I["ret_w"].rearrange("(k p) c -> p k c", p=128)
        wb3 = I["moba_w"].rearrange("(h d) c -> d h c", d=64)
        wo3 = I["w_out"].rearrange("(k p) c -> p k c", p=128)
        engs = ("dve", "act", "dve", "act")

        def load_w(fc):
            cols = slice(fc * 128, (fc + 1) * 128)
            j = fc % 2
            self.dma(cst[0][:], wr3[:, :, cols], writes=[("cst", 0)])
            self.dma(cst[1][0:64, :, :], wb3[:, :, cols], writes=[("cst", 1)])
            self.dma(cst[2][:], w3[:, :, OFFS["ga"] + fc * 128:OFFS["ga"] + (fc + 1) * 128], writes=[("cst", 2)])
            self.dma(cst[3][:], w3[:, :, OFFS["gb"] + fc * 128:OFFS["gb"] + (fc + 1) * 128], writes=[("cst", 3)])
            for i in range(4):
                p = 64 if i == 1 else 128
                self.cp(engs[i], cbf[j][i][0:p, :, :], cst[i][0:p, :, :], reads=[("cst", i)], writes=[("cbf", j, i)])

        load_w(0)
        it = 0
        for fc in range(8):
            j = fc % 2
            if fc + 1 < 8:
                load_w(fc + 1)
            for T in range(4):
                tc = slice(T * 512, (T + 1) * 512)
                b0 = 4 * (T % 2)
                q = it % 2
                it += 1
                for kc in range(8):
                    self.mm(ps[b0 + 2][:, :], cbf[j][2][:, kc, :], self.hTo[:, kc, tc], kc == 0, kc == 7,
                            reads=[("cbf", j, 2)] + [("hT", 16 + T * 4 + i) for i in range(4)], writes=[("bank", b0 + 2)])
                for kc in range(8):
                    self.mm(ps[b0 + 3][:, :], cbf[j][3][:, kc, :], self.hTo[:, kc, tc], kc == 0, kc == 7,
                            reads=[("cbf", j, 3)] + [("hT", 16 + T * 4 + i) for i in range(4)], writes=[("bank", b0 + 3)])
                for kc in range(8):
                    self.mm(ps[b0][:, :], cbf[j][0][:, kc, :], self.retT[:, kc, tc], kc == 0, kc == 7,
                            reads=[("cbf", j, 0)], writes=[("bank", b0)])
                for h in range(8):
                    self.mm(ps[b0 + 1][:, :], cbf[j][1][0:64, h, :], self.attT[0:64, h, tc], h == 0, h == 7,
                            reads=[("cbf", j, 1)], writes=[("bank", b0 + 1)])
                self.act(sa[q][:], ps[b0 + 2][:, :], AF.Sigmoid, reads=[("bank", b0 + 2)], writes=[("saB", q)])
                self.act(sg[q][:], ps[b0 + 3][:, :], AF.Sigmoid, reads=[("bank", b0 + 3)], writes=[("sgB", q)])
                self.tt("dve", sa[q][:], sa[q][:], ps[b0][:, :], ALU.mult, reads=[("saB", q), ("bank", b0)], writes=[("saB", q)])
                self.tt("dve", sg[q][:], sg[q][:], ps[b0 + 1][:, :], ALU.mult, reads=[("sgB", q), ("bank", b0 + 1)], writes=[("sgB", q)])
                self.tt("pool", mergedT[:, fc, tc], sa[q][:], sg[q][:], ALU.add, reads=[("saB", q), ("sgB", q)], writes=[("mT", T)])
        for j in range(8):
            i = j % 4
            self.dma(cst[i][:], wo3[:, :, j * 128:(j + 1) * 128], writes=[("cst", i)])
            self.cp(engs[i], wout[:, :, j * 128:(j + 1) * 128], cst[i][:], reads=[("cst", i)], writes=["wout"])
        tpb = [ps[4][:, :].bitcast(BF16), ps[5][:, :].bitcast(BF16)]

        def b2_front(t):
            s = t % 2
            self.dma(xo[s][:], I["xw"][S_OWN + t * 128:S_OWN + (t + 1) * 128, :], writes=[("xo", s)])
            for hf in range(2):
                bk = 2 * s + hf
                for kc in range(8):
                    self.mm(ps[bk][:, :], mergedT[:, kc, t * 128:(t + 1) * 128], wout[:, kc, hf * 512:(hf + 1) * 512], kc == 0, kc == 7,
                            reads=[("mT", t // 4), "wout"], writes=[("bank", bk)])
                self.tt("dve", xm[s][:, hf * 512:(hf + 1) * 512], ps[bk][:, :], xo[s][:, hf * 512:(hf + 1) * 512], ALU.add,
                        reads=[("bank", bk), ("xo", s)], writes=[("xm", s)])
            self.dma(self.xm_scr[t * 128:(t + 1) * 128, :], xm[s][:], reads=[("xm", s)], writes=[("xm_scr", t)], key=("xm_scr", s))
            self.act(junk[:], xm[s][:], AF.Square, reads=[("xm", s)], writes=["junkB", ("ssB", t)], accum_out=ss[:, t:t + 1])
            self.act(rs[:, t:t + 1], ss[:, t:t + 1], AF.Sqrt, reads=[("ssB", t)], writes=[("rsB", t)], scale=1.0 / D, bias=EPS)

        def b2_mid(t):
            s = t % 2
            self.recip(rs[:, t:t + 1], rs[:, t:t + 1], reads=[("rsB", t)], writes=[("rsB", t)])
            self.ts("dve", hn[:], xm[s][:], rs[:, t:t + 1], None, ALU.mult, reads=[("xm", s), ("rsB", t)], writes=["hnB"])
            tp = tpb[s].rearrange("p (k c) -> p k c", k=8)
            for kc in range(8):
                self.tr(tp[:, kc, :], hn[:, kc * 128:(kc + 1) * 128], self.ident[:], reads=["hnB", "ident"], writes=[("bank", 4 + s)])

        def b2_back(t):
            s = t % 2
            tp = tpb[s].rearrange("p (k c) -> p k c", k=8)
            self.tt("dve", self.hTo[:, :, t * 128:(t + 1) * 128], tp, self.g2col[:].unsqueeze(2).to_broadcast([128, 8, 128]), ALU.mult,
                    reads=[("bank", 4 + s), "g2col"], writes=[("hT", 16 + t)])

        b2_front(0)
        for t in range(16):
            if t + 1 < 16:
                b2_front(t + 1)
            b2_mid(t)
            if t >= 1:
                b2_back(t - 1)
        b2_back(15)


    def phaseC(self, sb):
        I, ps, mk = self.I, self.ps, self.mk
        NG, GT = 8, 256
        qT_scr = self.dscr("qT_scr", [128, 16, S_OWN])
        h2T = self.hTo
        skT = sb("skT", [128, 16, 128]); identf = sb("identfC", [128, 128])
        self.dma(identf[:], I["identf"][:, :], writes=["identf"])
        with contextlib.ExitStack() as es:
            skn = es.enter_context(self.nc.sbuf_tensor("skn", [128, 16, 128], F32))
            self.dma(skn[:], I["sub_keys"].rearrange("g n d -> n g d"), writes=["skn"])
            for hc in range(16):
                bk = hc % 2
                self.tr(ps[bk][:, 0:128], skn[:, hc, :], identf[:], reads=["skn", "identf"], writes=[("bank", bk)])
                self.cp("act" if hc % 2 else "dve", skT[:, hc, :], ps[bk][:, 0:128], reads=[("bank", bk)], writes=["skT"])
            wq = es.enter_context(self.nc.sbuf_tensor("wqC", [128, 8, 2048], BF16))
            wst2 = [es.enter_context(self.nc.sbuf_tensor("wstC%d" % i, [128, 8, 512], F32)) for i in range(2)]
            qst = [es.enter_context(self.nc.sbuf_tensor("qstC%d" % i, [128, 512], F32)) for i in range(2)]
            wq3 = I["peer_wq"].rearrange("(k p) c -> p k c", p=128)
            for j in range(4):
                wst = wst2[j % 2]
                self.dma(wst[:], wq3[:, :, j * 512:(j + 1) * 512], writes=[("wstC", j % 2)])
                self.cp("dve", wq[:, 0:4, j * 512:(j + 1) * 512], wst[:, 0:4, :], reads=[("wstC", j % 2)], writes=[("wqC", j, 0)])
                self.cp("act", wq[:, 4:8, j * 512:(j + 1) * 512], wst[:, 4:8, :], reads=[("wstC", j % 2)], writes=[("wqC", j, 1)])
            cnt = 0
            for hc in range(16):
                for T in range(4):
                    bk = 2 + cnt % 2
                    s = cnt % 2
                    cnt += 1
                    for kc in range(8):
                        self.mm(ps[bk][:, :], wq[:, kc, hc * 128:(hc + 1) * 128], h2T[:, kc, T * 512:(T + 1) * 512], kc == 0, kc == 7,
                                reads=[("wqC", hc // 4, 0), ("wqC", hc // 4, 1)], writes=[("bank", bk)])
                    self.cp("act" if s else "dve", qst[s][:], ps[bk][:, :], reads=[("bank", bk)], writes=[("qst", s)])
                    self.dma(qT_scr[:, hc, T * 512:(T + 1) * 512], qst[s][:], reads=[("qst", s)], writes=["qT_scr"], key=("qT_scr", s), nowaw=True)
        self.mk.barrier()
        EE = sb("EE", [128, 2, 16, 128])
        UTc = [sb("UTc%d" % i, [128, 8, 8, 128], BF16) for i in range(2)]
        Vc = [sb("Vc%d" % i, [128, 8, D], BF16) for i in range(2)]
        gact = [sb("gact%d" % i, [128, 8, GT], BF16) for i in range(2)]
        Ew = [sb("Ew%d" % i, [128, 8, 128]) for i in range(5)]
        Wh = [sb("Wh%d" % i, [128, 8, 8, 128], BF16) for i in range(2)]
        WaT = [sb("WaT%d" % i, [128, 8, 128], BF16) for i in range(2)]
        Dg = sb("Dg", [128, 2, 8, 128], BF16)
        qTg = sb("qTg", [128, 16, 128])
        negm = sb("negm", [128, 2, 16]); ev = sb("ev", [128, 2, 16, 16]); tmpE = sb("tmpE", [128, 128])
        tmpC = sb("tmpC", [128, 256]); t16all = sb("t16all", [128, 2, 8, 16])
        ec = sb("ec", [128, 2, 8]); Zs = sb("Zs", [128, 2, 8]); rz = sb("rz", [128, 2, 8])
        xmt = [sb("xmtC0", [128, D])] * 2
        ld = [0]

        def load_chunk(c):
            s = ld[0] % 2
            ld[0] += 1
            self.dma(UTc[s][:], self.UT_scr[c * 8:(c + 1) * 8, :, :, :].rearrange("a d k b -> d a k b"), writes=[("UTc", s)])
            self.dma(Vc[s][:], self.V_scr[c * 1024:(c + 1) * 1024, :].rearrange("(a b) d -> b a d", b=128), writes=[("Vc", s)])
            return s

        for g in range(NG):
            g0 = g * GT
            def st_scores(tl):
                if not (tl == 0 and g > 0):
                    self.dma(qTg[:], qT_scr[:, :, g0 + tl * 128:g0 + (tl + 1) * 128], writes=["qTg"])
                for hc in range(16):
                    bk = 4 * (1 - tl) + hc // 4
                    self.mm(ps[bk][:, (hc % 4) * 128:(hc % 4 + 1) * 128], qTg[:, hc, :], skT[:, hc, :], True, True,
                            reads=["qTg", "skT"], writes=[("bank", bk)])

            def st_max(tl):
                for q4 in range(4):
                    bk = 4 * (1 - tl) + q4
                    mk.op("dve", lambda e, bk=bk, q4=q4, tl=tl: e.tensor_reduce(out=negm[:, tl, q4 * 4:(q4 + 1) * 4],
                                                                            in_=ps[bk][:, :].rearrange("p (g n) -> p g n", n=128),
                                                                            axis=AX.X, op=ALU.max),
                          reads=[("bank", bk)], writes=[("negm", tl)])
                self.ts("dve", negm[:, tl, :], negm[:, tl, :], -1.0, None, ALU.mult, reads=[("negm", tl)], writes=[("negm", tl)])

            def st_exp(tl):
                for hc in range(16):
                    bk = 4 * (1 - tl) + hc // 4
                    self.act(EE[:, tl, hc, :], ps[bk][:, (hc % 4) * 128:(hc % 4 + 1) * 128], AF.Exp, reads=[("bank", bk), ("negm", tl)],
                             writes=[("EE", tl), ("EEh", tl, hc)], bias=negm[:, tl, hc:hc + 1], scale=1.0)

            def st_topk(tl):
                for hc in range(16):
                    mk.op("dve", lambda e, hc=hc, tl=tl: e.max(out=ev[:, tl, hc, 0:8], in_=EE[:, tl, hc, :]), reads=[("EEh", tl, hc)], writes=[("ev", tl)])
                    mk.op("dve", lambda e, hc=hc, tl=tl: e.match_replace(out=tmpE[:], in_to_replace=ev[:, tl, hc, 0:8], in_values=EE[:, tl, hc, :], imm_value=-1.0),
                          reads=[("EEh", tl, hc), ("ev", tl)], writes=["tmpE"])
                    mk.op("dve", lambda e, hc=hc, tl=tl: e.max(out=ev[:, tl, hc, 8:16], in_=tmpE[:]), reads=["tmpE"], writes=[("ev", tl)])

            def st_heads(tl):
                ev4 = ev[:, tl, :, :].rearrange("p (h c) k -> p h c k", c=2)
                for half in range(2):
                    es_ = 2 * tl + half
                    cnd = Ew[es_][:].rearrange("p a b -> p (a b)").rearrange("p (h i j) -> p h i j", h=4, i=16)
                    self.tt("dve", cnd, ev4[:, half * 4:(half + 1) * 4, 0, :].unsqueeze(3).to_broadcast([128, 4, 16, 16]),
                            ev4[:, half * 4:(half + 1) * 4, 1, :].unsqueeze(2).to_broadcast([128, 4, 16, 16]), ALU.mult,
                            reads=[("ev", tl)], writes=[("Ew", es_)] + [("Ew", es_, a) for a in range(8)])
                    for hh in range(4):
                        h = half * 4 + hh
                        cf = Ew[es_][:].rearrange("p a b -> p (a b)")[:, hh * 256:(hh + 1) * 256]
                        mk.op("dve", lambda e, cf=cf, h=h, tl=tl: e.max(out=t16all[:, tl, h, 0:8], in_=cf), reads=[("Ew", es_)] + [("Ew", es_, a) for a in range(8)], writes=[("t16a", tl)])
                        mk.op("dve", lambda e, cf=cf, h=h, tl=tl: e.match_replace(out=tmpC[:], in_to_replace=t16all[:, tl, h, 0:8], in_values=cf, imm_value=-1.0),
                              reads=[("Ew", es_), ("t16a", tl)] + [("Ew", es_, a) for a in range(8)], writes=["tmpC"])
                        mk.op("dve", lambda e, h=h, tl=tl: e.max(out=t16all[:, tl, h, 8:16], in_=tmpC[:]), reads=["tmpC"], writes=[("t16a", tl)])
                self.cp("dve", ec[:, tl, :], t16all[:, tl, :, 15], reads=[("t16a", tl)], writes=["ec"])
                mk.op("dve", lambda e, tl=tl: e.tensor_reduce(out=Zs[:, tl, :], in_=t16all[:, tl, :, :], axis=AX.X, op=ALU.add),
                      reads=[("t16a", tl)], writes=[("Zs", tl)])
                self.recip(rz[:, tl, :], Zs[:, tl, :], reads=[("Zs", tl)], writes=[("rz", tl)])
                for h in range(8):
                    self.ts("dve", Dg[:, tl, h, :], self.ident[:], rz[:, tl, h:h + 1], None, ALU.mult, reads=["ident", ("rz", tl)], writes=[("Dg", tl)])

            for stage in (st_scores, st_max, st_exp, st_topk, st_heads):
                stage(0)
                stage(1)
                if stage is st_scores and g + 1 < NG:
                    self.dma(qTg[:], qT_scr[:, :, g0 + GT:g0 + GT + 128], writes=["qTg"])
            nxt = load_chunk(0)
            units = [(c, tl) for c in range(16) for tl in range(2)]
            cslot = {}

            def emit_tail(u):
                c, tl = units[u]
                w = u % 2
                s_ = cslot[c]
                gs = c % 2
                for hf in range(2):
                    self.tt("dve", WaT[w][:, hf * 4:(hf + 1) * 4, :], ps[4 + hf][:, :].rearrange("p (a t) -> p a t", a=4),
                            gact[gs][:, hf * 4:(hf + 1) * 4, tl * 128:(tl + 1) * 128], ALU.mult,
                            reads=[("bank", 4 + hf), ("gact", gs)], writes=[("WaT", w)])
                for hf in range(2):
                    ob = tl * 2 + hf
                    for a in range(8):
                        self.mm(ps[ob][:, :], WaT[w][:, a, :], Vc[s_][:, a, hf * 512:(hf + 1) * 512], c == 0 and a == 0, c == 15 and a == 7,
                                reads=[("WaT", w), ("Vc", s_)], writes=[("bank", ob)])

            for u, (c, tl) in enumerate(units):
                for h in (4, 5, 6, 7):
                    e = Ew[h - 3]
                    for a in range(8):
                        self.act(e[:, a, :], EE[:, tl, 2 * h + 1, :], AF.Copy, reads=[("EE", tl)], writes=[("Ew", h - 3, a)],
                                 scale=EE[:, tl, 2 * h, c * 8 + a:c * 8 + a + 1])
                if tl == 0:
                    s = nxt
                    cslot[c] = s
                    gs = c % 2
                    for a in range(8):
                        bk = 6 + (a // 2) % 2
                        half = (a % 2) * GT
                        for kc in range(8):
                            self.mm(ps[bk][:, half:half + GT], UTc[s][:, a, kc, :], h2T[:, kc, g0:g0 + GT], kc == 0, kc == 7,
                                    reads=[("UTc", s)], writes=[("bank", bk)])
                        if a % 2 == 1:
                            self.act(gact[gs][:, a - 1:a + 1, :], ps[bk][:, :].rearrange("p (a t) -> p a t", a=2), AF.Gelu,
                                     reads=[("bank", bk)], writes=[("gact", gs)])
                wb = u % 2
                for h in range(8):
                    if h < 4:
                        si = 0
                        e = Ew[0]
                        self.tt("dve", e[:], EE[:, tl, 2 * h, c * 8:(c + 1) * 8].unsqueeze(2).to_broadcast([128, 8, 128]),
                                EE[:, tl, 2 * h + 1, :].unsqueeze(1).to_broadcast([128, 8, 128]), ALU.mult,
                                reads=[("EE", tl)], writes=[("Ew", 0)])
                        rd = [("Ew", 0)]
                    else:
                        si = h - 3
                        e = Ew[si]
                        rd = [("Ew", si, a) for a in range(8)]
                    self.stt("dve", Wh[wb][:, h, :, :], e[:], ec[:, tl, h:h + 1], e[:], ALU.is_ge, ALU.mult,
                             reads=rd + ["ec"], writes=[("Wh", wb, h)])
                if u > 0:
                    emit_tail(u - 1)
                if tl == 0 and c + 1 < 16:
                    nxt = load_chunk(c + 1)
                for a in range(8):
                    bk = 4 + a // 4
                    for h in range(8):
                        self.mm(ps[bk][:, (a % 4) * 128:(a % 4 + 1) * 128], Wh[wb][:, h, a, :], Dg[:, tl, h, :], h == 0, h == 7,
                                reads=[("Wh", wb, h), ("Dg", tl)], writes=[("bank", bk)])
            emit_tail(len(units) - 1)
            for tl in range(2):
                t = g * 2 + tl
                sx = 0
                self.dma(xmt[sx][:], self.xm_scr[t * 128:(t + 1) * 128, :], writes=[("xmt", sx)])
                for hf in range(2):
                    self.tt("dve", xmt[sx][:, hf * 512:(hf + 1) * 512], xmt[sx][:, hf * 512:(hf + 1) * 512], ps[tl * 2 + hf][:, :], ALU.add,
                            reads=[("xmt", sx), ("bank", tl * 2 + hf)], writes=[("xmt", sx)])
                self.dma(self.y[t * 128:(t + 1) * 128, :], xmt[sx][:], reads=[("xmt", sx)], writes=[("y", sx)], key=("y", sx))


def _prep_inputs(inputs):
    x = np.asarray(inputs["x"], np.float32)
    shared = {
        "w_in": np.ascontiguousarray(inputs["w_in"][0], np.float32),
        "ret_w": np.ascontiguousarray(inputs["ret_w_branch"][0], np.float32),
        "moba_w": np.ascontiguousarray(inputs["moba_w_branch"][0], np.float32),
        "w_out": np.ascontiguousarray(inputs["w_out"][0], np.float32),
        "peer_wq": np.ascontiguousarray(inputs["peer_w_q"][0], np.float32),
        "sub_keys": np.ascontiguousarray(np.asarray(inputs["peer_sub_keys"][0], np.float32).reshape(16, 128, 128)),
        "peer_u": np.ascontiguousarray(inputs["peer_u"][0], np.float32),
        "peer_v": np.ascontiguousarray(inputs["peer_v"][0], np.float32),
        "g1": np.ascontiguousarray(inputs["mix_norm_g"][0], np.float32),
        "g2": np.ascontiguousarray(inputs["ffn_norm_g"][0], np.float32),
        "mqg": np.ascontiguousarray(inputs["moba_q_gain"], np.float32).reshape(1, 64),
        "mkg": np.ascontiguousarray(inputs["moba_k_gain"], np.float32).reshape(1, 64),
        "rel_bias": np.ascontiguousarray(inputs["rel_bias"], np.float32),
    }
    consts = [_consts(0), _consts(1)]
    in_maps = []
    for c in range(NCORES):
        b, half = c // 2, c % 2
        if half == 1:
            xw = x[b]
        else:
            xw = np.concatenate([np.zeros((S_OWN, D), np.float32), x[b, :S_OWN]], 0)
        m = dict(shared)
        m["xw"] = np.ascontiguousarray(xw)
        for k, v in consts[half].items():
            if k != "gC":
                m[k] = v
        in_maps.append(m)
    return in_maps


_CACHE = {}


def kernel(**inputs):
    in_maps = _prep_inputs(inputs)
    if "nc" not in _CACHE:
        pb = PB()
        _CACHE["nc"] = pb.build()
        _CACHE["used"] = set(pb.I.keys())
    used = _CACHE["used"]
    in_maps = [{k: v for k, v in m.items() if k in used} for m in in_maps]
    res = run_bass_kernel_spmd(_CACHE["nc"], in_maps, core_ids=list(range(NCORES)))
    out = np.zeros((4, 4096, D), np.float32)
    for c in range(NCORES):
        b, half = c // 2, c % 2
        out[b, half * S_OWN:(half + 1) * S_OWN] = res.results[c]["y"]
    return out
```
